# Optimizing a Trainium2 kernel written in Bass

```python
import jax, jax.numpy as jnp
from jax import lax
import numpy as np

D_MODEL = 1024
BATCH = 32
SEQ = 2048
DEPTH = 4
DEC_BATCH = 32
DEC_SEQ = 16
PAST_LEN = 1024

CHUNK = 64
N_A_LAYERS = DEPTH // 2
N_B_LAYERS = DEPTH - N_A_LAYERS
CONV_WIDTH = 3
CONV_DIM = D_MODEL
SB_HEADS = 16
SB_HEAD_DIM = D_MODEL // SB_HEADS
Q_BLOCK = 128
PEER_HEADS = 8
PEER_N_KEYS = 128
PEER_N_EXPERTS = PEER_N_KEYS * PEER_N_KEYS
PEER_TOPK = 16
PEER_QDIM = 256
PEER_HALF = PEER_QDIM // 2
PEER_BLOCK = 256
LN_EPS = 1e-5
DEEPNORM_ALPHA = (2.0 * DEPTH) ** 0.25
DEEPNORM_BETA = (8.0 * DEPTH) ** -0.25

kernel_name = "yoco_shortconv_stickbreaking_peer_step"


def _layernorm(x, g, b):
    xf = x.astype(jnp.float32)
    mu = jnp.mean(xf, axis=-1, keepdims=True)
    var = jnp.mean(jnp.square(xf - mu), axis=-1, keepdims=True)
    y = (xf - mu) * lax.rsqrt(var + LN_EPS)
    return (y * g.astype(jnp.float32) + b.astype(jnp.float32)).astype(x.dtype)


def _short_conv_mixer(x, conv_prev, w_in, w_dw, w_out):
    S = x.shape[1]
    b_gate, c_gate, xt = jnp.split(x @ w_in, 3, axis=-1)
    u = c_gate * xt
    up = jnp.concatenate([conv_prev.astype(u.dtype), u], axis=1)
    acc = w_dw[0] * up[:, 0:S]
    for w in range(1, CONV_WIDTH):
        acc = acc + w_dw[w] * up[:, w:w + S]
    y = (b_gate * acc) @ w_out
    return y, up[:, -(CONV_WIDTH - 1):]


def _sb_block(q, k, v, q_pos):
    scale = SB_HEAD_DIM ** -0.5
    z = jnp.einsum("bqhd,bkhd->bhqk", q, k).astype(jnp.float32) * scale
    k_pos = jnp.arange(k.shape[1])
    causal = k_pos[None, :] < q_pos[:, None]
    log_not = jnp.where(causal, jax.nn.log_sigmoid(-z), 0.0)
    suffix = lax.cumsum(log_not, axis=3, reverse=True) - log_not
    a = jnp.where(causal, jnp.exp(jax.nn.log_sigmoid(z) + suffix), 0.0)
    return jnp.einsum("bhqk,bkhd->bqhd", a.astype(v.dtype), v)


def _sb_mixer(x, k_all, v_all, w_q, w_o, q_pos0):
    B, S, _ = x.shape
    q = (x @ w_q).reshape(B, S, SB_HEADS, SB_HEAD_DIM)
    q_pos = q_pos0 + jnp.arange(S)
    if S % Q_BLOCK == 0:
        nb = S // Q_BLOCK
        qb = q.reshape(B, nb, Q_BLOCK, SB_HEADS, SB_HEAD_DIM).transpose(1, 0, 2, 3, 4)
        pb = q_pos.reshape(nb, Q_BLOCK)
        ob = lax.map(lambda a: _sb_block(a[0], k_all, v_all, a[1]), (qb, pb))
        o = ob.transpose(1, 0, 2, 3, 4).reshape(B, S, SB_HEADS * SB_HEAD_DIM)
    else:
        o = _sb_block(q, k_all, v_all, q_pos).reshape(B, S, SB_HEADS * SB_HEAD_DIM)
    return o @ w_o


def _peer(x, w_q, subkeys, u_tab, v_tab):
    B, S, D = x.shape
    T = B * S
    n_blk = -(-T // PEER_BLOCK)
    pad = n_blk * PEER_BLOCK - T
    xp = jnp.pad(x.reshape(T, D), ((0, pad), (0, 0))).reshape(n_blk, PEER_BLOCK, D)
    K = PEER_TOPK

    def block(xb):
        q = (xb @ w_q).reshape(PEER_BLOCK, PEER_HEADS, 2, PEER_HALF).astype(jnp.float32)
        s = jnp.einsum("thpc,hpnc->thpn", q, subkeys.astype(jnp.float32))
        s_top, i_top = lax.top_k(s, K)
        cand = s_top[:, :, 0, :, None] + s_top[:, :, 1, None, :]
        cand_idx = i_top[:, :, 0, :, None] * PEER_N_KEYS + i_top[:, :, 1, None, :]
        best, pos = lax.top_k(cand.reshape(PEER_BLOCK, PEER_HEADS, K * K), K)
        expert = jnp.take_along_axis(cand_idx.reshape(PEER_BLOCK, PEER_HEADS, K * K), pos, axis=-1)
        g = jax.nn.softmax(best, axis=-1)
        act = jax.nn.gelu(jnp.einsum("td,thkd->thk", xb, u_tab[expert]), approximate=False)
        wgt = (g * act.astype(jnp.float32)).astype(xb.dtype)
        return jnp.einsum("thk,thkd->td", wgt, v_tab[expert])

    out = lax.map(block, xp).reshape(n_blk * PEER_BLOCK, D)[:T]
    return out.reshape(B, S, D)


def _trunk(x, conv_prev, k_past, v_past, q_pos0, conv_w_in, conv_w_dw, conv_w_out,
           sb_w_q, sb_w_o, kv_w_k, kv_w_v, peer_w_q, peer_subkeys, peer_u, peer_v, ln_g, ln_b):
    B, S, _ = x.shape
    h = x
    new_conv = []
    k_all = v_all = k_new = v_new = None
    for layer in range(DEPTH):
        if layer < N_A_LAYERS:
            mix, st = _short_conv_mixer(h, conv_prev[layer], conv_w_in[layer],
                                        conv_w_dw[layer], conv_w_out[layer])
            new_conv.append(st)
        else:
            if layer == N_A_LAYERS:
                k_new = (h @ kv_w_k).reshape(B, S, SB_HEADS, SB_HEAD_DIM)
                v_new = (h @ kv_w_v).reshape(B, S, SB_HEADS, SB_HEAD_DIM)
                if k_past is None:
                    k_all, v_all = k_new, v_new
                else:
                    k_all = jnp.concatenate([k_past.astype(k_new.dtype), k_new], axis=1)
                    v_all = jnp.concatenate([v_past.astype(v_new.dtype), v_new], axis=1)
            j = layer - N_A_LAYERS
            mix = _sb_mixer(h, k_all, v_all, sb_w_q[j], sb_w_o[j], q_pos0)
        h = _layernorm(DEEPNORM_ALPHA * h + mix, ln_g[layer, 0], ln_b[layer, 0])
        ff = _peer(h, peer_w_q[layer], peer_subkeys[layer], peer_u[layer], peer_v[layer])
        h = _layernorm(DEEPNORM_ALPHA * h + ff, ln_g[layer, 1], ln_b[layer, 1])
    return h, jnp.stack(new_conv, axis=0), k_new, v_new


def setup_inputs(seed: int = 0) -> dict:
    key = jax.random.key(seed)
    ks = jax.random.split(key, 20)
    f32 = jnp.float32
    HD = SB_HEADS * SB_HEAD_DIM
    nrm = lambda k, shape, s: jax.random.normal(k, shape, f32) * s
    return {
        "x_prompt": nrm(ks[0], (BATCH, SEQ, D_MODEL), 1.0),
        "x_sample": nrm(ks[1], (DEC_BATCH, DEC_SEQ, D_MODEL), 1.0),
        "state_conv": nrm(ks[2], (N_A_LAYERS, DEC_BATCH, CONV_WIDTH - 1, CONV_DIM), 1.0),
        "cache_k": nrm(ks[3], (DEC_BATCH, PAST_LEN, SB_HEADS, SB_HEAD_DIM), 1.0),
        "cache_v": nrm(ks[4], (DEC_BATCH, PAST_LEN, SB_HEADS, SB_HEAD_DIM), DEEPNORM_BETA),
        "conv_w_in": nrm(ks[5], (N_A_LAYERS, D_MODEL, 3 * CONV_DIM), D_MODEL ** -0.5),
        "conv_w_dw": nrm(ks[6], (N_A_LAYERS, CONV_WIDTH, CONV_DIM), CONV_WIDTH ** -0.5),
        "conv_w_out": nrm(ks[7], (N_A_LAYERS, CONV_DIM, D_MODEL), DEEPNORM_BETA * CONV_DIM ** -0.5),
        "sb_w_q": nrm(ks[8], (N_B_LAYERS, D_MODEL, HD), D_MODEL ** -0.5),
        "sb_w_o": nrm(ks[9], (N_B_LAYERS, HD, D_MODEL), DEEPNORM_BETA * HD ** -0.5),
        "kv_w_k": nrm(ks[10], (D_MODEL, HD), D_MODEL ** -0.5),
        "kv_w_v": nrm(ks[11], (D_MODEL, HD), DEEPNORM_BETA * D_MODEL ** -0.5),
        "peer_w_q": nrm(ks[12], (DEPTH, D_MODEL, PEER_HEADS * PEER_QDIM), D_MODEL ** -0.5),
        "peer_subkeys": nrm(ks[13], (DEPTH, PEER_HEADS, 2, PEER_N_KEYS, PEER_HALF), PEER_HALF ** -0.5),
        "peer_u": nrm(ks[14], (DEPTH, PEER_N_EXPERTS, D_MODEL), D_MODEL ** -0.5),
        "peer_v": nrm(ks[15], (DEPTH, PEER_N_EXPERTS, D_MODEL), DEEPNORM_BETA * PEER_HEADS ** -0.5),
        "ln_g": 1.0 + nrm(ks[16], (DEPTH, 2, D_MODEL), 0.02),
        "ln_b": nrm(ks[17], (DEPTH, 2, D_MODEL), 0.02),
    }


def reference(x_prompt, x_sample, state_conv, cache_k, cache_v, conv_w_in, conv_w_dw, conv_w_out,
              sb_w_q, sb_w_o, kv_w_k, kv_w_v, peer_w_q, peer_subkeys, peer_u, peer_v, ln_g, ln_b):
    zero_conv = jnp.zeros((N_A_LAYERS, x_prompt.shape[0], CONV_WIDTH - 1, CONV_DIM), x_prompt.dtype)
    y_prompt, new_conv_prompt, new_k_prompt, new_v_prompt = _trunk(
        x_prompt, zero_conv, None, None, 0, conv_w_in, conv_w_dw, conv_w_out,
        sb_w_q, sb_w_o, kv_w_k, kv_w_v, peer_w_q, peer_subkeys, peer_u, peer_v, ln_g, ln_b)
    y_sample, new_conv_sample, new_k_sample, new_v_sample = _trunk(
        x_sample, state_conv, cache_k, cache_v, cache_k.shape[1], conv_w_in, conv_w_dw, conv_w_out,
        sb_w_q, sb_w_o, kv_w_k, kv_w_v, peer_w_q, peer_subkeys, peer_u, peer_v, ln_g, ln_b)
    return (y_prompt, y_sample, new_conv_prompt, new_k_prompt, new_v_prompt,
            new_conv_sample, new_k_sample, new_v_sample)
```

```python
import numpy as np
from contextlib import ExitStack
import concourse.bass as bass
import concourse.mybir as mybir
from concourse.bass_utils import run_bass_kernel_spmd

F32 = mybir.dt.float32
BF16 = mybir.dt.bfloat16
I32 = mybir.dt.int32
U32 = mybir.dt.uint32
AF = mybir.ActivationFunctionType
ALU = mybir.AluOpType
AX = mybir.AxisListType

D = 1024
ALPHA = (2.0 * 4) ** 0.25
EPS = 1e-5
NEG = -1.0e30
SEM_LIMIT = 32000
SAME_RAW = True
NGS = 4
NWS = 3
GRP = 4
DBG = 0


class Buf:
    __slots__ = ("name", "w", "r")

    def __init__(self, name):
        self.name = name
        self.w = None
        self.r = []


class Lane:
    def __init__(self, pool, inc):
        self.pool = pool
        self.inc = inc
        self.sem = pool.pop()
        self.count = 0

    def next(self):
        if self.count + self.inc > SEM_LIMIT:
            self.sem = self.pool.pop()
            self.count = 0
        self.count += self.inc
        return (self.sem, self.count)


class Eng:
    def __init__(self, name, pool):
        self.name = name
        self.lane = Lane(pool, 1)
        self.q = []
        self.waited = {}


class Prog:
    def __init__(self, n_pseq=4, n_ptiles=16, n_sseq=4, n_past=8, n_layers=4):
        self.n_pseq, self.n_ptiles, self.n_sseq, self.n_past = n_pseq, n_ptiles, n_sseq, n_past
        self.n_layers = n_layers
        self.nc = bass.Bass("TRN2", target_bir_lowering=False)
        self.es = ExitStack()
        self.build()

    def _wait_for(self, eng, ev, kind):
        sem, val, src = ev
        if src == eng.name:
            if eng.name == "tensor" or kind != "raw" or not SAME_RAW:
                return
        key = id(sem)
        if eng.waited.get(key, 0) >= val:
            return
        eng.waited[key] = val
        eng.q.append(("w", sem, val))

    def _deps(self, eng, reads, writes):
        for b in reads:
            if b.w is not None:
                self._wait_for(eng, b.w, "raw")
        for b in writes:
            if b.w is not None:
                self._wait_for(eng, b.w, "waw")
            for r in b.r:
                self._wait_for(eng, r, "war")

    def _commit(self, ev, reads, writes):
        for b in reads:
            b.r.append(ev)
        for b in writes:
            b.w = ev
            b.r = []

    def op(self, engname, method, reads, writes, **kw):
        eng = self.eng[engname]
        self._deps(eng, reads, writes)
        sem, val = eng.lane.next()
        eng.q.append(("i", method, kw, sem, 1))
        self._commit((sem, val, engname), reads, writes)

    def dma(self, qname, lane, reads, writes, out, in_, method="dma_start", **kw):
        eng = self.eng[qname]
        self._deps(eng, reads, writes)
        sem, val = lane.next()
        kw = dict(kw)
        kw["out"] = out
        kw["in_"] = in_
        eng.q.append(("i", method, kw, sem, 16))
        ev = (sem, val, "dma")
        self._commit(ev, reads, writes)
        return ev

    def V(self, method, reads, writes, **kw):
        self.op("vector", method, reads, writes, **kw)

    def A(self, method, reads, writes, **kw):
        self.op("scalar", method, reads, writes, **kw)

    def T(self, method, reads, writes, **kw):
        self.op("tensor", method, reads, writes, **kw)

    def G(self, method, reads, writes, **kw):
        self.op("gpsimd", method, reads, writes, **kw)

    def act(self, reads, writes, out, in_, func, **kw):
        self.A("activation", reads, writes, out=out, in_=in_, func=func, **kw)

    def mm(self, reads, writes, out, lhsT, rhs, start, stop, **kw):
        self.T("matmul", reads, writes, out=out, lhsT=lhsT, rhs=rhs, start=start, stop=stop, **kw)

    def sb(self, name, shape, dt):
        return self.es.enter_context(self.nc.sbuf_tensor(name, shape, dt))

    def newlane(self):
        return Lane(self.sempool, 16)

    def build(self):
        nc = self.nc
        es = self.es
        NP, NT, NS, NPAST = self.n_pseq, self.n_ptiles, self.n_sseq, self.n_past
        SP = NT * 128
        SPAST = NPAST * 128

        def din(name, shape, dt=F32):
            return nc.dram_tensor(name, list(shape), dt, kind="ExternalInput").ap()

        def dout(name, shape, dt=F32):
            return nc.dram_tensor(name, list(shape), dt, kind="ExternalOutput").ap()

        self.xp = din("xp", [NP, SP, D])
        self.xs = din("xs", [NS, 16, D])
        self.stc = din("stc", [2, NS, 2, D])
        self.ck = din("ck", [NS, SPAST, D])
        self.cv = din("cv", [NS, SPAST, D])
        self.w_in = din("w_in", [2, D, 3 * D])
        self.wdw_d = din("wdw", [128, 48])
        self.w_out = din("w_out", [2, D, D])
        self.sbq = din("sbq", [2, D, D])
        self.sbo = din("sbo", [2, D, D])
        self.wk = din("wk", [D, D])
        self.wv = din("wv", [D, D])
        self.pwq = din("pwq", [4, D, 2 * D])
        self.skT_d = din("skT", [4, 128, 2048])
        self.uv = din("uv", [4, 16384, 2 * D])
        self.lng = din("lng", [8, D])
        self.lnb = din("lnb", [8, D])
        self.yp = dout("yp", [NP, SP, D])
        self.ys = dout("ys", [NS, 16, D])
        self.ncp = dout("ncp", [2, NP, 2, D])
        self.nkp = dout("nkp", [NP, SP, D])
        self.nvp = dout("nvp", [NP, SP, D])
        self.ncs = dout("ncs", [2, NS, 2, D])
        self.nks = dout("nks", [NS, 16, D])
        self.nvs = dout("nvs", [NS, 16, D])

        self.chunks = []
        self.cid = {}

        def addchunks(key, ap2d, ncols):
            for j in range(ncols // 512):
                self.cid[(key, j)] = len(self.chunks)
                self.chunks.append(("w", ap2d[:, j * 512:(j + 1) * 512]))

        for l in range(2):
            addchunks(("w_in", l), self.w_in[l], 3 * D)
            addchunks(("w_out", l), self.w_out[l], D)
            addchunks(("sbq", l), self.sbq[l], D)
            addchunks(("sbo", l), self.sbo[l], D)
        addchunks(("wk", 0), self.wk, D)
        addchunks(("wv", 0), self.wv, D)
        for l in range(4):
            addchunks(("pwq", l), self.pwq[l], 2 * D)
        for l in range(4):
            self.cid[(("sk", l), 0)] = len(self.chunks)
            self.chunks.append(("sk", self.skT_d[l]))
        NCH = len(self.chunks)
        self.wbf = nc.dram_tensor("wbf", [NCH, 128, 4096], BF16, kind="Internal").ap()
        self.wbf_buf = [Buf(f"wbf{i}") for i in range(NCH)]

        self.sempool = [es.enter_context(nc.semaphore(f"sm{i}")) for i in range(96)]
        self.eng = {n: Eng(n, self.sempool) for n in ["tensor", "vector", "scalar", "gpsimd", "sync"]}

        sb = self.sb
        self.wsl = sb("wsl", [128, NWS, 8, 512], BF16)
        self.wsl_buf = [Buf(f"wsl{i}") for i in range(NWS)]
        self.wsl_lane = [self.newlane() for _ in range(NWS)]
        self.KT = sb("KT", [128, 8, 16 * 128], BF16)
        self.Vr = sb("Vr", [128, 16, D], BF16)
        self.KT_buf = [Buf(f"KT{i}") for i in range(16)]
        self.V_buf = [Buf(f"V{i}") for i in range(16)]
        self.gsl = sb("gsl", [128, NGS, 2 * D], BF16)
        self.gsl_buf = [Buf(f"gsl{i}") for i in range(NGS)]
        self.gsl_lane = [self.newlane() for _ in range(NGS)]
        self.hb = [sb(f"hb{i}", [128, D], F32) for i in range(2)]
        self.hb_buf = [Buf(f"hb{i}") for i in range(2)]
        self.hb_lane_in = [self.newlane() for _ in range(2)]
        self.hb_lane_out = [self.newlane() for _ in range(2)]
        self.pre = sb("pre", [128, D], F32)
        self.pre_buf = Buf("pre")
        self.pre_lane = self.newlane()
        self.h16 = sb("h16", [128, D], BF16)
        self.h16_buf = Buf("h16")
        self.hT = sb("hT", [128, 8, 128], BF16)
        self.hT_buf = Buf("hT")
        self.big1 = sb("big1", [128, 2048], F32)
        self.big1_buf = Buf("big1")
        self.big2 = sb("big2", [128, 2048], F32)
        self.big2_buf = Buf("big2")
        self.big2_lane = self.newlane()
        self.scr = sb("scr", [128, 8192], BF16)
        self.scr_buf = Buf("scr")
        self.ub = [sb(f"ub{l}", [128, 8, 130], F32) for l in range(2)]
        self.ub_buf = [Buf(f"ub{l}") for l in range(2)]
        self.bacc = sb("bacc", [128, 8, 128], BF16)
        self.bacc_buf = Buf("bacc")
        self.lnp = sb("lnp", [128, 2, D], F32)
        self.lnp_buf = Buf("lnp")
        self.lnp_lane = self.newlane()
        self.qT = sb("qT", [128, 16, 128], BF16)
        self.qT_buf = Buf("qT")
        self.QT = self.qT[:, 0:8, :]
        self.QT_buf = self.qT_buf
        self.oTb = self.bacc
        self.oTb_buf = self.bacc_buf
        self.Cs = sb("Cs", [16, 512], BF16)
        self.Cs_buf = Buf("Cs")
        self.dg = [sb(f"dg{i}", [128, GRP, 128], BF16) for i in range(2)]
        self.dg_buf = [Buf(f"dg{i}") for i in range(2)]
        self.ab = [self.dg[i][:].rearrange("p a b -> p (a b)") for i in range(2)]
        self.ab_buf = self.dg_buf
        self.v16 = sb("v16", [128, 16, 16], F32)
        self.i16 = sb("i16", [128, 16, 16], U32)
        self.i16f = sb("i16f", [128, 16, 16], F32)
        self.tmp1 = sb("tmp1", [128, 128], F32)
        self.tmp2 = sb("tmp2", [128, 256], F32)
        self.best = sb("best", [128, 8, 16], F32)
        self.pos = sb("pos", [128, 8, 16], U32)
        self.posa = sb("posa", [128, 8, 16], I32)
        self.posb = sb("posb", [128, 8, 16], I32)
        self.af = sb("af", [128, 8, 16], F32)
        self.bf = sb("bf", [128, 8, 16], F32)
        self.e0 = sb("e0", [128, 8, 16], F32)
        self.e1 = sb("e1", [128, 8, 16], F32)
        self.ef = sb("ef", [128, 128], F32)
        self.eidx = sb("eidx", [128, 128], I32)
        self.gw = sb("gw", [128, 8, 16], F32)
        self.gex = sb("gex", [128, 8, 16], F32)
        self.gsum = sb("gsum", [128, 8], F32)
        self.actt = sb("actt", [128, 128], F32)
        self.gel = sb("gel", [128, 128], F32)
        self.w4 = sb("w4", [128, 128], F32)
        self.small_buf = {n: Buf(n) for n in ["v16", "i16", "i16f", "tmp1", "tmp2", "best", "pos", "posa", "posb",
                                                "af", "bf", "e0", "e1", "ef", "eidx", "gw", "gex", "gsum",
                                                "actt", "gel", "w4", "lnst", "cst", "cin"]}
        self.lnst = sb("lnst", [128, 16], F32)
        self.lnmv = sb("lnmv", [128, 4], F32)
        self.cst = sb("cst", [2, D], F32)
        self.cst_lane = self.newlane()
        self.cin = sb("cin", [2, D], F32)
        self.cin_lane = self.newlane()
        self.wdw = sb("wdwS", [128, 48], F32)
        self.wdw_buf = Buf("wdw")
        self.wdw_lane = self.newlane()
        self.iot = sb("iot", [128, 128], I32)
        self.identf = sb("identf", [128, 128], F32)
        self.identb = sb("identb", [128, 128], BF16)
        self.trineg = sb("trineg", [128, 128], BF16)
        self.mask4 = sb("mask4", [128, 4, 128], BF16)
        self.indall = sb("indall", [128, 32], BF16)
        self.seli = sb("seli", [16, 128], I32)
        self.selneg = sb("selneg", [16, 16, 128], BF16)
        self.iot16i = sb("iot16i", [128, 16], I32)
        self.iot16 = sb("iot16", [128, 16], F32)
        self.const_buf = Buf("const")
        self.P = [self.es.enter_context(nc.psum_tensor(f"ps{i}", [128, 1024], F32)) for i in range(4)]
        self.pbuf = [Buf(f"bank{i}") for i in range(8)]

        self.emit_consts()
        self.emit_phase0()
        self.emit_tiles()
        self.emit_final()
        self.replay()

    def bank(self, i):
        return self.P[i // 2][:, (i % 2) * 512:(i % 2 + 1) * 512]

    def emit_consts(self):
        cb = [self.const_buf]
        self.G("iota", [], cb, out=self.iot[:], pattern=[[1, 128]], base=0, channel_multiplier=-1)
        self.G("iota", [], cb, out=self.iot16i[:], pattern=[[1, 16]], base=0, channel_multiplier=0)
        V = self.V
        V("tensor_scalar", cb, cb, out=self.identf[:], in0=self.iot[:], scalar1=0.0, scalar2=None, op0=ALU.is_equal)
        V("tensor_scalar", cb, cb, out=self.identb[:], in0=self.iot[:], scalar1=0.0, scalar2=None, op0=ALU.is_equal)
        V("tensor_scalar", cb, cb, out=self.trineg[:], in0=self.iot[:], scalar1=0.0, scalar2=-1.0,
          op0=ALU.is_le, op1=ALU.mult)
        for hh in range(4):
            V("tensor_scalar", cb, cb, out=self.mask4[:, hh, :], in0=self.iot[:], scalar1=0.0, scalar2=None,
              op0=ALU.is_gt)
        V("memset", [], cb, ap=self.indall[:], constant=0.0)
        V("memset", cb, cb, ap=self.indall[:, 15:16], constant=1.0)
        for kbi in range(16):
            self.G("iota", cb, cb, out=self.seli[:], pattern=[[0, 128]], base=-kbi, channel_multiplier=1)
            V("tensor_scalar", cb, cb, out=self.selneg[:, kbi, :], in0=self.seli[:], scalar1=0.0, scalar2=-1.0,
              op0=ALU.is_gt, op1=ALU.mult)
        V("tensor_copy", cb, cb, out=self.iot16[:], in_=self.iot16i[:])
        self.dma("sync", self.wdw_lane, [], [self.wdw_buf], out=self.wdw[:], in_=self.wdw_d[:, :])

    def emit_phase0(self):
        for ci, (kind, src) in enumerate(self.chunks):
            s = ci % NWS
            if kind == "w":
                srcap = src.rearrange("(dc p) f -> p dc f", p=128)
                dst = self.wsl[:, s, :, :]
                dram = self.wbf[ci].rearrange("p (dc f) -> p dc f", dc=8)
            else:
                srcap = src
                dst = self.wsl[:, s, 0:4, :].rearrange("p a b -> p (a b)")
                dram = self.wbf[ci][:, 0:2048]
            self.dma("gpsimd", self.wsl_lane[s], [], [self.wsl_buf[s]], out=dst, in_=srcap)
            self.dma("sync", self.wsl_lane[s], [self.wsl_buf[s]], [self.wbf_buf[ci]], out=dram, in_=dst)

    def tile_chunk_order(self):
        o = []
        for l in range(2):
            if l >= self.n_layers:
                break
            o += [self.cid[(("w_in", l), j)] for j in range(6)]
            o += [self.cid[(("w_out", l), j)] for j in range(2)]
            o += [self.cid[(("pwq", l), j)] for j in range(4)]
            o += [self.cid[(("sk", l), 0)]]
        if self.n_layers > 2:
            o += [self.cid[(("wk", 0), j)] for j in range(2)]
            o += [self.cid[(("wv", 0), j)] for j in range(2)]
        for l in range(2, 4):
            if l >= self.n_layers:
                break
            o += [self.cid[(("sbq", l - 2), j)] for j in range(2)]
            o += [self.cid[(("sbo", l - 2), j)] for j in range(2)]
            o += [self.cid[(("pwq", l), j)] for j in range(4)]
            o += [self.cid[(("sk", l), 0)]]
        return o

    def wnext(self, expect):
        while self.w_issued < min(len(self.wlist), self.w_i + NWS):
            ci = self.wlist[self.w_issued]
            s = self.w_issued % NWS
            kind = self.chunks[ci][0]
            if kind == "w":
                dst = self.wsl[:, s, :, :]
                dram = self.wbf[ci].rearrange("p (dc f) -> p dc f", dc=8)
            else:
                dst = self.wsl[:, s, 0:4, :].rearrange("p a b -> p (a b)")
                dram = self.wbf[ci][:, 0:2048]
            self.dma("sync", self.wsl_lane[s], [self.wbf_buf[ci]], [self.wsl_buf[s]], out=dst, in_=dram)
            self.w_issued += 1
        ci = self.wlist[self.w_i]
        assert ci == self.cid[expect], (ci, expect)
        s = self.w_i % NWS
        self.w_i += 1
        return s

    def emit_tiles(self):
        seqs = []
        for b in range(self.n_pseq):
            seqs.append(dict(kind="p", b=b, nt=self.n_ptiles, nvalid=128, kb0=0))
        for b in range(self.n_sseq):
            seqs.append(dict(kind="s", b=b, nt=1, nvalid=16, kb0=self.n_past))
        ntiles = sum(s["nt"] for s in seqs)
        self.wlist = self.tile_chunk_order() * ntiles
        self.w_i = 0
        self.w_issued = 0
        self.tcount = 0
        self.stores = []
        for seq in seqs:
            if seq["kind"] == "s" and self.n_layers > 2:
                self.load_cache(seq)
            for ti in range(seq["nt"]):
                self.tile(seq, ti)
                self.tcount += 1

    def load_cache(self, seq):
        b = seq["b"]
        for kb in range(self.n_past):
            self.dma("sync", self.big2_lane, [], [self.big2_buf], out=self.big2[:, 0:D],
                     in_=self.ck[b, kb * 128:(kb + 1) * 128, :])
            self.act([self.big2_buf], [self.h16_buf], out=self.h16[:], in_=self.big2[:, 0:D], func=AF.Copy)
            self.k_transposes(kb)
            self.dma("sync", self.big2_lane, [], [self.big2_buf], out=self.big2[:, 0:D],
                     in_=self.cv[b, kb * 128:(kb + 1) * 128, :])
            self.act([self.big2_buf], [self.V_buf[kb]], out=self.Vr[:, kb, :], in_=self.big2[:, 0:D], func=AF.Copy)

    def k_transposes(self, kb):
        bi = 0
        pb = self.bank(bi).bitcast(BF16)
        for pair in range(8):
            self.T("transpose", [self.h16_buf, self.const_buf], [self.pbuf[bi]],
                   out=pb[:, pair * 128:(pair + 1) * 128], in_=self.h16[:, pair * 128:(pair + 1) * 128],
                   identity=self.identb[:])
        self.act([self.pbuf[bi]], [self.KT_buf[kb]], out=self.KT[:, :, kb * 128:(kb + 1) * 128],
                 in_=pb.rearrange("p (a b) -> p a b", a=8), func=AF.Copy)

    def make_hT(self, hi):
        self.act([self.hb_buf[hi]], [self.h16_buf], out=self.h16[:], in_=self.hb[hi][:], func=AF.Copy)
        bi = 1
        pb = self.bank(bi).bitcast(BF16)
        for c in range(8):
            self.T("transpose", [self.h16_buf, self.const_buf], [self.pbuf[bi]],
                   out=pb[:, c * 128:(c + 1) * 128], in_=self.h16[:, c * 128:(c + 1) * 128],
                   identity=self.identb[:])
        self.V("tensor_copy", [self.pbuf[bi]], [self.hT_buf], out=self.hT[:].rearrange("p a b -> p (a b)"),
               in_=pb)

    def layernorm(self, idx, hi):
        self.dma("sync", self.lnp_lane, [], [self.lnp_buf], out=self.lnp[:, 0:1, :],
                 in_=self.lng[idx:idx + 1, :].partition_broadcast(128))
        self.dma("sync", self.lnp_lane, [], [self.lnp_buf], out=self.lnp[:, 1:2, :],
                 in_=self.lnb[idx:idx + 1, :].partition_broadcast(128))
        sbuf = self.small_buf["lnst"]
        V = self.V
        V("bn_stats", [self.pre_buf], [sbuf], out=self.lnst[:, 0:6], in_=self.pre[:, 0:512])
        V("bn_stats", [self.pre_buf], [sbuf], out=self.lnst[:, 6:12], in_=self.pre[:, 512:1024])
        V("bn_aggr", [sbuf], [sbuf], out=self.lnmv[:, 0:2], in_=self.lnst[:, 0:12])
        V("tensor_scalar", [sbuf], [sbuf], out=self.lnmv[:, 2:3], in0=self.lnmv[:, 1:2], scalar1=EPS, scalar2=None,
          op0=ALU.add)
        self.act([sbuf], [sbuf], out=self.lnmv[:, 2:3], in_=self.lnmv[:, 2:3], func=AF.Ln)
        self.act([sbuf], [sbuf], out=self.lnmv[:, 3:4], in_=self.lnmv[:, 2:3], func=AF.Exp, scale=-0.5)
        V("tensor_scalar", [self.pre_buf, sbuf], [self.pre_buf], out=self.pre[:], in0=self.pre[:],
          scalar1=self.lnmv[:, 0:1], scalar2=self.lnmv[:, 3:4], op0=ALU.subtract, op1=ALU.mult)
        V("tensor_tensor", [self.pre_buf, self.lnp_buf], [self.pre_buf], out=self.pre[:], in0=self.pre[:],
          in1=self.lnp[:, 0, :], op=ALU.mult)
        V("tensor_tensor", [self.pre_buf, self.lnp_buf], [self.hb_buf[hi]], out=self.hb[hi][:], in0=self.pre[:],
          in1=self.lnp[:, 1, :], op=ALU.add)

    def residual(self, hi, pbanks):
        pidx = pbanks
        self.V("scalar_tensor_tensor", [self.hb_buf[hi], self.pbuf[2 * pidx], self.pbuf[2 * pidx + 1]],
               [self.pre_buf], out=self.pre[:], in0=self.hb[hi][:], scalar=ALPHA, in1=self.P[pidx][:, :],
               op0=ALU.mult, op1=ALU.add)

    def tok_major_mm(self, key, lhs, lhs_buf, pidx):
        for half in range(2):
            s = self.wnext((key, half))
            bi = 2 * pidx + half
            for dc in range(8):
                self.mm([lhs_buf, self.wsl_buf[s]], [self.pbuf[bi]], out=self.bank(bi), lhsT=lhs[:, dc, :],
                        rhs=self.wsl[:, s, dc, :], start=(dc == 0), stop=(dc == 7))

    def feat_major_chunk(self, key, j, bi):
        s = self.wnext((key, j))
        for fc in range(4):
            for dc in range(8):
                self.mm([self.hT_buf, self.wsl_buf[s]], [self.pbuf[bi]],
                        out=self.bank(bi)[:, fc * 128:(fc + 1) * 128], lhsT=self.wsl[:, s, dc, fc * 128:(fc + 1) * 128],
                        rhs=self.hT[:, dc, :], start=(dc == 0), stop=(dc == 7))

    def tile(self, seq, ti):
        hi = self.tcount % 2
        b, nv = seq["b"], seq["nvalid"]
        h = self.hb[hi]
        hbuf = self.hb_buf[hi]
        if seq["kind"] == "p":
            self.dma("sync", self.hb_lane_in[hi], [], [hbuf], out=h[:], in_=self.xp[b, ti * 128:(ti + 1) * 128, :])
        else:
            self.V("memset", [], [hbuf], ap=h[:], constant=0.0)
            self.dma("sync", self.hb_lane_in[hi], [], [hbuf], out=h[0:16, :], in_=self.xs[b, :, :])
        self.make_hT(hi)
        kb = seq["kb0"] + ti
        for l in range(self.n_layers):
            if l < 2:
                self.conv_layer(l, seq, ti, hi)
            else:
                if l == 2:
                    self.kv(seq, ti, hi, kb)
                self.attn_layer(l - 2, seq, ti, hi, kb)
            self.layernorm(2 * l, hi)
            self.make_hT(hi)
            self.peer(l, hi)
            self.layernorm(2 * l + 1, hi)
            if l < self.n_layers - 1:
                self.make_hT(hi)
        if seq["kind"] == "p":
            ev = self.dma("sync", self.hb_lane_out[hi], [hbuf], [], out=self.yp[b, ti * 128:(ti + 1) * 128, :], in_=h[:])
        else:
            ev = self.dma("sync", self.hb_lane_out[hi], [hbuf], [], out=self.ys[b, :, :], in_=h[0:16, :])
        self.stores.append(ev)

    def conv_layer(self, l, seq, ti, hi):
        V = self.V
        ub, ubb = self.ub[l], self.ub_buf[l]
        gates = self.scr[:].bitcast(F32)[:, 0:3072].rearrange("p (a b) -> p a b", a=24)
        if ti == 0:
            if seq["kind"] == "p":
                V("memset", [], [ubb], ap=ub[:, :, 0:2], constant=0.0)
            else:
                cb = self.small_buf["cin"]
                self.dma("sync", self.cin_lane, [], [cb], out=self.cin[:], in_=self.stc[l, seq["b"], :, :])
                bi = 0
                for c in range(8):
                    self.T("transpose", [cb, self.const_buf], [self.pbuf[bi]], out=self.bank(bi)[:, 2 * c:2 * c + 2],
                           in_=self.cin[0:2, c * 128:(c + 1) * 128], identity=self.identf[0:2, 0:2])
                V("tensor_copy", [self.pbuf[bi]], [ubb], out=ub[:, :, 0:2],
                  in_=self.bank(bi)[:, 0:16].rearrange("p (a b) -> p a b", a=8))
        for j in range(6):
            bi = j % 2
            self.feat_major_chunk(("w_in", l), j, bi)
            self.act([self.pbuf[bi]], [self.scr_buf], out=gates[:, 4 * j:4 * j + 4, :],
                     in_=self.bank(bi).rearrange("p (a b) -> p a b", a=4), func=AF.Copy)
        V("tensor_tensor", [self.scr_buf], [ubb], out=ub[:, :, 2:130], in0=gates[:, 8:16, :], in1=gates[:, 16:24, :],
          op=ALU.mult)
        acc = self.big1[:, 0:1024].rearrange("p (a b) -> p a b", a=8)
        ab = self.big1_buf
        for c in range(8):
            col = lambda w: self.wdw[:, (l * 3 + w) * 8 + c:(l * 3 + w) * 8 + c + 1]
            V("tensor_scalar", [ubb, self.wdw_buf], [ab], out=acc[:, c, :], in0=ub[:, c, 0:128], scalar1=col(0),
              scalar2=None, op0=ALU.mult)
            V("scalar_tensor_tensor", [ubb, ab, self.wdw_buf], [ab], out=acc[:, c, :], in0=ub[:, c, 1:129],
              scalar=col(1), in1=acc[:, c, :], op0=ALU.mult, op1=ALU.add)
            V("scalar_tensor_tensor", [ubb, ab, self.wdw_buf], [ab], out=acc[:, c, :], in0=ub[:, c, 2:130],
              scalar=col(2), in1=acc[:, c, :], op0=ALU.mult, op1=ALU.add)
        V("tensor_tensor", [self.scr_buf, ab], [self.bacc_buf], out=self.bacc[:], in0=gates[:, 0:8, :], in1=acc,
          op=ALU.mult)
        nv = seq["nvalid"]
        if ti == seq["nt"] - 1:
            cb = self.small_buf["cst"]
            for c in range(8):
                bi = c // 4
                self.T("transpose", [ubb, self.const_buf], [self.pbuf[bi]],
                       out=self.P[0][0:2, c * 128:(c + 1) * 128], in_=ub[:, c, nv:nv + 2], identity=self.identf[:])
            self.act([self.pbuf[0], self.pbuf[1]], [cb], out=self.cst[0:2, :], in_=self.P[0][0:2, :], func=AF.Copy)
            dst = (self.ncp if seq["kind"] == "p" else self.ncs)[l, seq["b"], :, :]
            ev = self.dma("sync", self.cst_lane, [cb], [], out=dst, in_=self.cst[0:2, :])
            self.stores.append(ev)
        else:
            V("tensor_copy", [ubb], [ubb], out=ub[:, :, 0:2], in_=ub[:, :, 128:130])
        self.tok_major_mm(("w_out", l), self.bacc, self.bacc_buf, 0)
        self.residual(hi, 0)

    def peer(self, l, hi):
        V, A = self.V, self.A
        sm = self.small_buf
        h, hbuf = self.hb[hi], self.hb_buf[hi]
        for j in range(4):
            bi = j % 2
            self.feat_major_chunk(("pwq", l), j, bi)
            self.act([self.pbuf[bi]], [self.qT_buf], out=self.qT[:, 4 * j:4 * j + 4, :],
                     in_=self.bank(bi).rearrange("p (a b) -> p a b", a=4), func=AF.Copy)
        s = self.wnext((("sk", l), 0))
        skT = self.wsl[:, s, 0:4, :].rearrange("p a b -> p (a b)")
        scs = self.big1[:].rearrange("p (a b) -> p a b", a=16)
        for g in range(16):
            bi = 4 + g // 4
            self.mm([self.qT_buf, self.wsl_buf[s]], [self.pbuf[bi]], out=self.bank(bi)[:, (g % 4) * 128:(g % 4 + 1) * 128],
                    lhsT=self.qT[:, g, :], rhs=skT[:, g * 128:(g + 1) * 128], start=True, stop=True)
        for q in range(4):
            self.act([self.pbuf[4 + q]], [self.big1_buf], out=self.big1[:, q * 512:(q + 1) * 512], in_=self.bank(4 + q),
                     func=AF.Copy)
        b1 = self.big1_buf
        for g in range(16):
            V("max", [b1], [sm["v16"]], out=self.v16[:, g, 0:8], in_=scs[:, g, :])
            V("max_index", [b1, sm["v16"]], [sm["i16"]], out=self.i16[:, g, 0:8], in_max=self.v16[:, g, 0:8],
              in_values=scs[:, g, :])
            V("match_replace", [b1, sm["v16"]], [sm["tmp1"]], out=self.tmp1[:], in_to_replace=self.v16[:, g, 0:8],
              in_values=scs[:, g, :], imm_value=NEG)
            V("max", [sm["tmp1"]], [sm["v16"]], out=self.v16[:, g, 8:16], in_=self.tmp1[:])
            V("max_index", [sm["tmp1"], sm["v16"]], [sm["i16"]], out=self.i16[:, g, 8:16], in_max=self.v16[:, g, 8:16],
              in_values=self.tmp1[:])
        V("tensor_copy", [sm["i16"]], [sm["i16f"]], out=self.i16f[:], in_=self.i16[:])
        cand = self.big2[:].rearrange("p (h a b) -> p h a b", h=8, a=16)
        v16h = self.v16[:].rearrange("p (h t) k -> p h t k", t=2)
        i16h = self.i16f[:].rearrange("p (h t) k -> p h t k", t=2)
        V("tensor_tensor", [sm["v16"]], [self.big2_buf], out=cand,
          in0=v16h[:, :, 0, :].unsqueeze(3).to_broadcast([128, 8, 16, 16]),
          in1=v16h[:, :, 1, :].unsqueeze(2).to_broadcast([128, 8, 16, 16]), op=ALU.add)
        b2 = self.big2_buf
        for hh in range(8):
            cf = self.big2[:, hh * 256:(hh + 1) * 256]
            V("max", [b2], [sm["best"]], out=self.best[:, hh, 0:8], in_=cf)
            V("max_index", [b2, sm["best"]], [sm["pos"]], out=self.pos[:, hh, 0:8], in_max=self.best[:, hh, 0:8],
              in_values=cf)
            V("match_replace", [b2, sm["best"]], [sm["tmp2"]], out=self.tmp2[:], in_to_replace=self.best[:, hh, 0:8],
              in_values=cf, imm_value=NEG)
            V("max", [sm["tmp2"]], [sm["best"]], out=self.best[:, hh, 8:16], in_=self.tmp2[:])
            V("max_index", [sm["tmp2"], sm["best"]], [sm["pos"]], out=self.pos[:, hh, 8:16],
              in_max=self.best[:, hh, 8:16], in_values=self.tmp2[:])
        posi = self.pos[:].bitcast(I32)
        V("tensor_single_scalar", [sm["pos"]], [sm["posa"]], out=self.posa[:], in_=posi, scalar=4,
          op=ALU.arith_shift_right)
        V("tensor_single_scalar", [sm["pos"]], [sm["posb"]], out=self.posb[:], in_=posi, scalar=15,
          op=ALU.bitwise_and)
        V("tensor_copy", [sm["posa"]], [sm["af"]], out=self.af[:], in_=self.posa[:])
        V("tensor_copy", [sm["posb"]], [sm["bf"]], out=self.bf[:], in_=self.posb[:])
        eq = self.big1[:].rearrange("p (h a b) -> p h a b", h=8, a=16)
        io = self.iot16[:].unsqueeze(1).unsqueeze(1).to_broadcast([128, 8, 16, 16])
        for (src, srcb, t, dst, dstb) in ((self.af, sm["af"], 0, self.e0, sm["e0"]), (self.bf, sm["bf"], 1, self.e1, sm["e1"])):
            V("tensor_tensor", [srcb, self.const_buf], [b1], out=eq,
              in0=src[:].unsqueeze(3).to_broadcast([128, 8, 16, 16]), in1=io, op=ALU.is_equal)
            V("tensor_tensor", [b1, sm["i16f"]], [b1], out=eq, in0=eq,
              in1=i16h[:, :, t, :].unsqueeze(2).to_broadcast([128, 8, 16, 16]), op=ALU.mult)
            V("tensor_reduce", [b1], [dstb], out=dst[:], in_=eq, axis=AX.X, op=ALU.add)
        V("scalar_tensor_tensor", [sm["e0"], sm["e1"]], [sm["ef"]], out=self.ef[:],
          in0=self.e0[:].rearrange("p a b -> p (a b)"), scalar=128.0, in1=self.e1[:].rearrange("p a b -> p (a b)"),
          op0=ALU.mult, op1=ALU.add)
        if l > 0:
            V("tensor_scalar", [sm["ef"]], [sm["ef"]], out=self.ef[:], in0=self.ef[:], scalar1=float(l * 16384),
              scalar2=None, op0=ALU.add)
        V("tensor_copy", [sm["ef"]], [sm["eidx"]], out=self.eidx[:], in_=self.ef[:])
        V("tensor_tensor", [sm["best"]], [sm["gex"]], out=self.gex[:], in0=self.best[:],
          in1=self.best[:, :, 0:1].to_broadcast([128, 8, 16]), op=ALU.subtract)
        self.act([sm["gex"]], [sm["gex"]], out=self.gex[:], in_=self.gex[:], func=AF.Exp)
        V("tensor_reduce", [sm["gex"]], [sm["gsum"]], out=self.gsum[:], in_=self.gex[:], axis=AX.X, op=ALU.add)
        V("reciprocal", [sm["gsum"]], [sm["gsum"]], out=self.gsum[:], in_=self.gsum[:])
        V("tensor_tensor", [sm["gex"], sm["gsum"]], [sm["gw"]], out=self.gw[:], in0=self.gex[:],
          in1=self.gsum[:].unsqueeze(2).to_broadcast([128, 8, 16]), op=ALU.mult)
        gwf = self.gw[:].rearrange("p a b -> p (a b)")
        uvt = self.uv.rearrange("l e d -> (l e) d")
        for j0 in range(0, 128, GRP):
            gi = (j0 // GRP) % 2
            for jj in range(GRP):
                j = j0 + jj
                sl = j % NGS
                self.dma("gpsimd", self.gsl_lane[sl], [sm["eidx"]], [self.gsl_buf[sl]], out=self.gsl[:, sl, :],
                         in_=uvt[:, :], method="indirect_dma_start", out_offset=None,
                         in_offset=bass.IndirectOffsetOnAxis(ap=self.eidx[:, j:j + 1], axis=0))
                V("scalar_tensor_tensor", [self.gsl_buf[sl], hbuf], [self.h16_buf, sm["actt"]], out=self.h16[:],
                  in0=self.gsl[:, sl, 0:D], scalar=1.0, in1=h[:], op0=ALU.mult, op1=ALU.mult,
                  accum_out=self.actt[:, j:j + 1])
            self.act([sm["actt"]], [sm["gel"]], out=self.gel[:, j0:j0 + GRP], in_=self.actt[:, j0:j0 + GRP], func=AF.Gelu)
            V("tensor_tensor", [sm["gel"], sm["gw"]], [sm["w4"]], out=self.w4[:, j0:j0 + GRP], in0=self.gel[:, j0:j0 + GRP],
              in1=gwf[:, j0:j0 + GRP], op=ALU.mult)
            V("tensor_tensor", [sm["w4"], self.const_buf], [self.dg_buf[gi]], out=self.dg[gi][:],
              in0=self.identb[:].unsqueeze(1).to_broadcast([128, GRP, 128]),
              in1=self.w4[:, j0:j0 + GRP].unsqueeze(2).to_broadcast([128, GRP, 128]), op=ALU.mult)
            for jj in range(GRP):
                j = j0 + jj
                sl = j % NGS
                for half in range(2):
                    bi = 2 + half
                    self.mm([self.dg_buf[gi], self.gsl_buf[sl]], [self.pbuf[bi]], out=self.bank(bi),
                            lhsT=self.dg[gi][:, jj, :], rhs=self.gsl[:, sl, D + half * 512:D + (half + 1) * 512],
                            start=(j == 0), stop=(j == 127))
        self.residual(hi, 1)

    def kv(self, seq, ti, hi, kb):
        nv, b = seq["nvalid"], seq["b"]
        r0 = ti * 128
        self.tok_major_mm(("wk", 0), self.hT, self.hT_buf, 0)
        self.act([self.pbuf[0], self.pbuf[1]], [self.pre_buf], out=self.pre[:], in_=self.P[0][:, :], func=AF.Copy)
        self.act([self.pbuf[0], self.pbuf[1]], [self.h16_buf], out=self.h16[:], in_=self.P[0][:, :], func=AF.Copy)
        if seq["kind"] == "p":
            ev = self.dma("sync", self.pre_lane, [self.pre_buf], [], out=self.nkp[b, r0:r0 + 128, :], in_=self.pre[:])
        else:
            ev = self.dma("sync", self.pre_lane, [self.pre_buf], [], out=self.nks[b, :, :], in_=self.pre[0:16, :])
        self.stores.append(ev)
        self.k_transposes(kb)
        self.tok_major_mm(("wv", 0), self.hT, self.hT_buf, 0)
        self.act([self.pbuf[0], self.pbuf[1]], [self.big2_buf], out=self.big2[:, 0:D], in_=self.P[0][:, :], func=AF.Copy)
        self.act([self.pbuf[0], self.pbuf[1]], [self.V_buf[kb]], out=self.Vr[:, kb, :], in_=self.P[0][:, :], func=AF.Copy)
        if seq["kind"] == "p":
            ev = self.dma("sync", self.big2_lane, [self.big2_buf], [], out=self.nvp[b, r0:r0 + 128, :], in_=self.big2[:, 0:D])
        else:
            ev = self.dma("sync", self.big2_lane, [self.big2_buf], [], out=self.nvs[b, :, :], in_=self.big2[0:16, 0:D])
        self.stores.append(ev)

    def attn_layer(self, j, seq, ti, hi, kb):
        V = self.V
        nkb = kb + 1
        V("memset", [], [self.qT_buf], ap=self.qT[:], constant=0.0)
        for jj in range(2):
            bi = jj % 2
            self.feat_major_chunk(("sbq", j), jj, bi)
            for r in range(2):
                self.act([self.pbuf[bi]], [self.qT_buf], out=self.qT[r * 64:(r + 1) * 64, 8 * jj + r:8 * jj + 8:2, :],
                         in_=self.bank(bi)[r * 64:(r + 1) * 64, :].rearrange("p (a b) -> p a b", a=4), func=AF.Copy,
                         scale=0.125)
        if DBG == 1:
            self.V("tensor_scalar", [self.hb_buf[hi]], [self.pre_buf], out=self.pre[:], in0=self.hb[hi][:], scalar1=ALPHA,
                   scalar2=None, op0=ALU.mult)
            for half in range(2):
                self.wnext((("sbo", j), half))
            return
        Pb = self.scr[:].rearrange("p (a b) -> p a b", a=16)
        Eb = self.big2[:, 0:512]
        oT = self.P[1]
        first_done = {}
        for g in range(4):
            heads = [4 * g + hh for hh in range(4)]

            def zmm(bi, start_first, last_stop):
                for hh, hd in enumerate(heads):
                    pair, r = hd // 2, hd % 2
                    self.mm([self.KT_buf[kbi_], self.QT_buf], [self.pbuf[bi]],
                            out=self.bank(bi)[:, hh * 128:(hh + 1) * 128],
                            lhsT=self.KT[:, pair, kbi_ * 128:(kbi_ + 1) * 128],
                            rhs=self.qT[:, hd, :],
                            start=(start_first and hh == 0), stop=(last_stop and hh == 3), skip_group_check=True)
            for kbi_ in range(nkb):
                zb = 4 + kbi_ % 2
                zmm(zb, True, True)
                self.act([self.pbuf[zb]], [self.big2_buf], out=Eb, in_=self.bank(zb), func=AF.Exp)
                self.act([self.big2_buf], [self.scr_buf], out=Pb[:, kbi_, :], in_=Eb, func=AF.Ln, bias=1.0)
                if kbi_ == kb:
                    V("tensor_tensor", [self.scr_buf, self.const_buf], [self.scr_buf], out=Pb[:, kbi_, :], in0=Pb[:, kbi_, :],
                      in1=self.mask4[:].rearrange("p a b -> p (a b)"), op=ALU.mult)
                self.mm([self.scr_buf, self.const_buf], [self.pbuf[6]], out=self.bank(6)[0:16, :],
                        lhsT=self.indall[:, 15 - kbi_:31 - kbi_], rhs=Pb[:, kbi_, :], start=(kbi_ == 0),
                        stop=(kbi_ == nkb - 1))
            if DBG == 2:
                continue
            self.act([self.pbuf[6]], [self.Cs_buf], out=self.Cs[:], in_=self.bank(6)[0:16, :], func=AF.Copy)
            for kbi_ in range(nkb):
                zs = kbi_ % 2
                ai = kbi_ % 2
                self.mm([self.scr_buf, self.const_buf], [self.pbuf[zs]], out=self.bank(zs), lhsT=self.trineg[:],
                        rhs=Pb[:, kbi_, :], start=True, stop=False, skip_group_check=True)
                self.mm([self.Cs_buf, self.const_buf], [self.pbuf[zs]], out=self.bank(zs), lhsT=self.selneg[0:16, kbi_, :],
                        rhs=self.Cs[:], start=False, stop=False, skip_group_check=True)
                zmm(zs, False, True)
                self.act([self.pbuf[zs]], [self.ab_buf[ai]], out=self.ab[ai], in_=self.bank(zs), func=AF.Exp)
                if DBG == 3:
                    continue
                if kbi_ == kb:
                    V("tensor_tensor", [self.ab_buf[ai], self.const_buf], [self.ab_buf[ai]], out=self.ab[ai],
                      in0=self.ab[ai], in1=self.mask4[:].rearrange("p a b -> p (a b)"), op=ALU.mult)
                for hh, hd in enumerate(heads):
                    pair, r = hd // 2, hd % 2
                    bk = pair // 4
                    key = (bk, r)
                    st = key not in first_done
                    first_done[key] = True
                    self.mm([self.ab_buf[ai], self.V_buf[kbi_]], [self.pbuf[2 + bk]],
                            out=oT[r * 64:(r + 1) * 64, pair * 128:(pair + 1) * 128],
                            lhsT=self.Vr[:, kbi_, hd * 64:(hd + 1) * 64], rhs=self.ab[ai][:, hh * 128:(hh + 1) * 128],
                            start=st, stop=(kbi_ == nkb - 1), skip_group_check=True)
        if DBG in (2, 3):
            self.V("tensor_scalar", [self.hb_buf[hi]], [self.pre_buf], out=self.pre[:], in0=self.hb[hi][:], scalar1=ALPHA,
                   scalar2=None, op0=ALU.mult)
            for half in range(2):
                self.wnext((("sbo", j), half))
            return
        self.act([self.pbuf[2], self.pbuf[3]], [self.oTb_buf], out=self.oTb[:].rearrange("p a b -> p (a b)"),
                 in_=oT[:, :], func=AF.Copy)
        self.tok_major_mm(("sbo", j), self.oTb, self.oTb_buf, 0)
        self.residual(hi, 0)

    def emit_final(self):
        eng = self.eng["sync"]
        for ev in self.stores:
            self._wait_for(eng, ev, "raw")

    def replay(self):
        nc = self.nc
        engs = self.eng
        with nc.Block() as block:
            def run(e, name):
                for it in engs[name].q:
                    if it[0] == "w":
                        e.wait_ge(it[1], it[2])
                    else:
                        _, method, kw, sem, inc = it
                        ins = getattr(e, method)(**kw)
                        ins.then_inc(sem, inc)

            @block.sync
            def _(e):
                run(e, "sync")

            @block.gpsimd
            def _(e):
                run(e, "gpsimd")

            @block.tensor
            def _(e):
                run(e, "tensor")

            @block.vector
            def _(e):
                run(e, "vector")

            @block.scalar
            def _(e):
                run(e, "scalar")
        self.es.close()


def make_in_maps(inputs, n_cores, n_pseq, n_sseq):
    f = lambda a: np.ascontiguousarray(np.asarray(a, dtype=np.float32))
    xp, xs = f(inputs["x_prompt"]), f(inputs["x_sample"])
    stc, ck, cv = f(inputs["state_conv"]), f(inputs["cache_k"]), f(inputs["cache_v"])
    ck = ck.reshape(ck.shape[0], ck.shape[1], -1)
    cv = cv.reshape(cv.shape[0], cv.shape[1], -1)
    wdw = f(inputs["conv_w_dw"])
    wdw_l = np.ascontiguousarray(wdw.reshape(2, 3, 8, 128).transpose(3, 0, 1, 2).reshape(128, 48))
    sk = f(inputs["peer_subkeys"])
    skT = np.ascontiguousarray(sk.transpose(0, 4, 1, 2, 3).reshape(4, 128, 2048))
    uv = np.concatenate([f(inputs["peer_u"]), f(inputs["peer_v"])], axis=-1)
    shared = dict(w_in=f(inputs["conv_w_in"]), wdw=wdw_l, w_out=f(inputs["conv_w_out"]), sbq=f(inputs["sb_w_q"]),
                  sbo=f(inputs["sb_w_o"]), wk=f(inputs["kv_w_k"]), wv=f(inputs["kv_w_v"]), pwq=f(inputs["peer_w_q"]),
                  skT=skT, uv=uv, lng=f(inputs["ln_g"]).reshape(8, D), lnb=f(inputs["ln_b"]).reshape(8, D))
    maps = []
    for c in range(n_cores):
        m = dict(shared)
        m["xp"] = np.ascontiguousarray(xp[c * n_pseq:(c + 1) * n_pseq])
        m["xs"] = np.ascontiguousarray(xs[c * n_sseq:(c + 1) * n_sseq])
        m["stc"] = np.ascontiguousarray(stc[:, c * n_sseq:(c + 1) * n_sseq])
        m["ck"] = np.ascontiguousarray(ck[c * n_sseq:(c + 1) * n_sseq])
        m["cv"] = np.ascontiguousarray(cv[c * n_sseq:(c + 1) * n_sseq])
        maps.append(m)
    return maps


def assemble(results, n_cores):
    cat = lambda k, ax: np.concatenate([np.asarray(r[k], dtype=np.float32) for r in results], axis=ax)
    yp, ys = cat("yp", 0), cat("ys", 0)
    ncp, ncs = cat("ncp", 1), cat("ncs", 1)
    nkp, nvp, nks, nvs = cat("nkp", 0), cat("nvp", 0), cat("nks", 0), cat("nvs", 0)
    r4 = lambda a: a.reshape(a.shape[0], a.shape[1], 16, 64)
    return (yp, ys, ncp, r4(nkp), r4(nvp), ncs, r4(nks), r4(nvs))


_PROG = {}


def kernel(**inputs):
    n_cores = 8
    if "full" not in _PROG:
        _PROG["full"] = Prog(4, 16, 4, 8)
    prog = _PROG["full"]
    maps = make_in_maps(inputs, n_cores, 4, 4)
    res = run_bass_kernel_spmd(prog.nc, maps, core_ids=list(range(n_cores)))
    return assemble(res.results, n_cores)
```

```python
import numpy as np
from contextlib import ExitStack
import concourse.bass as bass
import concourse.mybir as mybir
from concourse.bass_utils import run_bass_kernel_spmd

F32 = mybir.dt.float32
BF16 = mybir.dt.bfloat16
I32 = mybir.dt.int32
U32 = mybir.dt.uint32
AF = mybir.ActivationFunctionType
ALU = mybir.AluOpType
AX = mybir.AxisListType

D = 1024
ALPHA = (2.0 * 4) ** 0.25
EPS = 1e-5
NEG = -1.0e30
SEM_LIMIT = 32000
SAME_RAW = True
NGS = 8
NWS = 3
GRP = 4
DBG = 0


class Buf:
    __slots__ = ("name", "w", "r")

    def __init__(self, name):
        self.name = name
        self.w = None
        self.r = []


class Lane:
    def __init__(self, pool, inc):
        self.pool = pool
        self.inc = inc
        self.sem = pool.pop()
        self.count = 0

    def next(self):
        if self.count + self.inc > SEM_LIMIT:
            self.sem = self.pool.pop()
            self.count = 0
        self.count += self.inc
        return (self.sem, self.count)


class Eng:
    def __init__(self, name, pool):
        self.name = name
        self.lane = Lane(pool, 1)
        self.q = []
        self.waited = {}


class Prog:
    def __init__(self, n_pseq=4, n_ptiles=16, n_sseq=4, n_past=8, n_layers=4):
        self.n_pseq, self.n_ptiles, self.n_sseq, self.n_past = n_pseq, n_ptiles, n_sseq, n_past
        self.n_layers = n_layers
        self.nc = bass.Bass("TRN2", target_bir_lowering=False)
        self.es = ExitStack()
        self.build()

    def _wait_for(self, eng, ev, kind):
        sem, val, src = ev
        if src == eng.name:
            if eng.name == "tensor" or kind != "raw" or not SAME_RAW:
                return
        key = id(sem)
        if eng.waited.get(key, 0) >= val:
            return
        eng.waited[key] = val
        eng.q.append(("w", sem, val))

    def _deps(self, eng, reads, writes):
        for b in reads:
            if b.w is not None:
                self._wait_for(eng, b.w, "raw")
        for b in writes:
            if b.w is not None:
                self._wait_for(eng, b.w, "waw")
            for r in b.r:
                self._wait_for(eng, r, "war")

    def _commit(self, ev, reads, writes):
        for b in reads:
            b.r.append(ev)
        for b in writes:
            b.w = ev
            b.r = []

    def op(self, engname, method, reads, writes, **kw):
        eng = self.eng[engname]
        self._deps(eng, reads, writes)
        sem, val = eng.lane.next()
        eng.q.append(("i", method, kw, sem, 1))
        self._commit((sem, val, engname), reads, writes)

    def dma(self, qname, lane, reads, writes, out, in_, method="dma_start", **kw):
        eng = self.eng[qname]
        self._deps(eng, reads, writes)
        sem, val = lane.next()
        kw = dict(kw)
        kw["out"] = out
        kw["in_"] = in_
        eng.q.append(("i", method, kw, sem, 16))
        ev = (sem, val, "dma")
        self._commit(ev, reads, writes)
        return ev

    def V(self, method, reads, writes, **kw):
        self.op("vector", method, reads, writes, **kw)

    def A(self, method, reads, writes, **kw):
        self.op("scalar", method, reads, writes, **kw)

    def T(self, method, reads, writes, **kw):
        self.op("tensor", method, reads, writes, **kw)

    def G(self, method, reads, writes, **kw):
        self.op("gpsimd", method, reads, writes, **kw)

    def act(self, reads, writes, out, in_, func, **kw):
        self.A("activation", reads, writes, out=out, in_=in_, func=func, **kw)

    def mm(self, reads, writes, out, lhsT, rhs, start, stop, **kw):
        self.T("matmul", reads, writes, out=out, lhsT=lhsT, rhs=rhs, start=start, stop=stop, **kw)

    def sb(self, name, shape, dt):
        return self.es.enter_context(self.nc.sbuf_tensor(name, shape, dt))

    def newlane(self):
        return Lane(self.sempool, 16)

    def build(self):
        nc = self.nc
        es = self.es
        NP, NT, NS, NPAST = self.n_pseq, self.n_ptiles, self.n_sseq, self.n_past
        SP = NT * 128
        SPAST = NPAST * 128

        def din(name, shape, dt=F32):
            return nc.dram_tensor(name, list(shape), dt, kind="ExternalInput").ap()

        def dout(name, shape, dt=F32):
            return nc.dram_tensor(name, list(shape), dt, kind="ExternalOutput").ap()

        self.xp = din("xp", [NP, SP, D])
        self.xs = din("xs", [NS, 16, D])
        self.stc = din("stc", [2, NS, 2, D])
        self.ck = din("ck", [NS, SPAST, D])
        self.cv = din("cv", [NS, SPAST, D])
        self.w_in = din("w_in", [2, D, 3 * D])
        self.wdw_d = din("wdw", [128, 48])
        self.w_out = din("w_out", [2, D, D])
        self.sbq = din("sbq", [2, D, D])
        self.sbo = din("sbo", [2, D, D])
        self.wk = din("wk", [D, D])
        self.wv = din("wv", [D, D])
        self.pwq = din("pwq", [4, D, 2 * D])
        self.skT_d = din("skT", [4, 128, 2048])
        self.uv = din("uv", [4, 16384, 2 * D])
        self.lng = din("lng", [8, D])
        self.lnb = din("lnb", [8, D])
        self.yp = dout("yp", [NP, SP, D])
        self.ys = dout("ys", [NS, 16, D])
        self.ncp = dout("ncp", [2, NP, 2, D])
        self.nkp = dout("nkp", [NP, SP, D])
        self.nvp = dout("nvp", [NP, SP, D])
        self.ncs = dout("ncs", [2, NS, 2, D])
        self.nks = dout("nks", [NS, 16, D])
        self.nvs = dout("nvs", [NS, 16, D])

        self.chunks = []
        self.cid = {}

        def addchunks(key, ap2d, ncols):
            for j in range(ncols // 512):
                self.cid[(key, j)] = len(self.chunks)
                self.chunks.append(("w", ap2d[:, j * 512:(j + 1) * 512]))

        for l in range(2):
            addchunks(("w_in", l), self.w_in[l], 3 * D)
            addchunks(("w_out", l), self.w_out[l], D)
            addchunks(("sbq", l), self.sbq[l], D)
            addchunks(("sbo", l), self.sbo[l], D)
        addchunks(("wk", 0), self.wk, D)
        addchunks(("wv", 0), self.wv, D)
        for l in range(4):
            addchunks(("pwq", l), self.pwq[l], 2 * D)
        for l in range(4):
            self.cid[(("sk", l), 0)] = len(self.chunks)
            self.chunks.append(("sk", self.skT_d[l]))
        NCH = len(self.chunks)
        self.wbf = nc.dram_tensor("wbf", [NCH, 128, 4096], BF16, kind="Internal").ap()
        self.wbf_buf = [Buf(f"wbf{i}") for i in range(NCH)]

        self.sempool = [es.enter_context(nc.semaphore(f"sm{i}")) for i in range(96)]
        self.eng = {n: Eng(n, self.sempool) for n in ["tensor", "vector", "scalar", "gpsimd", "sync"]}

        sb = self.sb
        self.wsl = sb("wsl", [128, NWS, 8, 512], BF16)
        self.wsl_buf = [Buf(f"wsl{i}") for i in range(NWS)]
        self.wsl_lane = [self.newlane() for _ in range(NWS)]
        self.KT = sb("KT", [128, 8, 16 * 128], BF16)
        self.Vr = sb("Vr", [128, 16, D], BF16)
        self.KT_buf = [Buf(f"KT{i}") for i in range(16)]
        self.V_buf = [Buf(f"V{i}") for i in range(16)]
        self.gsl = sb("gsl", [128, NGS, 2 * D], BF16)
        self.gsl_buf = [Buf(f"gsl{i}") for i in range(NGS)]
        self.gsl_lane = [self.newlane() for _ in range(NGS)]
        self.hb = [sb(f"hb{i}", [128, D], F32) for i in range(2)]
        self.hb_buf = [Buf(f"hb{i}") for i in range(2)]
        self.hb_lane_in = [self.newlane() for _ in range(2)]
        self.hb_lane_out = [self.newlane() for _ in range(2)]
        self.pre = sb("pre", [128, D], F32)
        self.pre_buf = Buf("pre")
        self.pre_lane = self.newlane()
        self.h16 = sb("h16", [128, D], BF16)
        self.h16_buf = Buf("h16")
        self.hT = sb("hT", [128, 8, 128], BF16)
        self.hT_buf = Buf("hT")
        self.big1 = sb("big1", [128, 2048], F32)
        self.big1_buf = Buf("big1")
        self.big2 = sb("big2", [128, 2048], F32)
        self.big2_buf = Buf("big2")
        self.big2_lane = self.newlane()
        self.scr = sb("scr", [128, 8192], BF16)
        self.scr_buf = Buf("scr")
        self.ub = [sb(f"ub{l}", [128, 8, 130], F32) for l in range(2)]
        self.ub_buf = [Buf(f"ub{l}") for l in range(2)]
        self.bacc = sb("bacc", [128, 8, 128], BF16)
        self.bacc_buf = Buf("bacc")
        self.lnp = sb("lnp", [128, 2, D], F32)
        self.lnp_buf = Buf("lnp")
        self.lnp_lane = self.newlane()
        self.qT = sb("qT", [128, 16, 128], BF16)
        self.qT_buf = Buf("qT")
        self.QT = self.qT[:, 0:8, :]
        self.QT_buf = self.qT_buf
        self.oTb = self.bacc
        self.oTb_buf = self.bacc_buf
        self.Cs = sb("Cs", [16, 512], BF16)
        self.Cs_buf = Buf("Cs")
        self.dg = [sb(f"dg{i}", [128, GRP, 128], BF16) for i in range(2)]
        self.dg_buf = [Buf(f"dg{i}") for i in range(2)]
        self.ab = [self.dg[i][:].rearrange("p a b -> p (a b)") for i in range(2)]
        self.ab_buf = self.dg_buf
        self.v16 = sb("v16", [128, 16, 16], F32)
        self.i16 = sb("i16", [128, 16, 16], U32)
        self.i16f = sb("i16f", [128, 16, 16], F32)
        self.tmp1 = self.big2[:, 0:128]
        self.tmp2 = self.big1[:, 0:256]
        self.best = sb("best", [128, 8, 16], F32)
        self.pos = sb("pos", [128, 8, 16], U32)
        self.posa = self.big2[:, 0:128].bitcast(I32).rearrange("p (a b) -> p a b", a=8)
        self.posb = self.big2[:, 128:256].bitcast(I32).rearrange("p (a b) -> p a b", a=8)
        self.af = self.big2[:, 256:384].rearrange("p (a b) -> p a b", a=8)
        self.bf = self.big2[:, 384:512].rearrange("p (a b) -> p a b", a=8)
        self.e0 = self.big2[:, 512:640].rearrange("p (a b) -> p a b", a=8)
        self.e1 = self.big2[:, 640:768].rearrange("p (a b) -> p a b", a=8)
        self.ef = sb("ef", [128, 128], F32)
        self.eidx = sb("eidx", [128, 128], I32)
        self.gw = sb("gw", [128, 8, 16], F32)
        self.gex = sb("gex", [128, 8, 16], F32)
        self.gsum = sb("gsum", [128, 8], F32)
        self.actt = self.big1[:, 0:128]
        self.gel = self.big1[:, 128:256]
        self.w4 = self.big1[:, 256:384]
        self.small_buf = {n: Buf(n) for n in ["v16", "i16", "i16f", "tmp1", "tmp2", "best", "pos", "posa", "posb",
                                                "af", "bf", "e0", "e1", "ef", "eidx", "gw", "gex", "gsum",
                                                "actt", "gel", "w4", "lnst", "cst", "cin"]}
        self.small_buf["tmp1"] = self.big2_buf
        self.small_buf["tmp2"] = self.big1_buf
        self.small_buf["cst"] = self.big2_buf
        self.small_buf["cin"] = self.big2_buf
        self.lnst = sb("lnst", [128, 16], F32)
        self.lnmv = sb("lnmv", [128, 4], F32)
        self.cst = self.big2[0:2, 0:D]
        self.cst_lane = self.newlane()
        self.cin = self.big2[0:2, D:2 * D]
        self.cin_lane = self.newlane()
        self.wdw = sb("wdwS", [128, 48], F32)
        self.wdw_buf = Buf("wdw")
        self.wdw_lane = self.newlane()
        self.iot = sb("iot", [128, 128], I32)
        self.identf = sb("identf", [128, 128], F32)
        self.identb = sb("identb", [128, 128], BF16)
        self.trineg = sb("trineg", [128, 128], BF16)
        self.mask1 = sb("mask1", [128, 128], BF16)
        self.indall = sb("indall", [128, 32], BF16)
        self.seli = sb("seli", [16, 128], I32)
        self.selneg = sb("selneg", [16, 16, 128], BF16)
        self.iot16i = sb("iot16i", [128, 16], I32)
        self.iot16 = sb("iot16", [128, 16], F32)
        self.const_buf = Buf("const")
        self.P = [self.es.enter_context(nc.psum_tensor(f"ps{i}", [128, 1024], F32)) for i in range(4)]
        self.pbuf = [Buf(f"bank{i}") for i in range(8)]

        self.emit_consts()
        self.emit_phase0()
        self.emit_tiles()
        self.emit_final()
        self.replay()

    def bank(self, i):
        return self.P[i // 2][:, (i % 2) * 512:(i % 2 + 1) * 512]

    def emit_consts(self):
        cb = [self.const_buf]
        self.G("iota", [], cb, out=self.iot[:], pattern=[[1, 128]], base=0, channel_multiplier=-1)
        self.G("iota", [], cb, out=self.iot16i[:], pattern=[[1, 16]], base=0, channel_multiplier=0)
        V = self.V
        V("tensor_scalar", cb, cb, out=self.identf[:], in0=self.iot[:], scalar1=0.0, scalar2=None, op0=ALU.is_equal)
        V("tensor_scalar", cb, cb, out=self.identb[:], in0=self.iot[:], scalar1=0.0, scalar2=None, op0=ALU.is_equal)
        V("tensor_scalar", cb, cb, out=self.trineg[:], in0=self.iot[:], scalar1=0.0, scalar2=-1.0,
          op0=ALU.is_le, op1=ALU.mult)
        V("tensor_scalar", cb, cb, out=self.mask1[:], in0=self.iot[:], scalar1=0.0, scalar2=None, op0=ALU.is_gt)
        V("memset", [], cb, ap=self.indall[:], constant=0.0)
        V("memset", cb, cb, ap=self.indall[:, 15:16], constant=1.0)
        for kbi in range(16):
            self.G("iota", cb, cb, out=self.seli[:], pattern=[[0, 128]], base=-kbi, channel_multiplier=1)
            V("tensor_scalar", cb, cb, out=self.selneg[:, kbi, :], in0=self.seli[:], scalar1=0.0, scalar2=-1.0,
              op0=ALU.is_gt, op1=ALU.mult)
        V("tensor_copy", cb, cb, out=self.iot16[:], in_=self.iot16i[:])
        self.dma("sync", self.wdw_lane, [], [self.wdw_buf], out=self.wdw[:], in_=self.wdw_d[:, :])

    def emit_phase0(self):
        for ci, (kind, src) in enumerate(self.chunks):
            s = ci % NWS
            if kind == "w":
                srcap = src.rearrange("(dc p) f -> p dc f", p=128)
                dst = self.wsl[:, s, :, :]
                dram = self.wbf[ci].rearrange("p (dc f) -> p dc f", dc=8)
            else:
                srcap = src
                dst = self.wsl[:, s, 0:4, :].rearrange("p a b -> p (a b)")
                dram = self.wbf[ci][:, 0:2048]
            self.dma("gpsimd", self.wsl_lane[s], [], [self.wsl_buf[s]], out=dst, in_=srcap)
            self.dma("sync", self.wsl_lane[s], [self.wsl_buf[s]], [self.wbf_buf[ci]], out=dram, in_=dst)

    def tile_chunk_order(self):
        o = []
        for l in range(2):
            if l >= self.n_layers:
                break
            o += [self.cid[(("w_in", l), j)] for j in range(6)]
            o += [self.cid[(("w_out", l), j)] for j in range(2)]
            o += [self.cid[(("pwq", l), j)] for j in range(4)]
            o += [self.cid[(("sk", l), 0)]]
        if self.n_layers > 2:
            o += [self.cid[(("wk", 0), j)] for j in range(2)]
            o += [self.cid[(("wv", 0), j)] for j in range(2)]
        for l in range(2, 4):
            if l >= self.n_layers:
                break
            o += [self.cid[(("sbq", l - 2), j)] for j in range(2)]
            o += [self.cid[(("sbo", l - 2), j)] for j in range(2)]
            o += [self.cid[(("pwq", l), j)] for j in range(4)]
            o += [self.cid[(("sk", l), 0)]]
        return o

    def wnext(self, expect):
        while self.w_issued < min(len(self.wlist), self.w_i + NWS):
            ci = self.wlist[self.w_issued]
            s = self.w_issued % NWS
            kind = self.chunks[ci][0]
            if kind == "w":
                dst = self.wsl[:, s, :, :]
                dram = self.wbf[ci].rearrange("p (dc f) -> p dc f", dc=8)
            else:
                dst = self.wsl[:, s, 0:4, :].rearrange("p a b -> p (a b)")
                dram = self.wbf[ci][:, 0:2048]
            self.dma("sync", self.wsl_lane[s], [self.wbf_buf[ci]], [self.wsl_buf[s]], out=dst, in_=dram)
            self.w_issued += 1
        ci = self.wlist[self.w_i]
        assert ci == self.cid[expect], (ci, expect)
        s = self.w_i % NWS
        self.w_i += 1
        return s

    def emit_tiles(self):
        seqs = []
        for b in range(self.n_pseq):
            seqs.append(dict(kind="p", b=b, nt=self.n_ptiles, nvalid=128, kb0=0))
        for b in range(self.n_sseq):
            seqs.append(dict(kind="s", b=b, nt=1, nvalid=16, kb0=self.n_past))
        ntiles = sum(s["nt"] for s in seqs)
        self.wlist = self.tile_chunk_order() * ntiles
        self.w_i = 0
        self.w_issued = 0
        self.tcount = 0
        self.stores = []
        for seq in seqs:
            if seq["kind"] == "s" and self.n_layers > 2:
                self.load_cache(seq)
            for ti in range(seq["nt"]):
                self.tile(seq, ti)
                self.tcount += 1

    def load_cache(self, seq):
        b = seq["b"]
        for kb in range(self.n_past):
            self.dma("sync", self.big2_lane, [], [self.big2_buf], out=self.big2[:, 0:D],
                     in_=self.ck[b, kb * 128:(kb + 1) * 128, :])
            self.act([self.big2_buf], [self.h16_buf], out=self.h16[:], in_=self.big2[:, 0:D], func=AF.Copy)
            self.k_transposes(kb)
            self.dma("sync", self.big2_lane, [], [self.big2_buf], out=self.big2[:, 0:D],
                     in_=self.cv[b, kb * 128:(kb + 1) * 128, :])
            self.act([self.big2_buf], [self.V_buf[kb]], out=self.Vr[:, kb, :], in_=self.big2[:, 0:D], func=AF.Copy)

    def k_transposes(self, kb):
        bi = 0
        pb = self.bank(bi).bitcast(BF16)
        for pair in range(8):
            self.T("transpose", [self.h16_buf, self.const_buf], [self.pbuf[bi]],
                   out=pb[:, pair * 128:(pair + 1) * 128], in_=self.h16[:, pair * 128:(pair + 1) * 128],
                   identity=self.identb[:])
        self.act([self.pbuf[bi]], [self.KT_buf[kb]], out=self.KT[:, :, kb * 128:(kb + 1) * 128],
                 in_=pb.rearrange("p (a b) -> p a b", a=8), func=AF.Copy)

    def make_hT(self, hi):
        self.act([self.hb_buf[hi]], [self.h16_buf], out=self.h16[:], in_=self.hb[hi][:], func=AF.Copy)
        bi = 1
        pb = self.bank(bi).bitcast(BF16)
        for c in range(8):
            self.T("transpose", [self.h16_buf, self.const_buf], [self.pbuf[bi]],
                   out=pb[:, c * 128:(c + 1) * 128], in_=self.h16[:, c * 128:(c + 1) * 128],
                   identity=self.identb[:])
        self.V("tensor_copy", [self.pbuf[bi]], [self.hT_buf], out=self.hT[:].rearrange("p a b -> p (a b)"),
               in_=pb)

    def layernorm(self, idx, hi):
        self.dma("sync", self.lnp_lane, [], [self.lnp_buf], out=self.lnp[:, 0:1, :],
                 in_=self.lng[idx:idx + 1, :].partition_broadcast(128))
        self.dma("sync", self.lnp_lane, [], [self.lnp_buf], out=self.lnp[:, 1:2, :],
                 in_=self.lnb[idx:idx + 1, :].partition_broadcast(128))
        sbuf = self.small_buf["lnst"]
        V = self.V
        V("bn_stats", [self.pre_buf], [sbuf], out=self.lnst[:, 0:6], in_=self.pre[:, 0:512])
        V("bn_stats", [self.pre_buf], [sbuf], out=self.lnst[:, 6:12], in_=self.pre[:, 512:1024])
        V("bn_aggr", [sbuf], [sbuf], out=self.lnmv[:, 0:2], in_=self.lnst[:, 0:12])
        V("tensor_scalar", [sbuf], [sbuf], out=self.lnmv[:, 2:3], in0=self.lnmv[:, 1:2], scalar1=EPS, scalar2=None,
          op0=ALU.add)
        self.act([sbuf], [sbuf], out=self.lnmv[:, 2:3], in_=self.lnmv[:, 2:3], func=AF.Ln)
        self.act([sbuf], [sbuf], out=self.lnmv[:, 3:4], in_=self.lnmv[:, 2:3], func=AF.Exp, scale=-0.5)
        V("tensor_scalar", [self.pre_buf, sbuf], [self.pre_buf], out=self.pre[:], in0=self.pre[:],
          scalar1=self.lnmv[:, 0:1], scalar2=self.lnmv[:, 3:4], op0=ALU.subtract, op1=ALU.mult)
        V("tensor_tensor", [self.pre_buf, self.lnp_buf], [self.pre_buf], out=self.pre[:], in0=self.pre[:],
          in1=self.lnp[:, 0, :], op=ALU.mult)
        V("tensor_tensor", [self.pre_buf, self.lnp_buf], [self.hb_buf[hi]], out=self.hb[hi][:], in0=self.pre[:],
          in1=self.lnp[:, 1, :], op=ALU.add)

    def residual(self, hi, pbanks):
        pidx = pbanks
        self.V("scalar_tensor_tensor", [self.hb_buf[hi], self.pbuf[2 * pidx], self.pbuf[2 * pidx + 1]],
               [self.pre_buf], out=self.pre[:], in0=self.hb[hi][:], scalar=ALPHA, in1=self.P[pidx][:, :],
               op0=ALU.mult, op1=ALU.add)

    def tok_major_mm(self, key, lhs, lhs_buf, pidx):
        for half in range(2):
            s = self.wnext((key, half))
            bi = 2 * pidx + half
            for dc in range(8):
                self.mm([lhs_buf, self.wsl_buf[s]], [self.pbuf[bi]], out=self.bank(bi), lhsT=lhs[:, dc, :],
                        rhs=self.wsl[:, s, dc, :], start=(dc == 0), stop=(dc == 7))

    def feat_major_chunk(self, key, j, bi):
        s = self.wnext((key, j))
        for fc in range(4):
            for dc in range(8):
                self.mm([self.hT_buf, self.wsl_buf[s]], [self.pbuf[bi]],
                        out=self.bank(bi)[:, fc * 128:(fc + 1) * 128], lhsT=self.wsl[:, s, dc, fc * 128:(fc + 1) * 128],
                        rhs=self.hT[:, dc, :], start=(dc == 0), stop=(dc == 7))

    def tile(self, seq, ti):
        hi = self.tcount % 2
        b, nv = seq["b"], seq["nvalid"]
        h = self.hb[hi]
        hbuf = self.hb_buf[hi]
        if seq["kind"] == "p":
            self.dma("sync", self.hb_lane_in[hi], [], [hbuf], out=h[:], in_=self.xp[b, ti * 128:(ti + 1) * 128, :])
        else:
            self.V("memset", [], [hbuf], ap=h[:], constant=0.0)
            self.dma("sync", self.hb_lane_in[hi], [], [hbuf], out=h[0:16, :], in_=self.xs[b, :, :])
        self.make_hT(hi)
        kb = seq["kb0"] + ti
        for l in range(self.n_layers):
            if l < 2:
                self.conv_layer(l, seq, ti, hi)
            else:
                if l == 2:
                    self.kv(seq, ti, hi, kb)
                self.attn_layer(l - 2, seq, ti, hi, kb)
            self.layernorm(2 * l, hi)
            self.make_hT(hi)
            self.peer(l, hi)
            self.layernorm(2 * l + 1, hi)
            if l < self.n_layers - 1:
                self.make_hT(hi)
        if seq["kind"] == "p":
            ev = self.dma("sync", self.hb_lane_out[hi], [hbuf], [], out=self.yp[b, ti * 128:(ti + 1) * 128, :], in_=h[:])
        else:
            ev = self.dma("sync", self.hb_lane_out[hi], [hbuf], [], out=self.ys[b, :, :], in_=h[0:16, :])
        self.stores.append(ev)

    def conv_layer(self, l, seq, ti, hi):
        V = self.V
        ub, ubb = self.ub[l], self.ub_buf[l]
        gates = self.scr[:].bitcast(F32)[:, 0:3072].rearrange("p (a b) -> p a b", a=24)
        if ti == 0:
            if seq["kind"] == "p":
                V("memset", [], [ubb], ap=ub[:, :, 0:2], constant=0.0)
            else:
                cb = self.small_buf["cin"]
                self.dma("sync", self.cin_lane, [], [cb], out=self.cin, in_=self.stc[l, seq["b"], :, :])
                bi = 0
                for c in range(8):
                    self.T("transpose", [cb, self.const_buf], [self.pbuf[bi]], out=self.bank(bi)[:, 2 * c:2 * c + 2],
                           in_=self.big2[0:2, D + c * 128:D + (c + 1) * 128], identity=self.identf[0:2, 0:2])
                V("tensor_copy", [self.pbuf[bi]], [ubb], out=ub[:, :, 0:2],
                  in_=self.bank(bi)[:, 0:16].rearrange("p (a b) -> p a b", a=8))
        for j in range(6):
            bi = j % 2
            self.feat_major_chunk(("w_in", l), j, bi)
            self.act([self.pbuf[bi]], [self.scr_buf], out=gates[:, 4 * j:4 * j + 4, :],
                     in_=self.bank(bi).rearrange("p (a b) -> p a b", a=4), func=AF.Copy)
        V("tensor_tensor", [self.scr_buf], [ubb], out=ub[:, :, 2:130], in0=gates[:, 8:16, :], in1=gates[:, 16:24, :],
          op=ALU.mult)
        acc = self.big1[:, 0:1024].rearrange("p (a b) -> p a b", a=8)
        ab = self.big1_buf
        for c in range(8):
            col = lambda w: self.wdw[:, (l * 3 + w) * 8 + c:(l * 3 + w) * 8 + c + 1]
            V("tensor_scalar", [ubb, self.wdw_buf], [ab], out=acc[:, c, :], in0=ub[:, c, 0:128], scalar1=col(0),
              scalar2=None, op0=ALU.mult)
            V("scalar_tensor_tensor", [ubb, ab, self.wdw_buf], [ab], out=acc[:, c, :], in0=ub[:, c, 1:129],
              scalar=col(1), in1=acc[:, c, :], op0=ALU.mult, op1=ALU.add)
            V("scalar_tensor_tensor", [ubb, ab, self.wdw_buf], [ab], out=acc[:, c, :], in0=ub[:, c, 2:130],
              scalar=col(2), in1=acc[:, c, :], op0=ALU.mult, op1=ALU.add)
        V("tensor_tensor", [self.scr_buf, ab], [self.bacc_buf], out=self.bacc[:], in0=gates[:, 0:8, :], in1=acc,
          op=ALU.mult)
        nv = seq["nvalid"]
        if ti == seq["nt"] - 1:
            cb = self.small_buf["cst"]
            for c in range(8):
                bi = c // 4
                self.T("transpose", [ubb, self.const_buf], [self.pbuf[bi]],
                       out=self.P[0][0:2, c * 128:(c + 1) * 128], in_=ub[:, c, nv:nv + 2], identity=self.identf[:])
            self.act([self.pbuf[0], self.pbuf[1]], [cb], out=self.cst, in_=self.P[0][0:2, :], func=AF.Copy)
            dst = (self.ncp if seq["kind"] == "p" else self.ncs)[l, seq["b"], :, :]
            ev = self.dma("sync", self.cst_lane, [cb], [], out=dst, in_=self.cst)
            self.stores.append(ev)
        else:
            V("tensor_copy", [ubb], [ubb], out=ub[:, :, 0:2], in_=ub[:, :, 128:130])
        self.tok_major_mm(("w_out", l), self.bacc, self.bacc_buf, 0)
        self.residual(hi, 0)

    def peer(self, l, hi):
        V, A = self.V, self.A
        sm = self.small_buf
        h, hbuf = self.hb[hi], self.hb_buf[hi]
        for j in range(4):
            bi = j % 2
            self.feat_major_chunk(("pwq", l), j, bi)
            self.act([self.pbuf[bi]], [self.qT_buf], out=self.qT[:, 4 * j:4 * j + 4, :],
                     in_=self.bank(bi).rearrange("p (a b) -> p a b", a=4), func=AF.Copy)
        s = self.wnext((("sk", l), 0))
        skT = self.wsl[:, s, 0:4, :].rearrange("p a b -> p (a b)")
        scs = self.big1[:].rearrange("p (a b) -> p a b", a=16)
        for g in range(16):
            bi = 4 + g // 4
            self.mm([self.qT_buf, self.wsl_buf[s]], [self.pbuf[bi]], out=self.bank(bi)[:, (g % 4) * 128:(g % 4 + 1) * 128],
                    lhsT=self.qT[:, g, :], rhs=skT[:, g * 128:(g + 1) * 128], start=True, stop=True)
        for q in range(4):
            self.act([self.pbuf[4 + q]], [self.big1_buf], out=self.big1[:, q * 512:(q + 1) * 512], in_=self.bank(4 + q),
                     func=AF.Copy)
        b1 = self.big1_buf
        for g in range(16):
            V("max", [b1], [sm["v16"]], out=self.v16[:, g, 0:8], in_=scs[:, g, :])
            V("max_index", [b1, sm["v16"]], [sm["i16"]], out=self.i16[:, g, 0:8], in_max=self.v16[:, g, 0:8],
              in_values=scs[:, g, :])
            V("match_replace", [b1, sm["v16"]], [sm["tmp1"]], out=self.tmp1, in_to_replace=self.v16[:, g, 0:8],
              in_values=scs[:, g, :], imm_value=NEG)
            V("max", [sm["tmp1"]], [sm["v16"]], out=self.v16[:, g, 8:16], in_=self.tmp1)
            V("max_index", [sm["tmp1"], sm["v16"]], [sm["i16"]], out=self.i16[:, g, 8:16], in_max=self.v16[:, g, 8:16],
              in_values=self.tmp1)
        V("tensor_copy", [sm["i16"]], [sm["i16f"]], out=self.i16f[:], in_=self.i16[:])
        cand = self.big2[:].rearrange("p (h a b) -> p h a b", h=8, a=16)
        v16h = self.v16[:].rearrange("p (h t) k -> p h t k", t=2)
        i16h = self.i16f[:].rearrange("p (h t) k -> p h t k", t=2)
        V("tensor_tensor", [sm["v16"]], [self.big2_buf], out=cand,
          in0=v16h[:, :, 0, :].unsqueeze(3).to_broadcast([128, 8, 16, 16]),
          in1=v16h[:, :, 1, :].unsqueeze(2).to_broadcast([128, 8, 16, 16]), op=ALU.add)
        b2 = self.big2_buf
        for hh in range(8):
            cf = self.big2[:, hh * 256:(hh + 1) * 256]
            V("max", [b2], [sm["best"]], out=self.best[:, hh, 0:8], in_=cf)
            V("max_index", [b2, sm["best"]], [sm["pos"]], out=self.pos[:, hh, 0:8], in_max=self.best[:, hh, 0:8],
              in_values=cf)
            V("match_replace", [b2, sm["best"]], [sm["tmp2"]], out=self.tmp2, in_to_replace=self.best[:, hh, 0:8],
              in_values=cf, imm_value=NEG)
            V("max", [sm["tmp2"]], [sm["best"]], out=self.best[:, hh, 8:16], in_=self.tmp2)
            V("max_index", [sm["tmp2"], sm["best"]], [sm["pos"]], out=self.pos[:, hh, 8:16],
              in_max=self.best[:, hh, 8:16], in_values=self.tmp2)
        posi = self.pos[:].bitcast(I32)
        V("tensor_single_scalar", [sm["pos"]], [sm["posa"]], out=self.posa[:], in_=posi, scalar=4,
          op=ALU.arith_shift_right)
        V("tensor_single_scalar", [sm["pos"]], [sm["posb"]], out=self.posb[:], in_=posi, scalar=15,
          op=ALU.bitwise_and)
        V("tensor_copy", [sm["posa"]], [sm["af"]], out=self.af[:], in_=self.posa[:])
        V("tensor_copy", [sm["posb"]], [sm["bf"]], out=self.bf[:], in_=self.posb[:])
        eq = self.big1[:].rearrange("p (h a b) -> p h a b", h=8, a=16)
        io = self.iot16[:].unsqueeze(1).unsqueeze(1).to_broadcast([128, 8, 16, 16])
        for (src, srcb, t, dst, dstb) in ((self.af, sm["af"], 0, self.e0, sm["e0"]), (self.bf, sm["bf"], 1, self.e1, sm["e1"])):
            V("tensor_tensor", [srcb, self.const_buf], [b1], out=eq,
              in0=src[:].unsqueeze(3).to_broadcast([128, 8, 16, 16]), in1=io, op=ALU.is_equal)
            V("tensor_tensor", [b1, sm["i16f"]], [b1], out=eq, in0=eq,
              in1=i16h[:, :, t, :].unsqueeze(2).to_broadcast([128, 8, 16, 16]), op=ALU.mult)
            V("tensor_reduce", [b1], [dstb], out=dst[:], in_=eq, axis=AX.X, op=ALU.add)
        V("scalar_tensor_tensor", [sm["e0"], sm["e1"]], [sm["ef"]], out=self.ef[:],
          in0=self.e0[:].rearrange("p a b -> p (a b)"), scalar=128.0, in1=self.e1[:].rearrange("p a b -> p (a b)"),
          op0=ALU.mult, op1=ALU.add)
        if l > 0:
            V("tensor_scalar", [sm["ef"]], [sm["ef"]], out=self.ef[:], in0=self.ef[:], scalar1=float(l * 16384),
              scalar2=None, op0=ALU.add)
        V("tensor_copy", [sm["ef"]], [sm["eidx"]], out=self.eidx[:], in_=self.ef[:])
        V("tensor_tensor", [sm["best"]], [sm["gex"]], out=self.gex[:], in0=self.best[:],
          in1=self.best[:, :, 0:1].to_broadcast([128, 8, 16]), op=ALU.subtract)
        self.act([sm["gex"]], [sm["gex"]], out=self.gex[:], in_=self.gex[:], func=AF.Exp)
        V("tensor_reduce", [sm["gex"]], [sm["gsum"]], out=self.gsum[:], in_=self.gex[:], axis=AX.X, op=ALU.add)
        V("reciprocal", [sm["gsum"]], [sm["gsum"]], out=self.gsum[:], in_=self.gsum[:])
        V("tensor_tensor", [sm["gex"], sm["gsum"]], [sm["gw"]], out=self.gw[:], in0=self.gex[:],
          in1=self.gsum[:].unsqueeze(2).to_broadcast([128, 8, 16]), op=ALU.mult)
        gwf = self.gw[:].rearrange("p a b -> p (a b)")
        uvt = self.uv.rearrange("l e d -> (l e) d")
        for j0 in range(0, 128, GRP):
            gi = (j0 // GRP) % 2
            for jj in range(GRP):
                j = j0 + jj
                sl = j % NGS
                self.dma("gpsimd", self.gsl_lane[sl], [sm["eidx"]], [self.gsl_buf[sl]], out=self.gsl[:, sl, :],
                         in_=uvt[:, :], method="indirect_dma_start", out_offset=None,
                         in_offset=bass.IndirectOffsetOnAxis(ap=self.eidx[:, j:j + 1], axis=0))
                V("scalar_tensor_tensor", [self.gsl_buf[sl], hbuf], [self.h16_buf, sm["actt"]], out=self.h16[:],
                  in0=self.gsl[:, sl, 0:D], scalar=1.0, in1=h[:], op0=ALU.mult, op1=ALU.mult,
                  accum_out=self.actt[:, j:j + 1])
            self.act([sm["actt"]], [sm["gel"]], out=self.gel[:, j0:j0 + GRP], in_=self.actt[:, j0:j0 + GRP], func=AF.Gelu)
            V("tensor_tensor", [sm["gel"], sm["gw"]], [sm["w4"]], out=self.w4[:, j0:j0 + GRP], in0=self.gel[:, j0:j0 + GRP],
              in1=gwf[:, j0:j0 + GRP], op=ALU.mult)
            V("tensor_tensor", [sm["w4"], self.const_buf], [self.dg_buf[gi]], out=self.dg[gi][:],
              in0=self.identb[:].unsqueeze(1).to_broadcast([128, GRP, 128]),
              in1=self.w4[:, j0:j0 + GRP].unsqueeze(2).to_broadcast([128, GRP, 128]), op=ALU.mult)
            for jj in range(GRP):
                j = j0 + jj
                sl = j % NGS
                for half in range(2):
                    bi = 2 + half
                    self.mm([self.dg_buf[gi], self.gsl_buf[sl]], [self.pbuf[bi]], out=self.bank(bi),
                            lhsT=self.dg[gi][:, jj, :], rhs=self.gsl[:, sl, D + half * 512:D + (half + 1) * 512],
                            start=(j == 0), stop=(j == 127))
        self.residual(hi, 1)

    def kv(self, seq, ti, hi, kb):
        nv, b = seq["nvalid"], seq["b"]
        r0 = ti * 128
        self.tok_major_mm(("wk", 0), self.hT, self.hT_buf, 0)
        self.act([self.pbuf[0], self.pbuf[1]], [self.pre_buf], out=self.pre[:], in_=self.P[0][:, :], func=AF.Copy)
        self.act([self.pbuf[0], self.pbuf[1]], [self.h16_buf], out=self.h16[:], in_=self.P[0][:, :], func=AF.Copy)
        if seq["kind"] == "p":
            ev = self.dma("sync", self.pre_lane, [self.pre_buf], [], out=self.nkp[b, r0:r0 + 128, :], in_=self.pre[:])
        else:
            ev = self.dma("sync", self.pre_lane, [self.pre_buf], [], out=self.nks[b, :, :], in_=self.pre[0:16, :])
        self.stores.append(ev)
        self.k_transposes(kb)
        self.tok_major_mm(("wv", 0), self.hT, self.hT_buf, 0)
        self.act([self.pbuf[0], self.pbuf[1]], [self.big2_buf], out=self.big2[:, 0:D], in_=self.P[0][:, :], func=AF.Copy)
        self.act([self.pbuf[0], self.pbuf[1]], [self.V_buf[kb]], out=self.Vr[:, kb, :], in_=self.P[0][:, :], func=AF.Copy)
        if seq["kind"] == "p":
            ev = self.dma("sync", self.big2_lane, [self.big2_buf], [], out=self.nvp[b, r0:r0 + 128, :], in_=self.big2[:, 0:D])
        else:
            ev = self.dma("sync", self.big2_lane, [self.big2_buf], [], out=self.nvs[b, :, :], in_=self.big2[0:16, 0:D])
        self.stores.append(ev)

    def attn_layer(self, j, seq, ti, hi, kb):
        V = self.V
        nkb = kb + 1
        V("memset", [], [self.qT_buf], ap=self.qT[:], constant=0.0)
        for jj in range(2):
            bi = jj % 2
            self.feat_major_chunk(("sbq", j), jj, bi)
            for r in range(2):
                self.act([self.pbuf[bi]], [self.qT_buf], out=self.qT[r * 64:(r + 1) * 64, 8 * jj + r:8 * jj + 8:2, :],
                         in_=self.bank(bi)[r * 64:(r + 1) * 64, :].rearrange("p (a b) -> p a b", a=4), func=AF.Copy,
                         scale=0.125)
        if DBG == 1:
            self.V("tensor_scalar", [self.hb_buf[hi]], [self.pre_buf], out=self.pre[:], in0=self.hb[hi][:], scalar1=ALPHA,
                   scalar2=None, op0=ALU.mult)
            for half in range(2):
                self.wnext((("sbo", j), half))
            return
        Pb = self.scr[:].rearrange("p (a b) -> p a b", a=16)
        Eb = self.big2[:, 0:512]
        oT = self.P[1]
        first_done = {}
        for g in range(4):
            heads = [4 * g + hh for hh in range(4)]

            def zmm(bi, start_first, last_stop):
                for hh, hd in enumerate(heads):
                    pair, r = hd // 2, hd % 2
                    self.mm([self.KT_buf[kbi_], self.QT_buf], [self.pbuf[bi]],
                            out=self.bank(bi)[:, hh * 128:(hh + 1) * 128],
                            lhsT=self.KT[:, pair, kbi_ * 128:(kbi_ + 1) * 128],
                            rhs=self.qT[:, hd, :],
                            start=(start_first and hh == 0), stop=(last_stop and hh == 3), skip_group_check=True)
            for kbi_ in range(nkb):
                zb = 4 + kbi_ % 2
                zmm(zb, True, True)
                self.act([self.pbuf[zb]], [self.big2_buf], out=Eb, in_=self.bank(zb), func=AF.Exp)
                self.act([self.big2_buf], [self.scr_buf], out=Pb[:, kbi_, :], in_=Eb, func=AF.Ln, bias=1.0)
                if kbi_ == kb:
                    V("tensor_tensor", [self.scr_buf, self.const_buf], [self.scr_buf], out=Pb[:, kbi_, :].rearrange("p (a b) -> p a b", a=4),
                      in0=Pb[:, kbi_, :].rearrange("p (a b) -> p a b", a=4),
                      in1=self.mask1[:].unsqueeze(1).to_broadcast([128, 4, 128]), op=ALU.mult)
                self.mm([self.scr_buf, self.const_buf], [self.pbuf[6]], out=self.bank(6)[0:16, :],
                        lhsT=self.indall[:, 15 - kbi_:31 - kbi_], rhs=Pb[:, kbi_, :], start=(kbi_ == 0),
                        stop=(kbi_ == nkb - 1))
            if DBG == 2:
                continue
            self.act([self.pbuf[6]], [self.Cs_buf], out=self.Cs[:], in_=self.bank(6)[0:16, :], func=AF.Copy)
            for kbi_ in range(nkb):
                zs = kbi_ % 2
                ai = kbi_ % 2
                self.mm([self.scr_buf, self.const_buf], [self.pbuf[zs]], out=self.bank(zs), lhsT=self.trineg[:],
                        rhs=Pb[:, kbi_, :], start=True, stop=False, skip_group_check=True)
                self.mm([self.Cs_buf, self.const_buf], [self.pbuf[zs]], out=self.bank(zs), lhsT=self.selneg[0:16, kbi_, :],
                        rhs=self.Cs[:], start=False, stop=False, skip_group_check=True)
                zmm(zs, False, True)
                self.act([self.pbuf[zs]], [self.ab_buf[ai]], out=self.ab[ai], in_=self.bank(zs), func=AF.Exp)
                if DBG == 3:
                    continue
                if kbi_ == kb:
                    V("tensor_tensor", [self.ab_buf[ai], self.const_buf], [self.ab_buf[ai]], out=self.dg[ai][:],
                      in0=self.dg[ai][:], in1=self.mask1[:].unsqueeze(1).to_broadcast([128, 4, 128]), op=ALU.mult)
                for hh, hd in enumerate(heads):
                    pair, r = hd // 2, hd % 2
                    bk = pair // 4
                    key = (bk, r)
                    st = key not in first_done
                    first_done[key] = True
                    self.mm([self.ab_buf[ai], self.V_buf[kbi_]], [self.pbuf[2 + bk]],
                            out=oT[r * 64:(r + 1) * 64, pair * 128:(pair + 1) * 128],
                            lhsT=self.Vr[:, kbi_, hd * 64:(hd + 1) * 64], rhs=self.ab[ai][:, hh * 128:(hh + 1) * 128],
                            start=st, stop=(kbi_ == nkb - 1), skip_group_check=True)
        if DBG in (2, 3):
            self.V("tensor_scalar", [self.hb_buf[hi]], [self.pre_buf], out=self.pre[:], in0=self.hb[hi][:], scalar1=ALPHA,
                   scalar2=None, op0=ALU.mult)
            for half in range(2):
                self.wnext((("sbo", j), half))
            return
        self.act([self.pbuf[2], self.pbuf[3]], [self.oTb_buf], out=self.oTb[:].rearrange("p a b -> p (a b)"),
                 in_=oT[:, :], func=AF.Copy)
        self.tok_major_mm(("sbo", j), self.oTb, self.oTb_buf, 0)
        self.residual(hi, 0)

    def emit_final(self):
        eng = self.eng["sync"]
        for ev in self.stores:
            self._wait_for(eng, ev, "raw")

    def replay(self):
        nc = self.nc
        engs = self.eng
        with nc.Block() as block:
            def run(e, name):
                for it in engs[name].q:
                    if it[0] == "w":
                        e.wait_ge(it[1], it[2])
                    else:
                        _, method, kw, sem, inc = it
                        ins = getattr(e, method)(**kw)
                        ins.then_inc(sem, inc)

            @block.sync
            def _(e):
                run(e, "sync")

            @block.gpsimd
            def _(e):
                run(e, "gpsimd")

            @block.tensor
            def _(e):
                run(e, "tensor")

            @block.vector
            def _(e):
                run(e, "vector")

            @block.scalar
            def _(e):
                run(e, "scalar")
        self.es.close()


def make_in_maps(inputs, n_cores, n_pseq, n_sseq):
    f = lambda a: np.ascontiguousarray(np.asarray(a, dtype=np.float32))
    xp, xs = f(inputs["x_prompt"]), f(inputs["x_sample"])
    stc, ck, cv = f(inputs["state_conv"]), f(inputs["cache_k"]), f(inputs["cache_v"])
    ck = ck.reshape(ck.shape[0], ck.shape[1], -1)
    cv = cv.reshape(cv.shape[0], cv.shape[1], -1)
    wdw = f(inputs["conv_w_dw"])
    wdw_l = np.ascontiguousarray(wdw.reshape(2, 3, 8, 128).transpose(3, 0, 1, 2).reshape(128, 48))
    sk = f(inputs["peer_subkeys"])
    skT = np.ascontiguousarray(sk.transpose(0, 4, 1, 2, 3).reshape(4, 128, 2048))
    uv = np.concatenate([f(inputs["peer_u"]), f(inputs["peer_v"])], axis=-1)
    shared = dict(w_in=f(inputs["conv_w_in"]), wdw=wdw_l, w_out=f(inputs["conv_w_out"]), sbq=f(inputs["sb_w_q"]),
                  sbo=f(inputs["sb_w_o"]), wk=f(inputs["kv_w_k"]), wv=f(inputs["kv_w_v"]), pwq=f(inputs["peer_w_q"]),
                  skT=skT, uv=uv, lng=f(inputs["ln_g"]).reshape(8, D), lnb=f(inputs["ln_b"]).reshape(8, D))
    maps = []
    for c in range(n_cores):
        m = dict(shared)
        m["xp"] = np.ascontiguousarray(xp[c * n_pseq:(c + 1) * n_pseq])
        m["xs"] = np.ascontiguousarray(xs[c * n_sseq:(c + 1) * n_sseq])
        m["stc"] = np.ascontiguousarray(stc[:, c * n_sseq:(c + 1) * n_sseq])
        m["ck"] = np.ascontiguousarray(ck[c * n_sseq:(c + 1) * n_sseq])
        m["cv"] = np.ascontiguousarray(cv[c * n_sseq:(c + 1) * n_sseq])
        maps.append(m)
    return maps


def assemble(results, n_cores):
    cat = lambda k, ax: np.concatenate([np.asarray(r[k], dtype=np.float32) for r in results], axis=ax)
    yp, ys = cat("yp", 0), cat("ys", 0)
    ncp, ncs = cat("ncp", 1), cat("ncs", 1)
    nkp, nvp, nks, nvs = cat("nkp", 0), cat("nvp", 0), cat("nks", 0), cat("nvs", 0)
    r4 = lambda a: a.reshape(a.shape[0], a.shape[1], 16, 64)
    return (yp, ys, ncp, r4(nkp), r4(nvp), ncs, r4(nks), r4(nvs))


_PROG = {}


def kernel(**inputs):
    n_cores = 8
    if "full" not in _PROG:
        _PROG["full"] = Prog(4, 16, 4, 8)
    prog = _PROG["full"]
    maps = make_in_maps(inputs, n_cores, 4, 4)
    res = run_bass_kernel_spmd(prog.nc, maps, core_ids=list(range(n_cores)))
    return assemble(res.results, n_cores)
```

```python
import numpy as np
from contextlib import ExitStack
import concourse.bass as bass
import concourse.mybir as mybir
from concourse.bass_utils import run_bass_kernel_spmd

F32 = mybir.dt.float32
BF16 = mybir.dt.bfloat16
I32 = mybir.dt.int32
U32 = mybir.dt.uint32
AF = mybir.ActivationFunctionType
ALU = mybir.AluOpType
AX = mybir.AxisListType

D = 1024
ALPHA = (2.0 * 4) ** 0.25
EPS = 1e-5
NEG = -1.0e30
SEM_LIMIT = 32000
SAME_RAW = True
NGS = 8
NWS = 3
GRP = 4
DBG = 0


class Buf:
    __slots__ = ("name", "w", "r")

    def __init__(self, name):
        self.name = name
        self.w = None
        self.r = []


class Lane:
    def __init__(self, pool, inc):
        self.pool = pool
        self.inc = inc
        self.sem = pool.pop()
        self.count = 0

    def next(self):
        if self.count + self.inc > SEM_LIMIT:
            self.sem = self.pool.pop()
            self.count = 0
        self.count += self.inc
        return (self.sem, self.count)


class Eng:
    def __init__(self, name, pool):
        self.name = name
        self.lane = Lane(pool, 1)
        self.q = []
        self.waited = {}


class Prog:
    def __init__(self, n_pseq=4, n_ptiles=16, n_sseq=4, n_past=8, n_layers=4):
        self.n_pseq, self.n_ptiles, self.n_sseq, self.n_past = n_pseq, n_ptiles, n_sseq, n_past
        self.n_layers = n_layers
        self.nc = bass.Bass("TRN2", target_bir_lowering=False)
        self.es = ExitStack()
        self.build()

    def _need(self, eng, need, ev, kind):
        sem, val, src = ev
        if src == eng.name:
            if eng.name == "tensor" or kind != "raw" or not SAME_RAW:
                return
        key = id(sem)
        if eng.waited.get(key, 0) >= val:
            return
        if key not in need or need[key][1] < val:
            need[key] = (sem, val)

    def _flush(self, eng, need):
        for key, (sem, val) in need.items():
            eng.waited[key] = val
            eng.q.append(("w", sem, val))

    def _wait_for(self, eng, ev, kind):
        need = {}
        self._need(eng, need, ev, kind)
        self._flush(eng, need)

    def _deps(self, eng, reads, writes):
        need = {}
        for b in reads:
            if b.w is not None:
                self._need(eng, need, b.w, "raw")
        for b in writes:
            if b.w is not None:
                self._need(eng, need, b.w, "waw")
            for r in b.r:
                self._need(eng, need, r, "war")
        self._flush(eng, need)

    def _commit(self, ev, reads, writes):
        for b in reads:
            b.r.append(ev)
        for b in writes:
            b.w = ev
            b.r = []

    def op(self, engname, method, reads, writes, **kw):
        eng = self.eng[engname]
        self._deps(eng, reads, writes)
        sem, val = eng.lane.next()
        eng.q.append(("i", method, kw, sem, 1))
        self._commit((sem, val, engname), reads, writes)

    def dma(self, qname, lane, reads, writes, out, in_, method="dma_start", **kw):
        eng = self.eng[qname]
        self._deps(eng, reads, writes)
        sem, val = lane.next()
        kw = dict(kw)
        kw["out"] = out
        kw["in_"] = in_
        eng.q.append(("i", method, kw, sem, 16))
        ev = (sem, val, "dma")
        self._commit(ev, reads, writes)
        return ev

    def V(self, method, reads, writes, **kw):
        self.op("vector", method, reads, writes, **kw)

    def A(self, method, reads, writes, **kw):
        self.op("scalar", method, reads, writes, **kw)

    def T(self, method, reads, writes, **kw):
        self.op("tensor", method, reads, writes, **kw)

    def G(self, method, reads, writes, **kw):
        self.op("gpsimd", method, reads, writes, **kw)

    def act(self, reads, writes, out, in_, func, **kw):
        self.A("activation", reads, writes, out=out, in_=in_, func=func, **kw)

    def mm(self, reads, writes, out, lhsT, rhs, start, stop, **kw):
        self.T("matmul", reads, writes, out=out, lhsT=lhsT, rhs=rhs, start=start, stop=stop, **kw)

    def sb(self, name, shape, dt):
        return self.es.enter_context(self.nc.sbuf_tensor(name, shape, dt))

    def newlane(self):
        return Lane(self.sempool, 16)

    def build(self):
        nc = self.nc
        es = self.es
        NP, NT, NS, NPAST = self.n_pseq, self.n_ptiles, self.n_sseq, self.n_past
        SP = NT * 128
        SPAST = NPAST * 128

        def din(name, shape, dt=F32):
            return nc.dram_tensor(name, list(shape), dt, kind="ExternalInput").ap()

        def dout(name, shape, dt=F32):
            return nc.dram_tensor(name, list(shape), dt, kind="ExternalOutput").ap()

        self.xp = din("xp", [NP, SP, D])
        self.xs = din("xs", [NS, 16, D])
        self.stc = din("stc", [2, NS, 2, D])
        self.ck = din("ck", [NS, SPAST, D])
        self.cv = din("cv", [NS, SPAST, D])
        self.w_in = din("w_in", [2, D, 3 * D])
        self.wdw_d = din("wdw", [128, 48])
        self.w_out = din("w_out", [2, D, D])
        self.sbq = din("sbq", [2, D, D])
        self.sbo = din("sbo", [2, D, D])
        self.wk = din("wk", [D, D])
        self.wv = din("wv", [D, D])
        self.pwq = din("pwq", [4, D, 2 * D])
        self.skT_d = din("skT", [4, 128, 2048])
        self.uv = din("uv", [4, 16384, 2 * D])
        self.lng = din("lng", [8, D])
        self.lnb = din("lnb", [8, D])
        self.yp = dout("yp", [NP, SP, D])
        self.ys = dout("ys", [NS, 16, D])
        self.ncp = dout("ncp", [2, NP, 2, D])
        self.nkp = dout("nkp", [NP, SP, D])
        self.nvp = dout("nvp", [NP, SP, D])
        self.ncs = dout("ncs", [2, NS, 2, D])
        self.nks = dout("nks", [NS, 16, D])
        self.nvs = dout("nvs", [NS, 16, D])

        self.chunks = []
        self.cid = {}

        def addchunks(key, ap2d, ncols):
            for j in range(ncols // 512):
                self.cid[(key, j)] = len(self.chunks)
                self.chunks.append(("w", ap2d[:, j * 512:(j + 1) * 512]))

        for l in range(2):
            addchunks(("w_in", l), self.w_in[l], 3 * D)
            addchunks(("w_out", l), self.w_out[l], D)
            addchunks(("sbq", l), self.sbq[l], D)
            addchunks(("sbo", l), self.sbo[l], D)
        addchunks(("wk", 0), self.wk, D)
        addchunks(("wv", 0), self.wv, D)
        for l in range(4):
            addchunks(("pwq", l), self.pwq[l], 2 * D)
        for l in range(4):
            self.cid[(("sk", l), 0)] = len(self.chunks)
            self.chunks.append(("sk", self.skT_d[l]))
        NCH = len(self.chunks)
        self.wbf = nc.dram_tensor("wbf", [NCH, 128, 4096], BF16, kind="Internal").ap()
        self.wbf_buf = [Buf(f"wbf{i}") for i in range(NCH)]
        self.uvb = nc.dram_tensor("uvb", [4 * 16384, 2 * D], BF16, kind="Internal").ap()

        self.sempool = [es.enter_context(nc.semaphore(f"sm{i}")) for i in range(96)]
        self.eng = {n: Eng(n, self.sempool) for n in ["tensor", "vector", "scalar", "gpsimd", "sync"]}

        sb = self.sb
        self.wsl = sb("wsl", [128, NWS, 8, 512], BF16)
        self.wsl_buf = [Buf(f"wsl{i}") for i in range(NWS)]
        self.wsl_lane = [self.newlane() for _ in range(NWS)]
        self.KT = sb("KT", [128, 8, 16 * 128], BF16)
        self.Vr = sb("Vr", [128, 16, D], BF16)
        self.KT_buf = [Buf(f"KT{i}") for i in range(16)]
        self.V_buf = [Buf(f"V{i}") for i in range(16)]
        self.gsl = sb("gsl", [128, NGS, 2 * D], BF16)
        self.gsl_buf = [Buf(f"gsl{i}") for i in range(NGS)]
        self.gsl_lane = [self.newlane() for _ in range(NGS)]
        self.hb = [sb(f"hb{i}", [128, D], F32) for i in range(2)]
        self.hb_buf = [Buf(f"hb{i}") for i in range(2)]
        self.hb_lane_in = [self.newlane() for _ in range(2)]
        self.hb_lane_out = [self.newlane() for _ in range(2)]
        self.pre = sb("pre", [128, D], F32)
        self.pre_buf = Buf("pre")
        self.pre_lane = self.newlane()
        self.h16 = sb("h16", [128, D], BF16)
        self.h16_buf = Buf("h16")
        self.hT = sb("hT", [128, 8, 128], BF16)
        self.hT_buf = Buf("hT")
        self.big1 = sb("big1", [128, 2048], F32)
        self.big1_buf = Buf("big1")
        self.big2 = sb("big2", [128, 2048], F32)
        self.big2_buf = Buf("big2")
        self.big2_lane = self.newlane()
        self.scr = sb("scr", [128, 8192], BF16)
        self.scr_buf = Buf("scr")
        self.ub = [sb(f"ub{l}", [128, 8, 130], F32) for l in range(2)]
        self.ub_buf = [Buf(f"ub{l}") for l in range(2)]
        self.bacc = sb("bacc", [128, 8, 128], BF16)
        self.bacc_buf = Buf("bacc")
        self.lnp = sb("lnp", [128, 2, D], F32)
        self.lnp_buf = Buf("lnp")
        self.lnp_lane = self.newlane()
        self.qT = sb("qT", [128, 16, 128], BF16)
        self.qT_buf = Buf("qT")
        self.QT = self.qT[:, 0:8, :]
        self.QT_buf = self.qT_buf
        self.oTb = self.bacc
        self.oTb_buf = self.bacc_buf
        self.Cs = sb("Cs", [16, 512], BF16)
        self.Cs_buf = Buf("Cs")
        self.dg = [sb(f"dg{i}", [128, GRP, 128], BF16) for i in range(2)]
        self.dg_buf = [Buf(f"dg{i}") for i in range(2)]
        self.ab = [self.dg[i][:].rearrange("p a b -> p (a b)") for i in range(2)]
        self.ab_buf = self.dg_buf
        self.v16 = sb("v16", [128, 16, 16], F32)
        self.i16 = sb("i16", [128, 16, 16], U32)
        self.i16f = sb("i16f", [128, 16, 16], F32)
        self.tmp1 = self.big2[:, 0:128]
        self.tmp2 = self.big1[:, 0:256]
        self.best = sb("best", [128, 8, 16], F32)
        self.pos = sb("pos", [128, 8, 16], U32)
        self.posa = self.big2[:, 0:128].bitcast(I32).rearrange("p (a b) -> p a b", a=8)
        self.posb = self.big2[:, 128:256].bitcast(I32).rearrange("p (a b) -> p a b", a=8)
        self.af = self.big2[:, 256:384].rearrange("p (a b) -> p a b", a=8)
        self.bf = self.big2[:, 384:512].rearrange("p (a b) -> p a b", a=8)
        self.e0 = self.big2[:, 512:640].rearrange("p (a b) -> p a b", a=8)
        self.e1 = self.big2[:, 640:768].rearrange("p (a b) -> p a b", a=8)
        self.ef = sb("ef", [128, 128], F32)
        self.eidx = sb("eidx", [128, 128], I32)
        self.gw = sb("gw", [128, 8, 16], F32)
        self.gex = sb("gex", [128, 8, 16], F32)
        self.gsum = sb("gsum", [128, 8], F32)
        self.actt = self.big1[:, 0:128]
        self.gel = self.big1[:, 128:256]
        self.w4 = self.big1[:, 256:384]
        self.small_buf = {n: Buf(n) for n in ["v16", "i16", "i16f", "tmp1", "tmp2", "best", "pos", "posa", "posb",
                                                "af", "bf", "e0", "e1", "ef", "eidx", "gw", "gex", "gsum",
                                                "actt", "gel", "w4", "lnst", "cst", "cin"]}
        self.small_buf["tmp1"] = self.big2_buf
        self.small_buf["tmp2"] = self.big1_buf
        self.small_buf["cst"] = self.big2_buf
        self.small_buf["cin"] = self.big2_buf
        self.lnst = sb("lnst", [128, 16], F32)
        self.lnmv = sb("lnmv", [128, 4], F32)
        self.cst = self.big2[0:2, 0:D]
        self.cst_lane = self.newlane()
        self.cin = self.big2[0:2, D:2 * D]
        self.cin_lane = self.newlane()
        self.wdw = sb("wdwS", [128, 48], F32)
        self.wdw_buf = Buf("wdw")
        self.wdw_lane = self.newlane()
        self.iot = sb("iot", [128, 128], I32)
        self.identf = sb("identf", [128, 128], F32)
        self.identb = sb("identb", [128, 128], BF16)
        self.trineg = sb("trineg", [128, 128], BF16)
        self.mask1 = sb("mask1", [128, 128], BF16)
        self.indall = sb("indall", [128, 32], BF16)
        self.seli = sb("seli", [16, 128], I32)
        self.selneg = sb("selneg", [16, 16, 128], BF16)
        self.iot16i = sb("iot16i", [128, 16], I32)
        self.iot16 = sb("iot16", [128, 16], F32)
        self.const_buf = Buf("const")
        self.P = [self.es.enter_context(nc.psum_tensor(f"ps{i}", [128, 1024], F32)) for i in range(4)]
        self.pbuf = [Buf(f"bank{i}") for i in range(8)]

        self.emit_consts()
        self.emit_phase0()
        self.emit_phase0b()
        self.emit_tiles()
        self.emit_final()
        self.replay()

    def bank(self, i):
        return self.P[i // 2][:, (i % 2) * 512:(i % 2 + 1) * 512]

    def emit_consts(self):
        cb = [self.const_buf]
        self.G("iota", [], cb, out=self.iot[:], pattern=[[1, 128]], base=0, channel_multiplier=-1)
        self.G("iota", [], cb, out=self.iot16i[:], pattern=[[1, 16]], base=0, channel_multiplier=0)
        V = self.V
        V("tensor_scalar", cb, cb, out=self.identf[:], in0=self.iot[:], scalar1=0.0, scalar2=None, op0=ALU.is_equal)
        V("tensor_scalar", cb, cb, out=self.identb[:], in0=self.iot[:], scalar1=0.0, scalar2=None, op0=ALU.is_equal)
        V("tensor_scalar", cb, cb, out=self.trineg[:], in0=self.iot[:], scalar1=0.0, scalar2=-1.0,
          op0=ALU.is_le, op1=ALU.mult)
        V("tensor_scalar", cb, cb, out=self.mask1[:], in0=self.iot[:], scalar1=0.0, scalar2=None, op0=ALU.is_gt)
        V("memset", [], cb, ap=self.indall[:], constant=0.0)
        V("memset", cb, cb, ap=self.indall[:, 15:16], constant=1.0)
        for kbi in range(16):
            self.G("iota", cb, cb, out=self.seli[:], pattern=[[0, 128]], base=-kbi, channel_multiplier=1)
            V("tensor_scalar", cb, cb, out=self.selneg[:, kbi, :], in0=self.seli[:], scalar1=0.0, scalar2=-1.0,
              op0=ALU.is_gt, op1=ALU.mult)
        V("tensor_copy", cb, cb, out=self.iot16[:], in_=self.iot16i[:])
        self.dma("sync", self.wdw_lane, [], [self.wdw_buf], out=self.wdw[:], in_=self.wdw_d[:, :])

    def emit_phase0(self):
        for ci, (kind, src) in enumerate(self.chunks):
            s = ci % NWS
            if kind == "w":
                srcap = src.rearrange("(dc p) f -> p dc f", p=128)
                dst = self.wsl[:, s, :, :]
                dram = self.wbf[ci].rearrange("p (dc f) -> p dc f", dc=8)
            else:
                srcap = src
                dst = self.wsl[:, s, 0:4, :].rearrange("p a b -> p (a b)")
                dram = self.wbf[ci][:, 0:2048]
            self.dma("gpsimd", self.wsl_lane[s], [], [self.wsl_buf[s]], out=dst, in_=srcap)
            self.dma("sync", self.wsl_lane[s], [self.wsl_buf[s]], [self.wbf_buf[ci]], out=dram, in_=dst)

    def emit_phase0b(self):
        uvf = self.uv.rearrange("l e d -> (l e) d")
        R = NGS // 2
        rows = 128 * R
        nchunk = (4 * 16384) // rows
        evs = []
        for c in range(nchunk):
            hf = c % 2
            bufs = self.gsl_buf[hf * R:(hf + 1) * R]
            st = self.gsl[:, hf * R:(hf + 1) * R, :]
            src = uvf[c * rows:(c + 1) * rows, :].rearrange("(p r) d -> p r d", r=R)
            dst = self.uvb[c * rows:(c + 1) * rows, :].rearrange("(p r) d -> p r d", r=R)
            self.dma("gpsimd", self.gsl_lane[hf * R], [], bufs, out=st, in_=src)
            evs.append(self.dma("sync", self.gsl_lane[hf * R + 1], bufs, [], out=dst, in_=st))
        for ev in evs[-2:]:
            self._wait_for(self.eng["gpsimd"], ev, "raw")

    def tile_chunk_order(self):
        o = []
        for l in range(2):
            if l >= self.n_layers:
                break
            o += [self.cid[(("w_in", l), j)] for j in range(6)]
            o += [self.cid[(("w_out", l), j)] for j in range(2)]
            o += [self.cid[(("pwq", l), j)] for j in range(4)]
            o += [self.cid[(("sk", l), 0)]]
        if self.n_layers > 2:
            o += [self.cid[(("wk", 0), j)] for j in range(2)]
            o += [self.cid[(("wv", 0), j)] for j in range(2)]
        for l in range(2, 4):
            if l >= self.n_layers:
                break
            o += [self.cid[(("sbq", l - 2), j)] for j in range(2)]
            o += [self.cid[(("sbo", l - 2), j)] for j in range(2)]
            o += [self.cid[(("pwq", l), j)] for j in range(4)]
            o += [self.cid[(("sk", l), 0)]]
        return o

    def wnext(self, expect):
        while self.w_issued < min(len(self.wlist), self.w_i + NWS):
            ci = self.wlist[self.w_issued]
            s = self.w_issued % NWS
            kind = self.chunks[ci][0]
            if kind == "w":
                dst = self.wsl[:, s, :, :]
                dram = self.wbf[ci].rearrange("p (dc f) -> p dc f", dc=8)
            else:
                dst = self.wsl[:, s, 0:4, :].rearrange("p a b -> p (a b)")
                dram = self.wbf[ci][:, 0:2048]
            self.dma("sync", self.wsl_lane[s], [self.wbf_buf[ci]], [self.wsl_buf[s]], out=dst, in_=dram)
            self.w_issued += 1
        ci = self.wlist[self.w_i]
        assert ci == self.cid[expect], (ci, expect)
        s = self.w_i % NWS
        self.w_i += 1
        return s

    def emit_tiles(self):
        seqs = []
        for b in range(self.n_pseq):
            seqs.append(dict(kind="p", b=b, nt=self.n_ptiles, nvalid=128, kb0=0))
        for b in range(self.n_sseq):
            seqs.append(dict(kind="s", b=b, nt=1, nvalid=16, kb0=self.n_past))
        ntiles = sum(s["nt"] for s in seqs)
        self.wlist = self.tile_chunk_order() * ntiles
        self.w_i = 0
        self.w_issued = 0
        self.tcount = 0
        self.stores = []
        for seq in seqs:
            if seq["kind"] == "s" and self.n_layers > 2:
                self.load_cache(seq)
            for ti in range(seq["nt"]):
                self.tile(seq, ti)
                self.tcount += 1

    def load_cache(self, seq):
        b = seq["b"]
        for kb in range(self.n_past):
            self.dma("sync", self.big2_lane, [], [self.big2_buf], out=self.big2[:, 0:D],
                     in_=self.ck[b, kb * 128:(kb + 1) * 128, :])
            self.act([self.big2_buf], [self.h16_buf], out=self.h16[:], in_=self.big2[:, 0:D], func=AF.Copy)
            self.k_transposes(kb)
            self.dma("sync", self.big2_lane, [], [self.big2_buf], out=self.big2[:, 0:D],
                     in_=self.cv[b, kb * 128:(kb + 1) * 128, :])
            self.act([self.big2_buf], [self.V_buf[kb]], out=self.Vr[:, kb, :], in_=self.big2[:, 0:D], func=AF.Copy)

    def k_transposes(self, kb):
        bi = 0
        pb = self.bank(bi).bitcast(BF16)
        for pair in range(8):
            self.T("transpose", [self.h16_buf, self.const_buf], [self.pbuf[bi]],
                   out=pb[:, pair * 128:(pair + 1) * 128], in_=self.h16[:, pair * 128:(pair + 1) * 128],
                   identity=self.identb[:])
        self.act([self.pbuf[bi]], [self.KT_buf[kb]], out=self.KT[:, :, kb * 128:(kb + 1) * 128],
                 in_=pb.rearrange("p (a b) -> p a b", a=8), func=AF.Copy)

    def make_hT(self, hi):
        self.act([self.hb_buf[hi]], [self.h16_buf], out=self.h16[:], in_=self.hb[hi][:], func=AF.Copy)
        bi = 1
        pb = self.bank(bi).bitcast(BF16)
        for c in range(8):
            self.T("transpose", [self.h16_buf, self.const_buf], [self.pbuf[bi]],
                   out=pb[:, c * 128:(c + 1) * 128], in_=self.h16[:, c * 128:(c + 1) * 128],
                   identity=self.identb[:])
        self.V("tensor_copy", [self.pbuf[bi]], [self.hT_buf], out=self.hT[:].rearrange("p a b -> p (a b)"),
               in_=pb)

    def layernorm(self, idx, hi):
        self.dma("sync", self.lnp_lane, [], [self.lnp_buf], out=self.lnp[:, 0:1, :],
                 in_=self.lng[idx:idx + 1, :].partition_broadcast(128))
        self.dma("sync", self.lnp_lane, [], [self.lnp_buf], out=self.lnp[:, 1:2, :],
                 in_=self.lnb[idx:idx + 1, :].partition_broadcast(128))
        sbuf = self.small_buf["lnst"]
        V = self.V
        V("bn_stats", [self.pre_buf], [sbuf], out=self.lnst[:, 0:6], in_=self.pre[:, 0:512])
        V("bn_stats", [self.pre_buf], [sbuf], out=self.lnst[:, 6:12], in_=self.pre[:, 512:1024])
        V("bn_aggr", [sbuf], [sbuf], out=self.lnmv[:, 0:2], in_=self.lnst[:, 0:12])
        V("tensor_scalar", [sbuf], [sbuf], out=self.lnmv[:, 2:3], in0=self.lnmv[:, 1:2], scalar1=EPS, scalar2=None,
          op0=ALU.add)
        self.act([sbuf], [sbuf], out=self.lnmv[:, 2:3], in_=self.lnmv[:, 2:3], func=AF.Ln)
        self.act([sbuf], [sbuf], out=self.lnmv[:, 3:4], in_=self.lnmv[:, 2:3], func=AF.Exp, scale=-0.5)
        V("tensor_scalar", [self.pre_buf, sbuf], [self.pre_buf], out=self.pre[:], in0=self.pre[:],
          scalar1=self.lnmv[:, 0:1], scalar2=self.lnmv[:, 3:4], op0=ALU.subtract, op1=ALU.mult)
        V("tensor_tensor", [self.pre_buf, self.lnp_buf], [self.pre_buf], out=self.pre[:], in0=self.pre[:],
          in1=self.lnp[:, 0, :], op=ALU.mult)
        V("tensor_tensor", [self.pre_buf, self.lnp_buf], [self.hb_buf[hi]], out=self.hb[hi][:], in0=self.pre[:],
          in1=self.lnp[:, 1, :], op=ALU.add)

    def residual(self, hi, pbanks):
        pidx = pbanks
        self.V("scalar_tensor_tensor", [self.hb_buf[hi], self.pbuf[2 * pidx], self.pbuf[2 * pidx + 1]],
               [self.pre_buf], out=self.pre[:], in0=self.hb[hi][:], scalar=ALPHA, in1=self.P[pidx][:, :],
               op0=ALU.mult, op1=ALU.add)

    def tok_major_mm(self, key, lhs, lhs_buf, pidx):
        for half in range(2):
            s = self.wnext((key, half))
            bi = 2 * pidx + half
            for dc in range(8):
                self.mm([lhs_buf, self.wsl_buf[s]], [self.pbuf[bi]], out=self.bank(bi), lhsT=lhs[:, dc, :],
                        rhs=self.wsl[:, s, dc, :], start=(dc == 0), stop=(dc == 7))

    def feat_major_chunk(self, key, j, bi):
        s = self.wnext((key, j))
        for fc in range(4):
            for dc in range(8):
                self.mm([self.hT_buf, self.wsl_buf[s]], [self.pbuf[bi]],
                        out=self.bank(bi)[:, fc * 128:(fc + 1) * 128], lhsT=self.wsl[:, s, dc, fc * 128:(fc + 1) * 128],
                        rhs=self.hT[:, dc, :], start=(dc == 0), stop=(dc == 7))

    def tile(self, seq, ti):
        hi = self.tcount % 2
        b, nv = seq["b"], seq["nvalid"]
        h = self.hb[hi]
        hbuf = self.hb_buf[hi]
        if seq["kind"] == "p":
            self.dma("sync", self.hb_lane_in[hi], [], [hbuf], out=h[:], in_=self.xp[b, ti * 128:(ti + 1) * 128, :])
        else:
            self.V("memset", [], [hbuf], ap=h[:], constant=0.0)
            self.dma("sync", self.hb_lane_in[hi], [], [hbuf], out=h[0:16, :], in_=self.xs[b, :, :])
        self.make_hT(hi)
        kb = seq["kb0"] + ti
        for l in range(self.n_layers):
            if l < 2:
                self.conv_layer(l, seq, ti, hi)
            else:
                if l == 2:
                    self.kv(seq, ti, hi, kb)
                self.attn_layer(l - 2, seq, ti, hi, kb)
            self.layernorm(2 * l, hi)
            self.make_hT(hi)
            self.peer(l, hi)
            self.layernorm(2 * l + 1, hi)
            if l < self.n_layers - 1:
                self.make_hT(hi)
        if seq["kind"] == "p":
            ev = self.dma("sync", self.hb_lane_out[hi], [hbuf], [], out=self.yp[b, ti * 128:(ti + 1) * 128, :], in_=h[:])
        else:
            ev = self.dma("sync", self.hb_lane_out[hi], [hbuf], [], out=self.ys[b, :, :], in_=h[0:16, :])
        self.stores.append(ev)

    def conv_layer(self, l, seq, ti, hi):
        V = self.V
        ub, ubb = self.ub[l], self.ub_buf[l]
        gates = self.scr[:].bitcast(F32)[:, 0:3072].rearrange("p (a b) -> p a b", a=24)
        if ti == 0:
            if seq["kind"] == "p":
                V("memset", [], [ubb], ap=ub[:, :, 0:2], constant=0.0)
            else:
                cb = self.small_buf["cin"]
                self.dma("sync", self.cin_lane, [], [cb], out=self.cin, in_=self.stc[l, seq["b"], :, :])
                bi = 0
                for c in range(8):
                    self.T("transpose", [cb, self.const_buf], [self.pbuf[bi]], out=self.bank(bi)[:, 2 * c:2 * c + 2],
                           in_=self.big2[0:2, D + c * 128:D + (c + 1) * 128], identity=self.identf[0:2, 0:2])
                V("tensor_copy", [self.pbuf[bi]], [ubb], out=ub[:, :, 0:2],
                  in_=self.bank(bi)[:, 0:16].rearrange("p (a b) -> p a b", a=8))
        for j in range(6):
            bi = j % 2
            self.feat_major_chunk(("w_in", l), j, bi)
            self.act([self.pbuf[bi]], [self.scr_buf], out=gates[:, 4 * j:4 * j + 4, :],
                     in_=self.bank(bi).rearrange("p (a b) -> p a b", a=4), func=AF.Copy)
        V("tensor_tensor", [self.scr_buf], [ubb], out=ub[:, :, 2:130], in0=gates[:, 8:16, :], in1=gates[:, 16:24, :],
          op=ALU.mult)
        acc = self.big1[:, 0:1024].rearrange("p (a b) -> p a b", a=8)
        ab = self.big1_buf
        for c in range(8):
            col = lambda w: self.wdw[:, (l * 3 + w) * 8 + c:(l * 3 + w) * 8 + c + 1]
            V("tensor_scalar", [ubb, self.wdw_buf], [ab], out=acc[:, c, :], in0=ub[:, c, 0:128], scalar1=col(0),
              scalar2=None, op0=ALU.mult)
            V("scalar_tensor_tensor", [ubb, ab, self.wdw_buf], [ab], out=acc[:, c, :], in0=ub[:, c, 1:129],
              scalar=col(1), in1=acc[:, c, :], op0=ALU.mult, op1=ALU.add)
            V("scalar_tensor_tensor", [ubb, ab, self.wdw_buf], [ab], out=acc[:, c, :], in0=ub[:, c, 2:130],
              scalar=col(2), in1=acc[:, c, :], op0=ALU.mult, op1=ALU.add)
        V("tensor_tensor", [self.scr_buf, ab], [self.bacc_buf], out=self.bacc[:], in0=gates[:, 0:8, :], in1=acc,
          op=ALU.mult)
        nv = seq["nvalid"]
        if ti == seq["nt"] - 1:
            cb = self.small_buf["cst"]
            for c in range(8):
                bi = c // 4
                self.T("transpose", [ubb, self.const_buf], [self.pbuf[bi]],
                       out=self.P[0][0:2, c * 128:(c + 1) * 128], in_=ub[:, c, nv:nv + 2], identity=self.identf[:])
            self.act([self.pbuf[0], self.pbuf[1]], [cb], out=self.cst, in_=self.P[0][0:2, :], func=AF.Copy)
            dst = (self.ncp if seq["kind"] == "p" else self.ncs)[l, seq["b"], :, :]
            ev = self.dma("sync", self.cst_lane, [cb], [], out=dst, in_=self.cst)
            self.stores.append(ev)
        else:
            V("tensor_copy", [ubb], [ubb], out=ub[:, :, 0:2], in_=ub[:, :, 128:130])
        self.tok_major_mm(("w_out", l), self.bacc, self.bacc_buf, 0)
        self.residual(hi, 0)

    def peer(self, l, hi):
        V, A = self.V, self.A
        sm = self.small_buf
        h, hbuf = self.hb[hi], self.hb_buf[hi]
        for j in range(4):
            bi = j % 2
            self.feat_major_chunk(("pwq", l), j, bi)
            self.act([self.pbuf[bi]], [self.qT_buf], out=self.qT[:, 4 * j:4 * j + 4, :],
                     in_=self.bank(bi).rearrange("p (a b) -> p a b", a=4), func=AF.Copy)
        s = self.wnext((("sk", l), 0))
        skT = self.wsl[:, s, 0:4, :].rearrange("p a b -> p (a b)")
        scs = self.big1[:].rearrange("p (a b) -> p a b", a=16)
        for g in range(16):
            bi = 4 + g // 4
            self.mm([self.qT_buf, self.wsl_buf[s]], [self.pbuf[bi]], out=self.bank(bi)[:, (g % 4) * 128:(g % 4 + 1) * 128],
                    lhsT=self.qT[:, g, :], rhs=skT[:, g * 128:(g + 1) * 128], start=True, stop=True)
        for q in range(4):
            self.act([self.pbuf[4 + q]], [self.big1_buf], out=self.big1[:, q * 512:(q + 1) * 512], in_=self.bank(4 + q),
                     func=AF.Copy)
        b1 = self.big1_buf
        for g in range(16):
            V("max", [b1], [sm["v16"]], out=self.v16[:, g, 0:8], in_=scs[:, g, :])
            V("max_index", [b1, sm["v16"]], [sm["i16"]], out=self.i16[:, g, 0:8], in_max=self.v16[:, g, 0:8],
              in_values=scs[:, g, :])
            V("match_replace", [b1, sm["v16"]], [sm["tmp1"]], out=self.tmp1, in_to_replace=self.v16[:, g, 0:8],
              in_values=scs[:, g, :], imm_value=NEG)
            V("max", [sm["tmp1"]], [sm["v16"]], out=self.v16[:, g, 8:16], in_=self.tmp1)
            V("max_index", [sm["tmp1"], sm["v16"]], [sm["i16"]], out=self.i16[:, g, 8:16], in_max=self.v16[:, g, 8:16],
              in_values=self.tmp1)
        V("tensor_copy", [sm["i16"]], [sm["i16f"]], out=self.i16f[:], in_=self.i16[:])
        cand = self.big2[:].rearrange("p (h a b) -> p h a b", h=8, a=16)
        v16h = self.v16[:].rearrange("p (h t) k -> p h t k", t=2)
        i16h = self.i16f[:].rearrange("p (h t) k -> p h t k", t=2)
        V("tensor_tensor", [sm["v16"]], [self.big2_buf], out=cand,
          in0=v16h[:, :, 0, :].unsqueeze(3).to_broadcast([128, 8, 16, 16]),
          in1=v16h[:, :, 1, :].unsqueeze(2).to_broadcast([128, 8, 16, 16]), op=ALU.add)
        b2 = self.big2_buf
        for hh in range(8):
            cf = self.big2[:, hh * 256:(hh + 1) * 256]
            V("max", [b2], [sm["best"]], out=self.best[:, hh, 0:8], in_=cf)
            V("max_index", [b2, sm["best"]], [sm["pos"]], out=self.pos[:, hh, 0:8], in_max=self.best[:, hh, 0:8],
              in_values=cf)
            V("match_replace", [b2, sm["best"]], [sm["tmp2"]], out=self.tmp2, in_to_replace=self.best[:, hh, 0:8],
              in_values=cf, imm_value=NEG)
            V("max", [sm["tmp2"]], [sm["best"]], out=self.best[:, hh, 8:16], in_=self.tmp2)
            V("max_index", [sm["tmp2"], sm["best"]], [sm["pos"]], out=self.pos[:, hh, 8:16],
              in_max=self.best[:, hh, 8:16], in_values=self.tmp2)
        posi = self.pos[:].bitcast(I32)
        V("tensor_single_scalar", [sm["pos"]], [sm["posa"]], out=self.posa[:], in_=posi, scalar=4,
          op=ALU.arith_shift_right)
        V("tensor_single_scalar", [sm["pos"]], [sm["posb"]], out=self.posb[:], in_=posi, scalar=15,
          op=ALU.bitwise_and)
        V("tensor_copy", [sm["posa"]], [sm["af"]], out=self.af[:], in_=self.posa[:])
        V("tensor_copy", [sm["posb"]], [sm["bf"]], out=self.bf[:], in_=self.posb[:])
        eq = self.big1[:].rearrange("p (h a b) -> p h a b", h=8, a=16)
        io = self.iot16[:].unsqueeze(1).unsqueeze(1).to_broadcast([128, 8, 16, 16])
        for (src, srcb, t, dst, dstb) in ((self.af, sm["af"], 0, self.e0, sm["e0"]), (self.bf, sm["bf"], 1, self.e1, sm["e1"])):
            V("tensor_tensor", [srcb, self.const_buf], [b1], out=eq,
              in0=src[:].unsqueeze(3).to_broadcast([128, 8, 16, 16]), in1=io, op=ALU.is_equal)
            V("tensor_tensor", [b1, sm["i16f"]], [b1], out=eq, in0=eq,
              in1=i16h[:, :, t, :].unsqueeze(2).to_broadcast([128, 8, 16, 16]), op=ALU.mult)
            V("tensor_reduce", [b1], [dstb], out=dst[:], in_=eq, axis=AX.X, op=ALU.add)
        V("scalar_tensor_tensor", [sm["e0"], sm["e1"]], [sm["ef"]], out=self.ef[:],
          in0=self.e0[:].rearrange("p a b -> p (a b)"), scalar=128.0, in1=self.e1[:].rearrange("p a b -> p (a b)"),
          op0=ALU.mult, op1=ALU.add)
        if l > 0:
            V("tensor_scalar", [sm["ef"]], [sm["ef"]], out=self.ef[:], in0=self.ef[:], scalar1=float(l * 16384),
              scalar2=None, op0=ALU.add)
        V("tensor_copy", [sm["ef"]], [sm["eidx"]], out=self.eidx[:], in_=self.ef[:])
        V("tensor_tensor", [sm["best"]], [sm["gex"]], out=self.gex[:], in0=self.best[:],
          in1=self.best[:, :, 0:1].to_broadcast([128, 8, 16]), op=ALU.subtract)
        self.act([sm["gex"]], [sm["gex"]], out=self.gex[:], in_=self.gex[:], func=AF.Exp)
        V("tensor_reduce", [sm["gex"]], [sm["gsum"]], out=self.gsum[:], in_=self.gex[:], axis=AX.X, op=ALU.add)
        V("reciprocal", [sm["gsum"]], [sm["gsum"]], out=self.gsum[:], in_=self.gsum[:])
        V("tensor_tensor", [sm["gex"], sm["gsum"]], [sm["gw"]], out=self.gw[:], in0=self.gex[:],
          in1=self.gsum[:].unsqueeze(2).to_broadcast([128, 8, 16]), op=ALU.mult)
        gwf = self.gw[:].rearrange("p a b -> p (a b)")
        uvt = self.uvb
        for j0 in range(0, 128, GRP):
            gi = (j0 // GRP) % 2
            for jj in range(GRP):
                j = j0 + jj
                sl = j % NGS
                self.dma("gpsimd", self.gsl_lane[sl], [sm["eidx"]], [self.gsl_buf[sl]], out=self.gsl[:, sl, :],
                         in_=uvt[:, :], method="indirect_dma_start", out_offset=None,
                         in_offset=bass.IndirectOffsetOnAxis(ap=self.eidx[:, j:j + 1], axis=0))
                V("scalar_tensor_tensor", [self.gsl_buf[sl], hbuf], [self.h16_buf, sm["actt"]], out=self.h16[:],
                  in0=self.gsl[:, sl, 0:D], scalar=1.0, in1=h[:], op0=ALU.mult, op1=ALU.mult,
                  accum_out=self.actt[:, j:j + 1])
            self.act([sm["actt"]], [sm["gel"]], out=self.gel[:, j0:j0 + GRP], in_=self.actt[:, j0:j0 + GRP], func=AF.Gelu)
            V("tensor_tensor", [sm["gel"], sm["gw"]], [sm["w4"]], out=self.w4[:, j0:j0 + GRP], in0=self.gel[:, j0:j0 + GRP],
              in1=gwf[:, j0:j0 + GRP], op=ALU.mult)
            V("tensor_tensor", [sm["w4"], self.const_buf], [self.dg_buf[gi]], out=self.dg[gi][:],
              in0=self.identb[:].unsqueeze(1).to_broadcast([128, GRP, 128]),
              in1=self.w4[:, j0:j0 + GRP].unsqueeze(2).to_broadcast([128, GRP, 128]), op=ALU.mult)
            for jj in range(GRP):
                j = j0 + jj
                sl = j % NGS
                for half in range(2):
                    bi = 2 + half
                    self.mm([self.dg_buf[gi], self.gsl_buf[sl]], [self.pbuf[bi]], out=self.bank(bi),
                            lhsT=self.dg[gi][:, jj, :], rhs=self.gsl[:, sl, D + half * 512:D + (half + 1) * 512],
                            start=(j == 0), stop=(j == 127))
        self.residual(hi, 1)

    def kv(self, seq, ti, hi, kb):
        nv, b = seq["nvalid"], seq["b"]
        r0 = ti * 128
        self.tok_major_mm(("wk", 0), self.hT, self.hT_buf, 0)
        self.act([self.pbuf[0], self.pbuf[1]], [self.pre_buf], out=self.pre[:], in_=self.P[0][:, :], func=AF.Copy)
        self.act([self.pbuf[0], self.pbuf[1]], [self.h16_buf], out=self.h16[:], in_=self.P[0][:, :], func=AF.Copy)
        if seq["kind"] == "p":
            ev = self.dma("sync", self.pre_lane, [self.pre_buf], [], out=self.nkp[b, r0:r0 + 128, :], in_=self.pre[:])
        else:
            ev = self.dma("sync", self.pre_lane, [self.pre_buf], [], out=self.nks[b, :, :], in_=self.pre[0:16, :])
        self.stores.append(ev)
        self.k_transposes(kb)
        self.tok_major_mm(("wv", 0), self.hT, self.hT_buf, 0)
        self.act([self.pbuf[0], self.pbuf[1]], [self.big2_buf], out=self.big2[:, 0:D], in_=self.P[0][:, :], func=AF.Copy)
        self.act([self.pbuf[0], self.pbuf[1]], [self.V_buf[kb]], out=self.Vr[:, kb, :], in_=self.P[0][:, :], func=AF.Copy)
        if seq["kind"] == "p":
            ev = self.dma("sync", self.big2_lane, [self.big2_buf], [], out=self.nvp[b, r0:r0 + 128, :], in_=self.big2[:, 0:D])
        else:
            ev = self.dma("sync", self.big2_lane, [self.big2_buf], [], out=self.nvs[b, :, :], in_=self.big2[0:16, 0:D])
        self.stores.append(ev)

    def attn_layer(self, j, seq, ti, hi, kb):
        V = self.V
        nkb = kb + 1
        V("memset", [], [self.qT_buf], ap=self.qT[:], constant=0.0)
        for jj in range(2):
            bi = jj % 2
            self.feat_major_chunk(("sbq", j), jj, bi)
            for r in range(2):
                self.act([self.pbuf[bi]], [self.qT_buf], out=self.qT[r * 64:(r + 1) * 64, 8 * jj + r:8 * jj + 8:2, :],
                         in_=self.bank(bi)[r * 64:(r + 1) * 64, :].rearrange("p (a b) -> p a b", a=4), func=AF.Copy,
                         scale=0.125)
        if DBG == 1:
            self.V("tensor_scalar", [self.hb_buf[hi]], [self.pre_buf], out=self.pre[:], in0=self.hb[hi][:], scalar1=ALPHA,
                   scalar2=None, op0=ALU.mult)
            for half in range(2):
                self.wnext((("sbo", j), half))
            return
        Pb = self.scr[:].rearrange("p (a b) -> p a b", a=16)
        Eb = self.big2[:, 0:512]
        oT = self.P[1]
        first_done = {}
        for g in range(4):
            heads = [4 * g + hh for hh in range(4)]

            def zmm(bi, start_first, last_stop):
                for hh, hd in enumerate(heads):
                    pair, r = hd // 2, hd % 2
                    self.mm([self.KT_buf[kbi_], self.QT_buf], [self.pbuf[bi]],
                            out=self.bank(bi)[:, hh * 128:(hh + 1) * 128],
                            lhsT=self.KT[:, pair, kbi_ * 128:(kbi_ + 1) * 128],
                            rhs=self.qT[:, hd, :],
                            start=(start_first and hh == 0), stop=(last_stop and hh == 3), skip_group_check=True)
            for kbi_ in range(nkb):
                zb = 4 + kbi_ % 2
                zmm(zb, True, True)
                self.act([self.pbuf[zb]], [self.big2_buf], out=Eb, in_=self.bank(zb), func=AF.Exp)
                self.act([self.big2_buf], [self.scr_buf], out=Pb[:, kbi_, :], in_=Eb, func=AF.Ln, bias=1.0)
                if kbi_ == kb:
                    V("tensor_tensor", [self.scr_buf, self.const_buf], [self.scr_buf], out=Pb[:, kbi_, :].rearrange("p (a b) -> p a b", a=4),
                      in0=Pb[:, kbi_, :].rearrange("p (a b) -> p a b", a=4),
                      in1=self.mask1[:].unsqueeze(1).to_broadcast([128, 4, 128]), op=ALU.mult)
                self.mm([self.scr_buf, self.const_buf], [self.pbuf[6]], out=self.bank(6)[0:16, :],
                        lhsT=self.indall[:, 15 - kbi_:31 - kbi_], rhs=Pb[:, kbi_, :], start=(kbi_ == 0),
                        stop=(kbi_ == nkb - 1))
            if DBG == 2:
                continue
            self.act([self.pbuf[6]], [self.Cs_buf], out=self.Cs[:], in_=self.bank(6)[0:16, :], func=AF.Copy)
            for kbi_ in range(nkb):
                zs = kbi_ % 2
                ai = kbi_ % 2
                self.mm([self.scr_buf, self.const_buf], [self.pbuf[zs]], out=self.bank(zs), lhsT=self.trineg[:],
                        rhs=Pb[:, kbi_, :], start=True, stop=False, skip_group_check=True)
                self.mm([self.Cs_buf, self.const_buf], [self.pbuf[zs]], out=self.bank(zs), lhsT=self.selneg[0:16, kbi_, :],
                        rhs=self.Cs[:], start=False, stop=False, skip_group_check=True)
                zmm(zs, False, True)
                self.act([self.pbuf[zs]], [self.ab_buf[ai]], out=self.ab[ai], in_=self.bank(zs), func=AF.Exp)
                if DBG == 3:
                    continue
                if kbi_ == kb:
                    V("tensor_tensor", [self.ab_buf[ai], self.const_buf], [self.ab_buf[ai]], out=self.dg[ai][:],
                      in0=self.dg[ai][:], in1=self.mask1[:].unsqueeze(1).to_broadcast([128, 4, 128]), op=ALU.mult)
                for hh, hd in enumerate(heads):
                    pair, r = hd // 2, hd % 2
                    bk = pair // 4
                    key = (bk, r)
                    st = key not in first_done
                    first_done[key] = True
                    self.mm([self.ab_buf[ai], self.V_buf[kbi_]], [self.pbuf[2 + bk]],
                            out=oT[r * 64:(r + 1) * 64, pair * 128:(pair + 1) * 128],
                            lhsT=self.Vr[:, kbi_, hd * 64:(hd + 1) * 64], rhs=self.ab[ai][:, hh * 128:(hh + 1) * 128],
                            start=st, stop=(kbi_ == nkb - 1), skip_group_check=True)
        if DBG in (2, 3):
            self.V("tensor_scalar", [self.hb_buf[hi]], [self.pre_buf], out=self.pre[:], in0=self.hb[hi][:], scalar1=ALPHA,
                   scalar2=None, op0=ALU.mult)
            for half in range(2):
                self.wnext((("sbo", j), half))
            return
        self.act([self.pbuf[2], self.pbuf[3]], [self.oTb_buf], out=self.oTb[:].rearrange("p a b -> p (a b)"),
                 in_=oT[:, :], func=AF.Copy)
        self.tok_major_mm(("sbo", j), self.oTb, self.oTb_buf, 0)
        self.residual(hi, 0)

    def emit_final(self):
        eng = self.eng["sync"]
        for ev in self.stores:
            self._wait_for(eng, ev, "raw")

    def replay(self):
        nc = self.nc
        engs = self.eng
        with nc.Block() as block:
            def run(e, name):
                for it in engs[name].q:
                    if it[0] == "w":
                        e.wait_ge(it[1], it[2])
                    else:
                        _, method, kw, sem, inc = it
                        ins = getattr(e, method)(**kw)
                        ins.then_inc(sem, inc)

            @block.sync
            def _(e):
                run(e, "sync")

            @block.gpsimd
            def _(e):
                run(e, "gpsimd")

            @block.tensor
            def _(e):
                run(e, "tensor")

            @block.vector
            def _(e):
                run(e, "vector")

            @block.scalar
            def _(e):
                run(e, "scalar")
        self.es.close()


def make_in_maps(inputs, n_cores, n_pseq, n_sseq):
    f = lambda a: np.ascontiguousarray(np.asarray(a, dtype=np.float32))
    xp, xs = f(inputs["x_prompt"]), f(inputs["x_sample"])
    stc, ck, cv = f(inputs["state_conv"]), f(inputs["cache_k"]), f(inputs["cache_v"])
    ck = ck.reshape(ck.shape[0], ck.shape[1], -1)
    cv = cv.reshape(cv.shape[0], cv.shape[1], -1)
    wdw = f(inputs["conv_w_dw"])
    wdw_l = np.ascontiguousarray(wdw.reshape(2, 3, 8, 128).transpose(3, 0, 1, 2).reshape(128, 48))
    sk = f(inputs["peer_subkeys"])
    skT = np.ascontiguousarray(sk.transpose(0, 4, 1, 2, 3).reshape(4, 128, 2048))
    uv = np.concatenate([f(inputs["peer_u"]), f(inputs["peer_v"])], axis=-1)
    shared = dict(w_in=f(inputs["conv_w_in"]), wdw=wdw_l, w_out=f(inputs["conv_w_out"]), sbq=f(inputs["sb_w_q"]),
                  sbo=f(inputs["sb_w_o"]), wk=f(inputs["kv_w_k"]), wv=f(inputs["kv_w_v"]), pwq=f(inputs["peer_w_q"]),
                  skT=skT, uv=uv, lng=f(inputs["ln_g"]).reshape(8, D), lnb=f(inputs["ln_b"]).reshape(8, D))
    maps = []
    for c in range(n_cores):
        m = dict(shared)
        m["xp"] = np.ascontiguousarray(xp[c * n_pseq:(c + 1) * n_pseq])
        m["xs"] = np.ascontiguousarray(xs[c * n_sseq:(c + 1) * n_sseq])
        m["stc"] = np.ascontiguousarray(stc[:, c * n_sseq:(c + 1) * n_sseq])
        m["ck"] = np.ascontiguousarray(ck[c * n_sseq:(c + 1) * n_sseq])
        m["cv"] = np.ascontiguousarray(cv[c * n_sseq:(c + 1) * n_sseq])
        maps.append(m)
    return maps


def assemble(results, n_cores):
    cat = lambda k, ax: np.concatenate([np.asarray(r[k], dtype=np.float32) for r in results], axis=ax)
    yp, ys = cat("yp", 0), cat("ys", 0)
    ncp, ncs = cat("ncp", 1), cat("ncs", 1)
    nkp, nvp, nks, nvs = cat("nkp", 0), cat("nvp", 0), cat("nks", 0), cat("nvs", 0)
    r4 = lambda a: a.reshape(a.shape[0], a.shape[1], 16, 64)
    return (yp, ys, ncp, r4(nkp), r4(nvp), ncs, r4(nks), r4(nvs))


_PROG = {}


def kernel(**inputs):
    n_cores = 8
    if "full" not in _PROG:
        _PROG["full"] = Prog(4, 16, 4, 8)
    prog = _PROG["full"]
    maps = make_in_maps(inputs, n_cores, 4, 4)
    res = run_bass_kernel_spmd(prog.nc, maps, core_ids=list(range(n_cores)))
    return assemble(res.results, n_cores)
```

```python
import numpy as np
from contextlib import ExitStack
import concourse.bass as bass
import concourse.mybir as mybir
from concourse.bass_utils import run_bass_kernel_spmd

F32 = mybir.dt.float32
BF16 = mybir.dt.bfloat16
I32 = mybir.dt.int32
U32 = mybir.dt.uint32
AF = mybir.ActivationFunctionType
ALU = mybir.AluOpType
AX = mybir.AxisListType

D = 1024
ALPHA = (2.0 * 4) ** 0.25
EPS = 1e-5
NEG = -1.0e30
SEM_LIMIT = 32000
SAME_RAW = True
NGS = 8
NWS = 2
GRP = 4
DBG = 0


class Buf:
    __slots__ = ("name", "w", "r")

    def __init__(self, name):
        self.name = name
        self.w = None
        self.r = []


class Lane:
    def __init__(self, pool, inc):
        self.pool = pool
        self.inc = inc
        self.sem = pool.pop()
        self.count = 0

    def next(self):
        if self.count + self.inc > SEM_LIMIT:
            self.sem = self.pool.pop()
            self.count = 0
        self.count += self.inc
        return (self.sem, self.count)


class Eng:
    def __init__(self, name, pool):
        self.name = name
        self.lane = Lane(pool, 1)
        self.q = []
        self.waited = {}


class Prog:
    def __init__(self, n_pseq=4, n_ptiles=16, n_sseq=4, n_past=8, n_layers=4):
        self.n_pseq, self.n_ptiles, self.n_sseq, self.n_past = n_pseq, n_ptiles, n_sseq, n_past
        self.n_layers = n_layers
        self.recording = False
        self.nc = bass.Bass("TRN2", target_bir_lowering=False)
        self.es = ExitStack()
        self.build()

    def _need(self, eng, need, ev, kind):
        sem, val, src = ev
        if src == eng.name:
            if eng.name == "tensor" or kind != "raw" or not SAME_RAW:
                return
        key = id(sem)
        if eng.waited.get(key, 0) >= val:
            return
        if key not in need or need[key][1] < val:
            need[key] = (sem, val)

    def _flush(self, eng, need):
        for key, (sem, val) in need.items():
            eng.waited[key] = val
            eng.q.append(("w", sem, val))

    def _wait_for(self, eng, ev, kind):
        need = {}
        self._need(eng, need, ev, kind)
        self._flush(eng, need)

    def _deps(self, eng, reads, writes):
        need = {}
        for b in reads:
            if b.w is not None:
                self._need(eng, need, b.w, "raw")
        for b in writes:
            if b.w is not None:
                self._need(eng, need, b.w, "waw")
            for r in b.r:
                self._need(eng, need, r, "war")
        self._flush(eng, need)

    def _commit(self, ev, reads, writes):
        for b in reads:
            b.r.append(ev)
        for b in writes:
            b.w = ev
            b.r = []

    def op(self, engname, method, reads, writes, **kw):
        if self.recording:
            return
        eng = self.eng[engname]
        self._deps(eng, reads, writes)
        sem, val = eng.lane.next()
        eng.q.append(("i", method, kw, sem, 1))
        self._commit((sem, val, engname), reads, writes)

    def dma(self, qname, lane, reads, writes, out, in_, method="dma_start", **kw):
        if self.recording:
            return None
        eng = self.eng[qname]
        self._deps(eng, reads, writes)
        sem, val = lane.next()
        kw = dict(kw)
        kw["out"] = out
        kw["in_"] = in_
        eng.q.append(("i", method, kw, sem, 16))
        ev = (sem, val, "dma")
        self._commit(ev, reads, writes)
        return ev

    def V(self, method, reads, writes, **kw):
        self.op("vector", method, reads, writes, **kw)

    def A(self, method, reads, writes, **kw):
        self.op("scalar", method, reads, writes, **kw)

    def T(self, method, reads, writes, **kw):
        self.op("tensor", method, reads, writes, **kw)

    def G(self, method, reads, writes, **kw):
        self.op("gpsimd", method, reads, writes, **kw)

    def act(self, reads, writes, out, in_, func, **kw):
        self.A("activation", reads, writes, out=out, in_=in_, func=func, **kw)

    def mm(self, reads, writes, out, lhsT, rhs, start, stop, **kw):
        self.T("matmul", reads, writes, out=out, lhsT=lhsT, rhs=rhs, start=start, stop=stop, **kw)

    def sb(self, name, shape, dt):
        return self.es.enter_context(self.nc.sbuf_tensor(name, shape, dt))

    def newlane(self):
        return Lane(self.sempool, 16)

    def build(self):
        nc = self.nc
        es = self.es
        NP, NT, NS, NPAST = self.n_pseq, self.n_ptiles, self.n_sseq, self.n_past
        SP = NT * 128
        SPAST = NPAST * 128

        def din(name, shape, dt=F32):
            return nc.dram_tensor(name, list(shape), dt, kind="ExternalInput").ap()

        def dout(name, shape, dt=F32):
            return nc.dram_tensor(name, list(shape), dt, kind="ExternalOutput").ap()

        self.xp = din("xp", [NP, SP, D])
        self.xs = din("xs", [NS, 16, D])
        self.stc = din("stc", [2, NS, 2, D])
        self.ck = din("ck", [NS, SPAST, D])
        self.cv = din("cv", [NS, SPAST, D])
        self.w_in = din("w_in", [2, D, 3 * D])
        self.wdw_d = din("wdw", [128, 48])
        self.w_out = din("w_out", [2, D, D])
        self.sbq = din("sbq", [2, D, D])
        self.sbo = din("sbo", [2, D, D])
        self.wk = din("wk", [D, D])
        self.wv = din("wv", [D, D])
        self.pwq = din("pwq", [4, D, 2 * D])
        self.skT_d = din("skT", [4, 128, 2048])
        self.uv = din("uv", [4, 16384, 2 * D])
        self.lng = din("lng", [8, D])
        self.lnb = din("lnb", [8, D])
        self.yp = dout("yp", [NP, SP, D])
        self.ys = dout("ys", [NS, 16, D])
        self.ncp = dout("ncp", [2, NP, 2, D])
        self.nkp = dout("nkp", [NP, SP, D])
        self.nvp = dout("nvp", [NP, SP, D])
        self.ncs = dout("ncs", [2, NS, 2, D])
        self.nks = dout("nks", [NS, 16, D])
        self.nvs = dout("nvs", [NS, 16, D])

        self.chunks = []
        self.cid = {}

        def addchunks(key, ap2d, ncols):
            for j in range(ncols // 512):
                self.cid[(key, j)] = len(self.chunks)
                self.chunks.append(("w", ap2d[:, j * 512:(j + 1) * 512]))

        for l in range(2):
            addchunks(("w_in", l), self.w_in[l], 3 * D)
            addchunks(("w_out", l), self.w_out[l], D)
            addchunks(("sbq", l), self.sbq[l], D)
            addchunks(("sbo", l), self.sbo[l], D)
        addchunks(("wk", 0), self.wk, D)
        addchunks(("wv", 0), self.wv, D)
        for l in range(4):
            addchunks(("pwq", l), self.pwq[l], 2 * D)
        for l in range(4):
            self.cid[(("sk", l), 0)] = len(self.chunks)
            self.chunks.append(("sk", self.skT_d[l]))
        NCH = len(self.chunks)
        self.wbf = nc.dram_tensor("wbf", [NCH, 128, 4096], BF16, kind="Internal").ap()
        self.wbf_buf = [Buf(f"wbf{i}") for i in range(NCH)]
        self.uvb = nc.dram_tensor("uvb", [4 * 16384, 2 * D], BF16, kind="Internal").ap()

        self.sempool = [es.enter_context(nc.semaphore(f"sm{i}")) for i in range(96)]
        self.eng = {n: Eng(n, self.sempool) for n in ["tensor", "vector", "scalar", "gpsimd", "sync"]}

        sb = self.sb
        self.wsl = sb("wsl", [128, NWS, 8, 512], BF16)
        self.wsl_buf = [Buf(f"wsl{i}") for i in range(NWS)]
        self.wsl_lane = [self.newlane() for _ in range(NWS)]
        self.KT = sb("KT", [128, 8, 16 * 128], BF16)
        self.Vr = sb("Vr", [128, 16, D], BF16)
        self.KT_buf = [Buf(f"KT{i}") for i in range(16)]
        self.V_buf = [Buf(f"V{i}") for i in range(16)]
        self.gsl = sb("gsl", [128, NGS, 2 * D], BF16)
        self.gsl_buf = [Buf(f"gsl{i}") for i in range(NGS)]
        self.gsl_lane = [self.newlane() for _ in range(NGS)]
        self.hb = [sb(f"hb{i}", [128, D], F32) for i in range(2)]
        self.hb_buf = [Buf(f"hb{i}") for i in range(2)]
        self.hb_lane_in = [self.newlane() for _ in range(2)]
        self.hb_lane_out = [self.newlane() for _ in range(2)]
        self.pre = sb("pre", [128, D], F32)
        self.pre_buf = Buf("pre")
        self.pre_lane = self.newlane()
        self.h16 = sb("h16", [128, D], BF16)
        self.h16_buf = Buf("h16")
        self.hTs = [sb(f"hT{i}", [128, 8, 128], BF16) for i in range(2)]
        self.hTs_buf = [Buf(f"hT{i}") for i in range(2)]
        self.hT, self.hT_buf = self.hTs[0], self.hTs_buf[0]
        self.junk = sb("junk", [128, D], BF16)
        self.junk_buf = Buf("junk")
        self.big1 = sb("big1", [128, 2048], F32)
        self.big1_buf = Buf("big1")
        self.big2 = sb("big2", [128, 2048], F32)
        self.big2_buf = Buf("big2")
        self.big2_lane = self.newlane()
        self.scr = sb("scr", [128, 8192], BF16)
        self.scr_buf = Buf("scr")
        self.ub = [sb(f"ub{l}", [128, 8, 130], F32) for l in range(2)]
        self.ub_buf = [Buf(f"ub{l}") for l in range(2)]
        self.bacc = sb("bacc", [128, 8, 128], BF16)
        self.bacc_buf = Buf("bacc")
        self.lnp = sb("lnp", [128, 2, D], F32)
        self.lnp_buf = Buf("lnp")
        self.lnp_lane = self.newlane()
        self.qT = sb("qT", [128, 16, 128], BF16)
        self.qT_buf = Buf("qT")
        self.QT = self.qT[:, 0:8, :]
        self.QT_buf = self.qT_buf
        self.oTb = self.bacc
        self.oTb_buf = self.bacc_buf
        self.Cs = sb("Cs", [16, 512], BF16)
        self.Cs_buf = Buf("Cs")
        self.dg = [sb(f"dg{i}", [128, GRP, 128], BF16) for i in range(2)]
        self.dg_buf = [Buf(f"dg{i}") for i in range(2)]
        self.abt = [sb(f"ab{i}", [128, 4, 128], BF16) for i in range(2)]
        self.ab = [self.abt[i][:].rearrange("p a b -> p (a b)") for i in range(2)]
        self.ab_buf = [Buf(f"ab{i}") for i in range(2)]
        self.v16 = sb("v16", [128, 16, 16], F32)
        self.i16 = sb("i16", [128, 16, 16], U32)
        self.i16f = sb("i16f", [128, 16, 16], F32)
        self.tmp1 = self.big2[:, 0:128]
        self.tmp2 = self.big1[:, 0:256]
        self.best = sb("best", [128, 8, 16], F32)
        self.pos = sb("pos", [128, 8, 16], U32)
        self.posa = self.big2[:, 0:128].bitcast(I32).rearrange("p (a b) -> p a b", a=8)
        self.posb = self.big2[:, 128:256].bitcast(I32).rearrange("p (a b) -> p a b", a=8)
        self.af = self.big2[:, 256:384].rearrange("p (a b) -> p a b", a=8)
        self.bf = self.big2[:, 384:512].rearrange("p (a b) -> p a b", a=8)
        self.e0 = self.big2[:, 512:640].rearrange("p (a b) -> p a b", a=8)
        self.e1 = self.big2[:, 640:768].rearrange("p (a b) -> p a b", a=8)
        self.ef = sb("ef", [128, 128], F32)
        self.eidxs = [sb(f"eidx{i}", [128, 128], I32) for i in range(2)]
        self.eidx = self.eidxs[0]
        self.gws = [sb(f"gw{i}", [128, 8, 16], F32) for i in range(2)]
        self.gw = self.gws[0]
        self.gex = sb("gex", [128, 8, 16], F32)
        self.gsum = sb("gsum", [128, 8], F32)
        self.actt = sb("actt", [128, 128], F32)[:]
        self.gel = sb("gel", [128, 128], F32)[:]
        self.w4 = sb("w4", [128, 128], F32)[:]
        self.small_buf = {n: Buf(n) for n in ["v16", "i16", "i16f", "tmp1", "tmp2", "best", "pos", "posa", "posb",
                                                "af", "bf", "e0", "e1", "ef", "eidx", "gw", "gex", "gsum",
                                                "actt", "gel", "w4", "lnst", "cst", "cin"]}
        self.small_buf["tmp1"] = self.big2_buf
        self.small_buf["tmp2"] = self.big1_buf
        self.small_buf["cst"] = self.big2_buf
        self.small_buf["cin"] = self.big2_buf
        self.eidx_bufs = [Buf("eidx0"), Buf("eidx1")]
        self.gw_bufs = [Buf("gw0"), Buf("gw1")]
        self.lnst = sb("lnst", [128, 16], F32)
        self.lnmv = sb("lnmv", [128, 4], F32)
        self.cst = self.big2[0:2, 0:D]
        self.cst_lane = self.newlane()
        self.cin = self.big2[0:2, D:2 * D]
        self.cin_lane = self.newlane()
        self.wdw = sb("wdwS", [128, 48], F32)
        self.wdw_buf = Buf("wdw")
        self.wdw_lane = self.newlane()
        self.iot = sb("iot", [128, 128], I32)
        self.identf = sb("identf", [128, 128], F32)
        self.identb = sb("identb", [128, 128], BF16)
        self.trineg = sb("trineg", [128, 128], BF16)
        self.mask1 = sb("mask1", [128, 128], BF16)
        self.indall = sb("indall", [128, 32], BF16)
        self.seli = sb("seli", [16, 128], I32)
        self.selneg = sb("selneg", [16, 16, 128], BF16)
        self.iot16i = sb("iot16i", [128, 16], I32)
        self.iot16 = sb("iot16", [128, 16], F32)
        self.const_buf = Buf("const")
        self.P = [self.es.enter_context(nc.psum_tensor(f"ps{i}", [128, 1024], F32)) for i in range(4)]
        self.pbuf = [Buf(f"bank{i}") for i in range(8)]

        self.emit_consts()
        self.emit_phase0()
        self.emit_phase0b()
        self.emit_tiles()
        self.emit_final()
        self.replay()

    def bank(self, i):
        return self.P[i // 2][:, (i % 2) * 512:(i % 2 + 1) * 512]

    def emit_consts(self):
        cb = [self.const_buf]
        self.G("iota", [], cb, out=self.iot[:], pattern=[[1, 128]], base=0, channel_multiplier=-1)
        self.G("iota", [], cb, out=self.iot16i[:], pattern=[[1, 16]], base=0, channel_multiplier=0)
        V = self.V
        V("tensor_scalar", cb, cb, out=self.identf[:], in0=self.iot[:], scalar1=0.0, scalar2=None, op0=ALU.is_equal)
        V("tensor_scalar", cb, cb, out=self.identb[:], in0=self.iot[:], scalar1=0.0, scalar2=None, op0=ALU.is_equal)
        V("tensor_scalar", cb, cb, out=self.trineg[:], in0=self.iot[:], scalar1=0.0, scalar2=-1.0,
          op0=ALU.is_le, op1=ALU.mult)
        V("tensor_scalar", cb, cb, out=self.mask1[:], in0=self.iot[:], scalar1=0.0, scalar2=None, op0=ALU.is_gt)
        V("memset", [], cb, ap=self.indall[:], constant=0.0)
        V("memset", cb, cb, ap=self.indall[:, 15:16], constant=1.0)
        for kbi in range(16):
            self.G("iota", cb, cb, out=self.seli[:], pattern=[[0, 128]], base=-kbi, channel_multiplier=1)
            V("tensor_scalar", cb, cb, out=self.selneg[:, kbi, :], in0=self.seli[:], scalar1=0.0, scalar2=-1.0,
              op0=ALU.is_gt, op1=ALU.mult)
        V("tensor_copy", cb, cb, out=self.iot16[:], in_=self.iot16i[:])
        self.dma("sync", self.wdw_lane, [], [self.wdw_buf], out=self.wdw[:], in_=self.wdw_d[:, :])

    def emit_phase0(self):
        for ci, (kind, src) in enumerate(self.chunks):
            s = ci % NWS
            if kind == "w":
                srcap = src.rearrange("(dc p) f -> p dc f", p=128)
                dst = self.wsl[:, s, :, :]
                dram = self.wbf[ci].rearrange("p (dc f) -> p dc f", dc=8)
            else:
                srcap = src
                dst = self.wsl[:, s, 0:4, :].rearrange("p a b -> p (a b)")
                dram = self.wbf[ci][:, 0:2048]
            self.dma("gpsimd", self.wsl_lane[s], [], [self.wsl_buf[s]], out=dst, in_=srcap)
            self.dma("sync", self.wsl_lane[s], [self.wsl_buf[s]], [self.wbf_buf[ci]], out=dram, in_=dst)

    def emit_phase0b(self):
        uvf = self.uv.rearrange("l e d -> (l e) d")
        R = NGS // 2
        rows = 128 * R
        nchunk = (4 * 16384) // rows
        evs = []
        for c in range(nchunk):
            hf = c % 2
            bufs = self.gsl_buf[hf * R:(hf + 1) * R]
            st = self.gsl[:, hf * R:(hf + 1) * R, :]
            src = uvf[c * rows:(c + 1) * rows, :].rearrange("(p r) d -> p r d", r=R)
            dst = self.uvb[c * rows:(c + 1) * rows, :].rearrange("(p r) d -> p r d", r=R)
            self.dma("gpsimd", self.gsl_lane[hf * R], [], bufs, out=st, in_=src)
            evs.append(self.dma("sync", self.gsl_lane[hf * R + 1], bufs, [], out=dst, in_=st))
        for ev in evs[-2:]:
            self._wait_for(self.eng["gpsimd"], ev, "raw")

    def tile_chunk_order(self):
        o = []
        for l in range(2):
            if l >= self.n_layers:
                break
            o += [self.cid[(("w_in", l), j)] for j in range(6)]
            o += [self.cid[(("w_out", l), j)] for j in range(2)]
            o += [self.cid[(("pwq", l), j)] for j in range(4)]
            o += [self.cid[(("sk", l), 0)]]
        if self.n_layers > 2:
            o += [self.cid[(("wk", 0), j)] for j in range(2)]
            o += [self.cid[(("wv", 0), j)] for j in range(2)]
        for l in range(2, 4):
            if l >= self.n_layers:
                break
            o += [self.cid[(("sbq", l - 2), j)] for j in range(2)]
            o += [self.cid[(("sbo", l - 2), j)] for j in range(2)]
            o += [self.cid[(("pwq", l), j)] for j in range(4)]
            o += [self.cid[(("sk", l), 0)]]
        return o

    def wnext(self, expect):
        if self.recording:
            self.wlist.append(self.cid[expect])
            return 0
        while self.w_issued < min(len(self.wlist), self.w_i + NWS):
            ci = self.wlist[self.w_issued]
            s = self.w_issued % NWS
            kind = self.chunks[ci][0]
            if kind == "w":
                dst = self.wsl[:, s, :, :]
                dram = self.wbf[ci].rearrange("p (dc f) -> p dc f", dc=8)
            else:
                dst = self.wsl[:, s, 0:4, :].rearrange("p a b -> p (a b)")
                dram = self.wbf[ci][:, 0:2048]
            self.dma("sync", self.wsl_lane[s], [self.wbf_buf[ci]], [self.wsl_buf[s]], out=dst, in_=dram)
            self.w_issued += 1
        ci = self.wlist[self.w_i]
        assert ci == self.cid[expect], (ci, expect)
        s = self.w_i % NWS
        self.w_i += 1
        return s

    def emit_tiles(self):
        seqs = []
        for b in range(self.n_pseq):
            seqs.append(dict(kind="p", b=b, nt=self.n_ptiles, nvalid=128, kb0=0))
        for b in range(self.n_sseq):
            seqs.append(dict(kind="s", b=b, nt=1, nvalid=16, kb0=self.n_past))
        tiles = [(seq, ti) for seq in seqs for ti in range(seq["nt"])]
        self.stores = []
        self.recording = True
        self.wlist = []
        self.units = {}
        self.schedule(tiles)
        self.recording = False
        self.stores = []
        self.w_i = 0
        self.w_issued = 0
        self.schedule(tiles)

    def schedule(self, tiles):
        nph = 2 * self.n_layers
        pending = list(enumerate(tiles))

        def new_tile(other):
            if not pending:
                return None
            tid, (seq, ti) = pending[0]
            if other is not None and other["seq"] is not seq:
                return None
            pending.pop(0)
            return dict(tid=tid, seq=seq, gen=self.tile_gen(seq, ti, tid), ph=0)

        def run_pair(g, d):
            items = [x for x in (g, d) if x is not None]
            done = {id(x): 0 for x in items}
            fin = {id(x): False for x in items}
            tot = {id(x): max(1, self.units.get((x["tid"], x["ph"]), 1)) for x in items}
            while not all(fin.values()):
                cand = [x for x in items if not fin[id(x)]]
                x = min(cand, key=lambda y: done[id(y)] / tot[id(y)])
                r = next(x["gen"])
                done[id(x)] += 1
                if r == "end":
                    fin[id(x)] = True
                    if self.recording:
                        self.units[(x["tid"], x["ph"])] = done[id(x)]
                    x["ph"] += 1

        d = new_tile(None)
        g = None
        while d is not None or g is not None:
            run_pair(g, d)
            new_g = d
            if g is not None and g["ph"] < nph:
                new_d = g
            else:
                new_d = new_tile(new_g)
            g, d = new_g, new_d

    def tile_gen(self, seq, ti, tid):
        hi = tid % 2
        b = seq["b"]
        h = self.hb[hi]
        hbuf = self.hb_buf[hi]
        if seq["kind"] == "s" and self.n_layers > 2:
            yield from self.load_cache(seq)
        if seq["kind"] == "p":
            self.dma("sync", self.hb_lane_in[hi], [], [hbuf], out=h[:], in_=self.xp[b, ti * 128:(ti + 1) * 128, :])
        else:
            self.V("memset", [], [hbuf], ap=h[:], constant=0.0)
            self.dma("sync", self.hb_lane_in[hi], [], [hbuf], out=h[0:16, :], in_=self.xs[b, :, :])
        self.make_hT(hi)
        yield "u"
        kb = seq["kb0"] + ti
        for l in range(self.n_layers):
            if l < 2:
                yield from self.conv_layer(l, seq, ti, hi)
            else:
                if l == 2:
                    self.kv(seq, ti, hi, kb)
                    yield "u"
                yield from self.attn_layer(l - 2, seq, ti, hi, kb)
            self.layernorm(2 * l, hi)
            self.make_hT(hi)
            yield "u"
            yield from self.peer_front(l, hi)
            yield "end"
            yield from self.peer_gather(l, hi)
            self.layernorm(2 * l + 1, hi)
            if l < self.n_layers - 1:
                self.make_hT(hi)
            else:
                if seq["kind"] == "p":
                    ev = self.dma("sync", self.hb_lane_out[hi], [hbuf], [], out=self.yp[b, ti * 128:(ti + 1) * 128, :],
                                  in_=h[:])
                else:
                    ev = self.dma("sync", self.hb_lane_out[hi], [hbuf], [], out=self.ys[b, :, :], in_=h[0:16, :])
                self.stores.append(ev)
            yield "end"

    def load_cache(self, seq):
        b = seq["b"]
        for kb in range(self.n_past):
            self.dma("sync", self.big2_lane, [], [self.big2_buf], out=self.big2[:, 0:D],
                     in_=self.ck[b, kb * 128:(kb + 1) * 128, :])
            self.act([self.big2_buf], [self.h16_buf], out=self.h16[:], in_=self.big2[:, 0:D], func=AF.Copy)
            self.k_transposes(kb)
            self.dma("sync", self.big2_lane, [], [self.big2_buf], out=self.big2[:, 0:D],
                     in_=self.cv[b, kb * 128:(kb + 1) * 128, :])
            self.act([self.big2_buf], [self.V_buf[kb]], out=self.Vr[:, kb, :], in_=self.big2[:, 0:D], func=AF.Copy)
            yield "u"

    def k_transposes(self, kb):
        bi = 0
        pb = self.bank(bi).bitcast(BF16)
        for pair in range(8):
            self.T("transpose", [self.h16_buf, self.const_buf], [self.pbuf[bi]],
                   out=pb[:, pair * 128:(pair + 1) * 128], in_=self.h16[:, pair * 128:(pair + 1) * 128],
                   identity=self.identb[:])
        self.act([self.pbuf[bi]], [self.KT_buf[kb]], out=self.KT[:, :, kb * 128:(kb + 1) * 128],
                 in_=pb.rearrange("p (a b) -> p a b", a=8), func=AF.Copy)

    def make_hT(self, hi):
        self.act([self.hb_buf[hi]], [self.h16_buf], out=self.h16[:], in_=self.hb[hi][:], func=AF.Copy)
        bi = 7
        pb = self.bank(bi).bitcast(BF16)
        for c in range(8):
            self.T("transpose", [self.h16_buf, self.const_buf], [self.pbuf[bi]],
                   out=pb[:, c * 128:(c + 1) * 128], in_=self.h16[:, c * 128:(c + 1) * 128],
                   identity=self.identb[:])
        self.V("tensor_copy", [self.pbuf[bi]], [self.hTs_buf[hi]], out=self.hTs[hi][:].rearrange("p a b -> p (a b)"),
               in_=pb)

    def layernorm(self, idx, hi):
        self.dma("sync", self.lnp_lane, [], [self.lnp_buf], out=self.lnp[:, 0:1, :],
                 in_=self.lng[idx:idx + 1, :].partition_broadcast(128))
        self.dma("sync", self.lnp_lane, [], [self.lnp_buf], out=self.lnp[:, 1:2, :],
                 in_=self.lnb[idx:idx + 1, :].partition_broadcast(128))
        sbuf = self.small_buf["lnst"]
        V = self.V
        V("bn_stats", [self.pre_buf], [sbuf], out=self.lnst[:, 0:6], in_=self.pre[:, 0:512])
        V("bn_stats", [self.pre_buf], [sbuf], out=self.lnst[:, 6:12], in_=self.pre[:, 512:1024])
        V("bn_aggr", [sbuf], [sbuf], out=self.lnmv[:, 0:2], in_=self.lnst[:, 0:12])
        V("tensor_scalar", [sbuf], [sbuf], out=self.lnmv[:, 2:3], in0=self.lnmv[:, 1:2], scalar1=EPS, scalar2=None,
          op0=ALU.add)
        self.act([sbuf], [sbuf], out=self.lnmv[:, 2:3], in_=self.lnmv[:, 2:3], func=AF.Ln)
        self.act([sbuf], [sbuf], out=self.lnmv[:, 3:4], in_=self.lnmv[:, 2:3], func=AF.Exp, scale=-0.5)
        V("tensor_scalar", [self.pre_buf, sbuf], [self.pre_buf], out=self.pre[:], in0=self.pre[:],
          scalar1=self.lnmv[:, 0:1], scalar2=self.lnmv[:, 3:4], op0=ALU.subtract, op1=ALU.mult)
        V("tensor_tensor", [self.pre_buf, self.lnp_buf], [self.pre_buf], out=self.pre[:], in0=self.pre[:],
          in1=self.lnp[:, 0, :], op=ALU.mult)
        V("tensor_tensor", [self.pre_buf, self.lnp_buf], [self.hb_buf[hi]], out=self.hb[hi][:], in0=self.pre[:],
          in1=self.lnp[:, 1, :], op=ALU.add)

    def residual(self, hi, pbanks):
        pidx = pbanks
        self.V("scalar_tensor_tensor", [self.hb_buf[hi], self.pbuf[2 * pidx], self.pbuf[2 * pidx + 1]],
               [self.pre_buf], out=self.pre[:], in0=self.hb[hi][:], scalar=ALPHA, in1=self.P[pidx][:, :],
               op0=ALU.mult, op1=ALU.add)

    def tok_major_mm(self, key, lhs, lhs_buf, pidx):
        for half in range(2):
            s = self.wnext((key, half))
            bi = 2 * pidx + half
            for dc in range(8):
                self.mm([lhs_buf, self.wsl_buf[s]], [self.pbuf[bi]], out=self.bank(bi), lhsT=lhs[:, dc, :],
                        rhs=self.wsl[:, s, dc, :], start=(dc == 0), stop=(dc == 7))

    def feat_major_chunk(self, key, j, bi, hi):
        s = self.wnext((key, j))
        for fc in range(4):
            for dc in range(8):
                self.mm([self.hTs_buf[hi], self.wsl_buf[s]], [self.pbuf[bi]],
                        out=self.bank(bi)[:, fc * 128:(fc + 1) * 128], lhsT=self.wsl[:, s, dc, fc * 128:(fc + 1) * 128],
                        rhs=self.hTs[hi][:, dc, :], start=(dc == 0), stop=(dc == 7))

    def conv_layer(self, l, seq, ti, hi):
        V = self.V
        ub, ubb = self.ub[l], self.ub_buf[l]
        gates = self.scr[:].bitcast(F32)[:, 0:3072].rearrange("p (a b) -> p a b", a=24)
        if ti == 0:
            if seq["kind"] == "p":
                V("memset", [], [ubb], ap=ub[:, :, 0:2], constant=0.0)
            else:
                cb = self.small_buf["cin"]
                self.dma("sync", self.cin_lane, [], [cb], out=self.cin, in_=self.stc[l, seq["b"], :, :])
                bi = 0
                for c in range(8):
                    self.T("transpose", [cb, self.const_buf], [self.pbuf[bi]], out=self.bank(bi)[:, 2 * c:2 * c + 2],
                           in_=self.big2[0:2, D + c * 128:D + (c + 1) * 128], identity=self.identf[0:2, 0:2])
                V("tensor_copy", [self.pbuf[bi]], [ubb], out=ub[:, :, 0:2],
                  in_=self.bank(bi)[:, 0:16].rearrange("p (a b) -> p a b", a=8))
        for j in range(6):
            bi = j % 2
            self.feat_major_chunk(("w_in", l), j, bi, hi)
            self.act([self.pbuf[bi]], [self.scr_buf], out=gates[:, 4 * j:4 * j + 4, :],
                     in_=self.bank(bi).rearrange("p (a b) -> p a b", a=4), func=AF.Copy)
            yield "u"
        V("tensor_tensor", [self.scr_buf], [ubb], out=ub[:, :, 2:130], in0=gates[:, 8:16, :], in1=gates[:, 16:24, :],
          op=ALU.mult)
        acc = self.big1[:, 0:1024].rearrange("p (a b) -> p a b", a=8)
        ab = self.big1_buf
        for c in range(8):
            col = lambda w: self.wdw[:, (l * 3 + w) * 8 + c:(l * 3 + w) * 8 + c + 1]
            V("tensor_scalar", [ubb, self.wdw_buf], [ab], out=acc[:, c, :], in0=ub[:, c, 0:128], scalar1=col(0),
              scalar2=None, op0=ALU.mult)
            V("scalar_tensor_tensor", [ubb, ab, self.wdw_buf], [ab], out=acc[:, c, :], in0=ub[:, c, 1:129],
              scalar=col(1), in1=acc[:, c, :], op0=ALU.mult, op1=ALU.add)
            V("scalar_tensor_tensor", [ubb, ab, self.wdw_buf], [ab], out=acc[:, c, :], in0=ub[:, c, 2:130],
              scalar=col(2), in1=acc[:, c, :], op0=ALU.mult, op1=ALU.add)
        V("tensor_tensor", [self.scr_buf, ab], [self.bacc_buf], out=self.bacc[:], in0=gates[:, 0:8, :], in1=acc,
          op=ALU.mult)
        yield "u"
        nv = seq["nvalid"]
        if ti == seq["nt"] - 1:
            cb = self.small_buf["cst"]
            for c in range(8):
                bi = c // 4
                self.T("transpose", [ubb, self.const_buf], [self.pbuf[bi]],
                       out=self.P[0][0:2, c * 128:(c + 1) * 128], in_=ub[:, c, nv:nv + 2], identity=self.identf[:])
            self.act([self.pbuf[0], self.pbuf[1]], [cb], out=self.cst, in_=self.P[0][0:2, :], func=AF.Copy)
            dst = (self.ncp if seq["kind"] == "p" else self.ncs)[l, seq["b"], :, :]
            ev = self.dma("sync", self.cst_lane, [cb], [], out=dst, in_=self.cst)
            self.stores.append(ev)
        else:
            V("tensor_copy", [ubb], [ubb], out=ub[:, :, 0:2], in_=ub[:, :, 128:130])
        self.tok_major_mm(("w_out", l), self.bacc, self.bacc_buf, 0)
        self.residual(hi, 0)

    def peer_front(self, l, hi):
        V, A = self.V, self.A
        sm = dict(self.small_buf)
        sm["eidx"] = self.eidx_bufs[hi]
        sm["gw"] = self.gw_bufs[hi]
        eidx_t, gw_t = self.eidxs[hi], self.gws[hi]
        h, hbuf = self.hb[hi], self.hb_buf[hi]
        for j in range(4):
            bi = j % 2
            self.feat_major_chunk(("pwq", l), j, bi, hi)
            self.act([self.pbuf[bi]], [self.qT_buf], out=self.qT[:, 4 * j:4 * j + 4, :],
                     in_=self.bank(bi).rearrange("p (a b) -> p a b", a=4), func=AF.Copy)
            yield "u"
        s = self.wnext((("sk", l), 0))
        skT = self.wsl[:, s, 0:4, :].rearrange("p a b -> p (a b)")
        scs = self.big1[:].rearrange("p (a b) -> p a b", a=16)
        for g in range(16):
            bi = 4 + g // 4
            self.mm([self.qT_buf, self.wsl_buf[s]], [self.pbuf[bi]], out=self.bank(bi)[:, (g % 4) * 128:(g % 4 + 1) * 128],
                    lhsT=self.qT[:, g, :], rhs=skT[:, g * 128:(g + 1) * 128], start=True, stop=True)
        for q in range(4):
            self.act([self.pbuf[4 + q]], [self.big1_buf], out=self.big1[:, q * 512:(q + 1) * 512], in_=self.bank(4 + q),
                     func=AF.Copy)
        b1 = self.big1_buf
        for g in range(16):
            V("max", [b1], [sm["v16"]], out=self.v16[:, g, 0:8], in_=scs[:, g, :])
            V("max_index", [b1, sm["v16"]], [sm["i16"]], out=self.i16[:, g, 0:8], in_max=self.v16[:, g, 0:8],
              in_values=scs[:, g, :])
            V("match_replace", [b1, sm["v16"]], [sm["tmp1"]], out=self.tmp1, in_to_replace=self.v16[:, g, 0:8],
              in_values=scs[:, g, :], imm_value=NEG)
            V("max", [sm["tmp1"]], [sm["v16"]], out=self.v16[:, g, 8:16], in_=self.tmp1)
            V("max_index", [sm["tmp1"], sm["v16"]], [sm["i16"]], out=self.i16[:, g, 8:16], in_max=self.v16[:, g, 8:16],
              in_values=self.tmp1)
            if g % 2 == 1:
                yield "u"
        V("tensor_copy", [sm["i16"]], [sm["i16f"]], out=self.i16f[:], in_=self.i16[:])
        cand = self.big2[:].rearrange("p (h a b) -> p h a b", h=8, a=16)
        v16h = self.v16[:].rearrange("p (h t) k -> p h t k", t=2)
        i16h = self.i16f[:].rearrange("p (h t) k -> p h t k", t=2)
        V("tensor_tensor", [sm["v16"]], [self.big2_buf], out=cand,
          in0=v16h[:, :, 0, :].unsqueeze(3).to_broadcast([128, 8, 16, 16]),
          in1=v16h[:, :, 1, :].unsqueeze(2).to_broadcast([128, 8, 16, 16]), op=ALU.add)
        b2 = self.big2_buf
        for hh in range(8):
            cf = self.big2[:, hh * 256:(hh + 1) * 256]
            V("max", [b2], [sm["best"]], out=self.best[:, hh, 0:8], in_=cf)
            V("max_index", [b2, sm["best"]], [sm["pos"]], out=self.pos[:, hh, 0:8], in_max=self.best[:, hh, 0:8],
              in_values=cf)
            V("match_replace", [b2, sm["best"]], [sm["tmp2"]], out=self.tmp2, in_to_replace=self.best[:, hh, 0:8],
              in_values=cf, imm_value=NEG)
            V("max", [sm["tmp2"]], [sm["best"]], out=self.best[:, hh, 8:16], in_=self.tmp2)
            V("max_index", [sm["tmp2"], sm["best"]], [sm["pos"]], out=self.pos[:, hh, 8:16],
              in_max=self.best[:, hh, 8:16], in_values=self.tmp2)
            if hh % 2 == 1:
                yield "u"
        posi = self.pos[:].bitcast(I32)
        V("tensor_single_scalar", [sm["pos"]], [sm["posa"]], out=self.posa[:], in_=posi, scalar=4,
          op=ALU.arith_shift_right)
        V("tensor_single_scalar", [sm["pos"]], [sm["posb"]], out=self.posb[:], in_=posi, scalar=15,
          op=ALU.bitwise_and)
        V("tensor_copy", [sm["posa"]], [sm["af"]], out=self.af[:], in_=self.posa[:])
        V("tensor_copy", [sm["posb"]], [sm["bf"]], out=self.bf[:], in_=self.posb[:])
        eq = self.big1[:].rearrange("p (h a b) -> p h a b", h=8, a=16)
        io = self.iot16[:].unsqueeze(1).unsqueeze(1).to_broadcast([128, 8, 16, 16])
        for (src, srcb, t, dst, dstb) in ((self.af, sm["af"], 0, self.e0, sm["e0"]), (self.bf, sm["bf"], 1, self.e1, sm["e1"])):
            V("tensor_tensor", [srcb, self.const_buf], [b1], out=eq,
              in0=src[:].unsqueeze(3).to_broadcast([128, 8, 16, 16]), in1=io, op=ALU.is_equal)
            V("tensor_tensor", [b1, sm["i16f"]], [b1], out=eq, in0=eq,
              in1=i16h[:, :, t, :].unsqueeze(2).to_broadcast([128, 8, 16, 16]), op=ALU.mult)
            V("tensor_reduce", [b1], [dstb], out=dst[:], in_=eq, axis=AX.X, op=ALU.add)
            yield "u"
        V("scalar_tensor_tensor", [sm["e0"], sm["e1"]], [sm["ef"]], out=self.ef[:],
          in0=self.e0[:].rearrange("p a b -> p (a b)"), scalar=128.0, in1=self.e1[:].rearrange("p a b -> p (a b)"),
          op0=ALU.mult, op1=ALU.add)
        if l > 0:
            V("tensor_scalar", [sm["ef"]], [sm["ef"]], out=self.ef[:], in0=self.ef[:], scalar1=float(l * 16384),
              scalar2=None, op0=ALU.add)
        V("tensor_copy", [sm["ef"]], [sm["eidx"]], out=eidx_t[:], in_=self.ef[:])
        V("tensor_tensor", [sm["best"]], [sm["gex"]], out=self.gex[:], in0=self.best[:],
          in1=self.best[:, :, 0:1].to_broadcast([128, 8, 16]), op=ALU.subtract)
        self.act([sm["gex"]], [sm["gex"]], out=self.gex[:], in_=self.gex[:], func=AF.Exp)
        V("tensor_reduce", [sm["gex"]], [sm["gsum"]], out=self.gsum[:], in_=self.gex[:], axis=AX.X, op=ALU.add)
        V("reciprocal", [sm["gsum"]], [sm["gsum"]], out=self.gsum[:], in_=self.gsum[:])
        V("tensor_tensor", [sm["gex"], sm["gsum"]], [sm["gw"]], out=gw_t[:], in0=self.gex[:],
          in1=self.gsum[:].unsqueeze(2).to_broadcast([128, 8, 16]), op=ALU.mult)
        yield "u"

    def peer_gather(self, l, hi):
        V, A = self.V, self.A
        sm = dict(self.small_buf)
        sm["eidx"] = self.eidx_bufs[hi]
        sm["gw"] = self.gw_bufs[hi]
        eidx_t, gw_t = self.eidxs[hi], self.gws[hi]
        h, hbuf = self.hb[hi], self.hb_buf[hi]
        gwf = gw_t[:].rearrange("p a b -> p (a b)")
        uvt = self.uvb
        for j0 in range(0, 128, GRP):
            gi = (j0 // GRP) % 2
            for jj in range(GRP):
                j = j0 + jj
                sl = j % NGS
                self.dma("gpsimd", self.gsl_lane[sl], [sm["eidx"]], [self.gsl_buf[sl]], out=self.gsl[:, sl, :],
                         in_=uvt[:, :], method="indirect_dma_start", out_offset=None,
                         in_offset=bass.IndirectOffsetOnAxis(ap=eidx_t[:, j:j + 1], axis=0))
                V("scalar_tensor_tensor", [self.gsl_buf[sl], hbuf], [self.junk_buf, sm["actt"]], out=self.junk[:],
                  in0=self.gsl[:, sl, 0:D], scalar=1.0, in1=h[:], op0=ALU.mult, op1=ALU.mult,
                  accum_out=self.actt[:, j:j + 1])
            self.act([sm["actt"]], [sm["gel"]], out=self.gel[:, j0:j0 + GRP], in_=self.actt[:, j0:j0 + GRP], func=AF.Gelu)
            V("tensor_tensor", [sm["gel"], sm["gw"]], [sm["w4"]], out=self.w4[:, j0:j0 + GRP], in0=self.gel[:, j0:j0 + GRP],
              in1=gwf[:, j0:j0 + GRP], op=ALU.mult)
            V("tensor_tensor", [sm["w4"], self.const_buf], [self.dg_buf[gi]], out=self.dg[gi][:],
              in0=self.identb[:].unsqueeze(1).to_broadcast([128, GRP, 128]),
              in1=self.w4[:, j0:j0 + GRP].unsqueeze(2).to_broadcast([128, GRP, 128]), op=ALU.mult)
            for jj in range(GRP):
                j = j0 + jj
                sl = j % NGS
                for half in range(2):
                    bi = 2 + half
                    self.mm([self.dg_buf[gi], self.gsl_buf[sl]], [self.pbuf[bi]], out=self.bank(bi),
                            lhsT=self.dg[gi][:, jj, :], rhs=self.gsl[:, sl, D + half * 512:D + (half + 1) * 512],
                            start=(j == 0), stop=(j == 127))
            yield "u"
        self.residual(hi, 1)

    def kv(self, seq, ti, hi, kb):
        nv, b = seq["nvalid"], seq["b"]
        r0 = ti * 128
        self.tok_major_mm(("wk", 0), self.hTs[hi], self.hTs_buf[hi], 0)
        self.act([self.pbuf[0], self.pbuf[1]], [self.pre_buf], out=self.pre[:], in_=self.P[0][:, :], func=AF.Copy)
        self.act([self.pbuf[0], self.pbuf[1]], [self.h16_buf], out=self.h16[:], in_=self.P[0][:, :], func=AF.Copy)
        if seq["kind"] == "p":
            ev = self.dma("sync", self.pre_lane, [self.pre_buf], [], out=self.nkp[b, r0:r0 + 128, :], in_=self.pre[:])
        else:
            ev = self.dma("sync", self.pre_lane, [self.pre_buf], [], out=self.nks[b, :, :], in_=self.pre[0:16, :])
        self.stores.append(ev)
        self.k_transposes(kb)
        self.tok_major_mm(("wv", 0), self.hTs[hi], self.hTs_buf[hi], 0)
        self.act([self.pbuf[0], self.pbuf[1]], [self.big2_buf], out=self.big2[:, 0:D], in_=self.P[0][:, :], func=AF.Copy)
        self.act([self.pbuf[0], self.pbuf[1]], [self.V_buf[kb]], out=self.Vr[:, kb, :], in_=self.P[0][:, :], func=AF.Copy)
        if seq["kind"] == "p":
            ev = self.dma("sync", self.big2_lane, [self.big2_buf], [], out=self.nvp[b, r0:r0 + 128, :], in_=self.big2[:, 0:D])
        else:
            ev = self.dma("sync", self.big2_lane, [self.big2_buf], [], out=self.nvs[b, :, :], in_=self.big2[0:16, 0:D])
        self.stores.append(ev)

    def attn_layer(self, j, seq, ti, hi, kb):
        V = self.V
        nkb = kb + 1
        V("memset", [], [self.qT_buf], ap=self.qT[:], constant=0.0)
        for jj in range(2):
            bi = jj % 2
            self.feat_major_chunk(("sbq", j), jj, bi, hi)
            for r in range(2):
                self.act([self.pbuf[bi]], [self.qT_buf], out=self.qT[r * 64:(r + 1) * 64, 8 * jj + r:8 * jj + 8:2, :],
                         in_=self.bank(bi)[r * 64:(r + 1) * 64, :].rearrange("p (a b) -> p a b", a=4), func=AF.Copy,
                         scale=0.125)
            yield "u"
        if DBG == 1:
            self.V("tensor_scalar", [self.hb_buf[hi]], [self.pre_buf], out=self.pre[:], in0=self.hb[hi][:], scalar1=ALPHA,
                   scalar2=None, op0=ALU.mult)
            for half in range(2):
                self.wnext((("sbo", j), half))
            return
        Pb = self.scr[:].rearrange("p (a b) -> p a b", a=16)
        Eb = self.big2[:, 0:512]
        oT = self.P[0]
        first_done = {}
        for g in range(4):
            heads = [4 * g + hh for hh in range(4)]

            def zmm(bi, start_first, last_stop):
                for hh, hd in enumerate(heads):
                    pair, r = hd // 2, hd % 2
                    self.mm([self.KT_buf[kbi_], self.QT_buf], [self.pbuf[bi]],
                            out=self.bank(bi)[:, hh * 128:(hh + 1) * 128],
                            lhsT=self.KT[:, pair, kbi_ * 128:(kbi_ + 1) * 128],
                            rhs=self.qT[:, hd, :],
                            start=(start_first and hh == 0), stop=(last_stop and hh == 3), skip_group_check=True)
            for kbi_ in range(nkb):
                zb = 4 + kbi_ % 2
                zmm(zb, True, True)
                self.act([self.pbuf[zb]], [self.big2_buf], out=Eb, in_=self.bank(zb), func=AF.Exp)
                self.act([self.big2_buf], [self.scr_buf], out=Pb[:, kbi_, :], in_=Eb, func=AF.Ln, bias=1.0)
                if kbi_ == kb:
                    V("tensor_tensor", [self.scr_buf, self.const_buf], [self.scr_buf], out=Pb[:, kbi_, :].rearrange("p (a b) -> p a b", a=4),
                      in0=Pb[:, kbi_, :].rearrange("p (a b) -> p a b", a=4),
                      in1=self.mask1[:].unsqueeze(1).to_broadcast([128, 4, 128]), op=ALU.mult)
                self.mm([self.scr_buf, self.const_buf], [self.pbuf[6]], out=self.bank(6)[0:16, :],
                        lhsT=self.indall[:, 15 - kbi_:31 - kbi_], rhs=Pb[:, kbi_, :], start=(kbi_ == 0),
                        stop=(kbi_ == nkb - 1))
                if kbi_ % 2 == 1:
                    yield "u"
            if DBG == 2:
                continue
            self.act([self.pbuf[6]], [self.Cs_buf], out=self.Cs[:], in_=self.bank(6)[0:16, :], func=AF.Copy)
            for kbi_ in range(nkb):
                zs = 4 + kbi_ % 2
                ai = kbi_ % 2
                self.mm([self.scr_buf, self.const_buf], [self.pbuf[zs]], out=self.bank(zs), lhsT=self.trineg[:],
                        rhs=Pb[:, kbi_, :], start=True, stop=False, skip_group_check=True)
                self.mm([self.Cs_buf, self.const_buf], [self.pbuf[zs]], out=self.bank(zs), lhsT=self.selneg[0:16, kbi_, :],
                        rhs=self.Cs[:], start=False, stop=False, skip_group_check=True)
                zmm(zs, False, True)
                self.act([self.pbuf[zs]], [self.ab_buf[ai]], out=self.ab[ai], in_=self.bank(zs), func=AF.Exp)
                if DBG == 3:
                    continue
                if kbi_ == kb:
                    V("tensor_tensor", [self.ab_buf[ai], self.const_buf], [self.ab_buf[ai]], out=self.abt[ai][:],
                      in0=self.abt[ai][:], in1=self.mask1[:].unsqueeze(1).to_broadcast([128, 4, 128]), op=ALU.mult)
                for hh, hd in enumerate(heads):
                    pair, r = hd // 2, hd % 2
                    bk = pair // 4
                    key = (bk, r)
                    st = key not in first_done
                    first_done[key] = True
                    self.mm([self.ab_buf[ai], self.V_buf[kbi_]], [self.pbuf[bk]],
                            out=oT[r * 64:(r + 1) * 64, pair * 128:(pair + 1) * 128],
                            lhsT=self.Vr[:, kbi_, hd * 64:(hd + 1) * 64], rhs=self.ab[ai][:, hh * 128:(hh + 1) * 128],
                            start=st, stop=(kbi_ == nkb - 1), skip_group_check=True)
                if kbi_ % 2 == 1:
                    yield "u"
        if DBG in (2, 3):
            self.V("tensor_scalar", [self.hb_buf[hi]], [self.pre_buf], out=self.pre[:], in0=self.hb[hi][:], scalar1=ALPHA,
                   scalar2=None, op0=ALU.mult)
            for half in range(2):
                self.wnext((("sbo", j), half))
            return
        self.act([self.pbuf[0], self.pbuf[1]], [self.oTb_buf], out=self.oTb[:].rearrange("p a b -> p (a b)"),
                 in_=oT[:, :], func=AF.Copy)
        self.tok_major_mm(("sbo", j), self.oTb, self.oTb_buf, 0)
        self.residual(hi, 0)

    def emit_final(self):
        eng = self.eng["sync"]
        for ev in self.stores:
            self._wait_for(eng, ev, "raw")

    def replay(self):
        nc = self.nc
        engs = self.eng
        with nc.Block() as block:
            def run(e, name):
                for it in engs[name].q:
                    if it[0] == "w":
                        e.wait_ge(it[1], it[2])
                    else:
                        _, method, kw, sem, inc = it
                        ins = getattr(e, method)(**kw)
                        ins.then_inc(sem, inc)

            @block.sync
            def _(e):
                run(e, "sync")

            @block.gpsimd
            def _(e):
                run(e, "gpsimd")

            @block.tensor
            def _(e):
                run(e, "tensor")

            @block.vector
            def _(e):
                run(e, "vector")

            @block.scalar
            def _(e):
                run(e, "scalar")
        self.es.close()


def make_in_maps(inputs, n_cores, n_pseq, n_sseq):
    f = lambda a: np.ascontiguousarray(np.asarray(a, dtype=np.float32))
    xp, xs = f(inputs["x_prompt"]), f(inputs["x_sample"])
    stc, ck, cv = f(inputs["state_conv"]), f(inputs["cache_k"]), f(inputs["cache_v"])
    ck = ck.reshape(ck.shape[0], ck.shape[1], -1)
    cv = cv.reshape(cv.shape[0], cv.shape[1], -1)
    wdw = f(inputs["conv_w_dw"])
    wdw_l = np.ascontiguousarray(wdw.reshape(2, 3, 8, 128).transpose(3, 0, 1, 2).reshape(128, 48))
    sk = f(inputs["peer_subkeys"])
    skT = np.ascontiguousarray(sk.transpose(0, 4, 1, 2, 3).reshape(4, 128, 2048))
    uv = np.concatenate([f(inputs["peer_u"]), f(inputs["peer_v"])], axis=-1)
    shared = dict(w_in=f(inputs["conv_w_in"]), wdw=wdw_l, w_out=f(inputs["conv_w_out"]), sbq=f(inputs["sb_w_q"]),
                  sbo=f(inputs["sb_w_o"]), wk=f(inputs["kv_w_k"]), wv=f(inputs["kv_w_v"]), pwq=f(inputs["peer_w_q"]),
                  skT=skT, uv=uv, lng=f(inputs["ln_g"]).reshape(8, D), lnb=f(inputs["ln_b"]).reshape(8, D))
    maps = []
    for c in range(n_cores):
        m = dict(shared)
        m["xp"] = np.ascontiguousarray(xp[c * n_pseq:(c + 1) * n_pseq])
        m["xs"] = np.ascontiguousarray(xs[c * n_sseq:(c + 1) * n_sseq])
        m["stc"] = np.ascontiguousarray(stc[:, c * n_sseq:(c + 1) * n_sseq])
        m["ck"] = np.ascontiguousarray(ck[c * n_sseq:(c + 1) * n_sseq])
        m["cv"] = np.ascontiguousarray(cv[c * n_sseq:(c + 1) * n_sseq])
        maps.append(m)
    return maps


def assemble(results, n_cores):
    cat = lambda k, ax: np.concatenate([np.asarray(r[k], dtype=np.float32) for r in results], axis=ax)
    yp, ys = cat("yp", 0), cat("ys", 0)
    ncp, ncs = cat("ncp", 1), cat("ncs", 1)
    nkp, nvp, nks, nvs = cat("nkp", 0), cat("nvp", 0), cat("nks", 0), cat("nvs", 0)
    r4 = lambda a: a.reshape(a.shape[0], a.shape[1], 16, 64)
    return (yp, ys, ncp, r4(nkp), r4(nvp), ncs, r4(nks), r4(nvs))


_PROG = {}


def kernel(**inputs):
    n_cores = 8
    if "full" not in _PROG:
        _PROG["full"] = Prog(4, 16, 4, 8)
    prog = _PROG["full"]
    maps = make_in_maps(inputs, n_cores, 4, 4)
    res = run_bass_kernel_spmd(prog.nc, maps, core_ids=list(range(n_cores)))
    return assemble(res.results, n_cores)
```

```python
import numpy as np
from contextlib import ExitStack
import concourse.bass as bass
import concourse.mybir as mybir
from concourse.bass_utils import run_bass_kernel_spmd

F32 = mybir.dt.float32
BF16 = mybir.dt.bfloat16
I32 = mybir.dt.int32
U32 = mybir.dt.uint32
AF = mybir.ActivationFunctionType
ALU = mybir.AluOpType
AX = mybir.AxisListType

D = 1024
ALPHA = (2.0 * 4) ** 0.25
EPS = 1e-5
NEG = -1.0e30
SEM_LIMIT = 32000
SAME_RAW = True
NGS = 8
NWS = 2
GRP = 4
DBG = 0


class Buf:
    __slots__ = ("name", "w", "r")

    def __init__(self, name):
        self.name = name
        self.w = None
        self.r = []


class Lane:
    def __init__(self, pool, inc):
        self.pool = pool
        self.inc = inc
        self.sem = pool.pop()
        self.count = 0

    def next(self):
        if self.count + self.inc > SEM_LIMIT:
            self.sem = self.pool.pop()
            self.count = 0
        self.count += self.inc
        return (self.sem, self.count)


class Eng:
    def __init__(self, name, pool):
        self.name = name
        self.lane = Lane(pool, 1)
        self.q = []
        self.waited = {}


class Prog:
    def __init__(self, n_pseq=4, n_ptiles=16, n_sseq=4, n_past=8, n_layers=4):
        self.n_pseq, self.n_ptiles, self.n_sseq, self.n_past = n_pseq, n_ptiles, n_sseq, n_past
        self.n_layers = n_layers
        self.recording = False
        self.nc = bass.Bass("TRN2", target_bir_lowering=False)
        self.es = ExitStack()
        self.build()

    def _need(self, eng, need, ev, kind):
        sem, val, src = ev
        if src == eng.name:
            if eng.name == "tensor" or kind != "raw" or not SAME_RAW:
                return
        key = id(sem)
        if eng.waited.get(key, 0) >= val:
            return
        if key not in need or need[key][1] < val:
            need[key] = (sem, val)

    def _flush(self, eng, need):
        for key, (sem, val) in need.items():
            eng.waited[key] = val
            eng.q.append(("w", sem, val))

    def _wait_for(self, eng, ev, kind):
        need = {}
        self._need(eng, need, ev, kind)
        self._flush(eng, need)

    def _deps(self, eng, reads, writes):
        need = {}
        for b in reads:
            if b.w is not None:
                self._need(eng, need, b.w, "raw")
        for b in writes:
            if b.w is not None:
                self._need(eng, need, b.w, "waw")
            for r in b.r:
                self._need(eng, need, r, "war")
        self._flush(eng, need)

    def _commit(self, ev, reads, writes):
        for b in reads:
            b.r.append(ev)
        for b in writes:
            b.w = ev
            b.r = []

    def op(self, engname, method, reads, writes, **kw):
        if self.recording:
            return
        eng = self.eng[engname]
        self._deps(eng, reads, writes)
        sem, val = eng.lane.next()
        eng.q.append(("i", method, kw, sem, 1))
        self._commit((sem, val, engname), reads, writes)

    def dma(self, qname, lane, reads, writes, out, in_, method="dma_start", **kw):
        if self.recording:
            return None
        eng = self.eng[qname]
        self._deps(eng, reads, writes)
        sem, val = lane.next()
        kw = dict(kw)
        kw["out"] = out
        kw["in_"] = in_
        eng.q.append(("i", method, kw, sem, 16))
        ev = (sem, val, "dma")
        self._commit(ev, reads, writes)
        return ev

    def V(self, method, reads, writes, **kw):
        self.op("vector", method, reads, writes, **kw)

    def A(self, method, reads, writes, **kw):
        self.op("scalar", method, reads, writes, **kw)

    def T(self, method, reads, writes, **kw):
        self.op("tensor", method, reads, writes, **kw)

    def G(self, method, reads, writes, **kw):
        self.op("gpsimd", method, reads, writes, **kw)

    def act(self, reads, writes, out, in_, func, **kw):
        self.A("activation", reads, writes, out=out, in_=in_, func=func, **kw)

    def mm(self, reads, writes, out, lhsT, rhs, start, stop, **kw):
        self.T("matmul", reads, writes, out=out, lhsT=lhsT, rhs=rhs, start=start, stop=stop, **kw)

    def sb(self, name, shape, dt):
        return self.es.enter_context(self.nc.sbuf_tensor(name, shape, dt))

    def newlane(self):
        return Lane(self.sempool, 16)

    def build(self):
        nc = self.nc
        es = self.es
        NP, NT, NS, NPAST = self.n_pseq, self.n_ptiles, self.n_sseq, self.n_past
        SP = NT * 128
        SPAST = NPAST * 128

        def din(name, shape, dt=F32):
            return nc.dram_tensor(name, list(shape), dt, kind="ExternalInput").ap()

        def dout(name, shape, dt=F32):
            return nc.dram_tensor(name, list(shape), dt, kind="ExternalOutput").ap()

        self.xp = din("xp", [NP, SP, D])
        self.xs = din("xs", [NS, 16, D])
        self.stc = din("stc", [2, NS, 2, D])
        self.ck = din("ck", [NS, SPAST, D])
        self.cv = din("cv", [NS, SPAST, D])
        self.w_in = din("w_in", [2, D, 3 * D])
        self.wdw_d = din("wdw", [128, 48])
        self.w_out = din("w_out", [2, D, D])
        self.sbq = din("sbq", [2, D, D])
        self.sbo = din("sbo", [2, D, D])
        self.wk = din("wk", [D, D])
        self.wv = din("wv", [D, D])
        self.pwq = din("pwq", [4, D, 2 * D])
        self.skT_d = din("skT", [4, 128, 2048])
        self.uv = din("uv", [4, 16384, 2 * D])
        self.lng = din("lng", [8, D])
        self.lnb = din("lnb", [8, D])
        self.yp = dout("yp", [NP, SP, D])
        self.ys = dout("ys", [NS, 16, D])
        self.ncp = dout("ncp", [2, NP, 2, D])
        self.nkp = dout("nkp", [NP, SP, D])
        self.nvp = dout("nvp", [NP, SP, D])
        self.ncs = dout("ncs", [2, NS, 2, D])
        self.nks = dout("nks", [NS, 16, D])
        self.nvs = dout("nvs", [NS, 16, D])

        self.chunks = []
        self.cid = {}

        def addchunks(key, ap2d, ncols):
            for j in range(ncols // 512):
                self.cid[(key, j)] = len(self.chunks)
                self.chunks.append(("w", ap2d[:, j * 512:(j + 1) * 512]))

        for l in range(2):
            addchunks(("w_in", l), self.w_in[l], 3 * D)
            addchunks(("w_out", l), self.w_out[l], D)
            addchunks(("sbq", l), self.sbq[l], D)
            addchunks(("sbo", l), self.sbo[l], D)
        addchunks(("wk", 0), self.wk, D)
        addchunks(("wv", 0), self.wv, D)
        for l in range(4):
            addchunks(("pwq", l), self.pwq[l], 2 * D)
        for l in range(4):
            self.cid[(("sk", l), 0)] = len(self.chunks)
            self.chunks.append(("sk", self.skT_d[l]))
        NCH = len(self.chunks)
        self.wbf = nc.dram_tensor("wbf", [NCH, 128, 4096], BF16, kind="Internal").ap()
        self.wbf_buf = [Buf(f"wbf{i}") for i in range(NCH)]
        self.uvb = nc.dram_tensor("uvb", [4 * 16384, 2 * D], BF16, kind="Internal").ap()

        self.sempool = [es.enter_context(nc.semaphore(f"sm{i}")) for i in range(96)]
        self.eng = {n: Eng(n, self.sempool) for n in ["tensor", "vector", "scalar", "gpsimd", "sync"]}

        sb = self.sb
        self.wsl = sb("wsl", [128, NWS, 8, 512], BF16)
        self.wsl_buf = [Buf(f"wsl{i}") for i in range(NWS)]
        self.wsl_lane = [self.newlane() for _ in range(NWS)]
        self.KT = sb("KT", [128, 8, 16 * 128], BF16)
        self.Vr = sb("Vr", [128, 16, D], BF16)
        self.KT_buf = [Buf(f"KT{i}") for i in range(16)]
        self.V_buf = [Buf(f"V{i}") for i in range(16)]
        self.gsl = sb("gsl", [128, NGS, 2 * D], BF16)
        self.gsl_buf = [Buf(f"gsl{i}") for i in range(NGS)]
        self.gsl_lane = [self.newlane() for _ in range(NGS)]
        self.hb = [sb(f"hb{i}", [128, D], F32) for i in range(2)]
        self.hb_buf = [Buf(f"hb{i}") for i in range(2)]
        self.hb_lane_in = [self.newlane() for _ in range(2)]
        self.hb_lane_out = [self.newlane() for _ in range(2)]
        self.pre = sb("pre", [128, D], F32)
        self.pre_buf = Buf("pre")
        self.pre_lane = self.newlane()
        self.h16 = sb("h16", [128, D], BF16)
        self.h16_buf = Buf("h16")
        self.hTs = [sb(f"hT{i}", [128, 8, 128], BF16) for i in range(2)]
        self.hTs_buf = [Buf(f"hT{i}") for i in range(2)]
        self.hT, self.hT_buf = self.hTs[0], self.hTs_buf[0]
        self.junk = sb("junk", [128, D], BF16)
        self.junk_buf = Buf("junk")
        self.big1 = sb("big1", [128, 2048], F32)
        self.big1_buf = Buf("big1")
        self.big2 = sb("big2", [128, 2048], F32)
        self.big2_buf = Buf("big2")
        self.big2_lane = self.newlane()
        self.scr = sb("scr", [128, 8192], BF16)
        self.scr_buf = Buf("scr")
        self.ub = [sb(f"ub{l}", [128, 8, 130], F32) for l in range(2)]
        self.ub_buf = [Buf(f"ub{l}") for l in range(2)]
        self.bacc = sb("bacc", [128, 8, 128], BF16)
        self.bacc_buf = Buf("bacc")
        self.lnp = sb("lnp", [128, 2, D], F32)
        self.lnp_buf = Buf("lnp")
        self.lnp_lane = self.newlane()
        self.qT = sb("qT", [128, 16, 128], BF16)
        self.qT_buf = Buf("qT")
        self.QT = self.qT[:, 0:8, :]
        self.QT_buf = self.qT_buf
        self.oTb = self.bacc
        self.oTb_buf = self.bacc_buf
        self.Cs = sb("Cs", [16, 512], BF16)
        self.Cs_buf = Buf("Cs")
        self.dg = [sb(f"dg{i}", [128, GRP, 128], BF16) for i in range(2)]
        self.dg_buf = [Buf(f"dg{i}") for i in range(2)]
        self.abt = [sb(f"ab{i}", [128, 4, 128], BF16) for i in range(2)]
        self.ab = [self.abt[i][:].rearrange("p a b -> p (a b)") for i in range(2)]
        self.ab_buf = [Buf(f"ab{i}") for i in range(2)]
        self.v16 = sb("v16", [128, 16, 16], F32)
        self.i16 = sb("i16", [128, 16, 16], U32)
        self.i16f = sb("i16f", [128, 16, 16], F32)
        self.tmp1 = self.big2[:, 0:128]
        self.tmp2 = self.big1[:, 0:256]
        self.best = sb("best", [128, 8, 16], F32)
        self.pos = sb("pos", [128, 8, 16], U32)
        self.posa = self.big2[:, 0:128].bitcast(I32).rearrange("p (a b) -> p a b", a=8)
        self.posb = self.big2[:, 128:256].bitcast(I32).rearrange("p (a b) -> p a b", a=8)
        self.af = self.big2[:, 256:384].rearrange("p (a b) -> p a b", a=8)
        self.bf = self.big2[:, 384:512].rearrange("p (a b) -> p a b", a=8)
        self.e0 = self.big2[:, 512:640].rearrange("p (a b) -> p a b", a=8)
        self.e1 = self.big2[:, 640:768].rearrange("p (a b) -> p a b", a=8)
        self.ef = sb("ef", [128, 128], F32)
        self.eidxs = [sb(f"eidx{i}", [128, 128], I32) for i in range(2)]
        self.eidx = self.eidxs[0]
        self.gws = [sb(f"gw{i}", [128, 8, 16], F32) for i in range(2)]
        self.gw = self.gws[0]
        self.gex = sb("gex", [128, 8, 16], F32)
        self.gsum = sb("gsum", [128, 8], F32)
        self.actt = sb("actt", [128, 128], F32)[:]
        self.gel = sb("gel", [128, 128], F32)[:]
        self.w4 = sb("w4", [128, 128], F32)[:]
        self.small_buf = {n: Buf(n) for n in ["v16", "i16", "i16f", "tmp1", "tmp2", "best", "pos", "posa", "posb",
                                                "af", "bf", "e0", "e1", "ef", "eidx", "gw", "gex", "gsum",
                                                "actt", "gel", "w4", "lnst", "cst", "cin"]}
        self.small_buf["tmp1"] = self.big2_buf
        self.small_buf["tmp2"] = self.big1_buf
        self.small_buf["cst"] = self.big2_buf
        self.small_buf["cin"] = self.big2_buf
        self.eidx_bufs = [Buf("eidx0"), Buf("eidx1")]
        self.gw_bufs = [Buf("gw0"), Buf("gw1")]
        self.lnst = sb("lnst", [128, 16], F32)
        self.lnmv = sb("lnmv", [128, 4], F32)
        self.cst = self.big2[0:2, 0:D]
        self.cst_lane = self.newlane()
        self.cin = self.big2[0:2, D:2 * D]
        self.cin_lane = self.newlane()
        self.wdw = sb("wdwS", [128, 48], F32)
        self.wdw_buf = Buf("wdw")
        self.wdw_lane = self.newlane()
        self.iot = sb("iot", [128, 128], I32)
        self.identf = sb("identf", [128, 128], F32)
        self.identb = sb("identb", [128, 128], BF16)
        self.trineg = sb("trineg", [128, 128], BF16)
        self.mask1 = sb("mask1", [128, 128], BF16)
        self.indall = sb("indall", [128, 32], BF16)
        self.seli = sb("seli", [16, 128], I32)
        self.selneg = sb("selneg", [16, 16, 128], BF16)
        self.iot16i = sb("iot16i", [128, 16], I32)
        self.iot16 = sb("iot16", [128, 16], F32)
        self.const_buf = Buf("const")
        self.P = [self.es.enter_context(nc.psum_tensor(f"ps{i}", [128, 1024], F32)) for i in range(4)]
        self.pbuf = [Buf(f"bank{i}") for i in range(8)]

        self.emit_consts()
        self.emit_phase0()
        self.emit_phase0b()
        self.emit_tiles()
        self.emit_final()
        self.replay()

    def bank(self, i):
        return self.P[i // 2][:, (i % 2) * 512:(i % 2 + 1) * 512]

    def emit_consts(self):
        cb = [self.const_buf]
        self.G("iota", [], cb, out=self.iot[:], pattern=[[1, 128]], base=0, channel_multiplier=-1)
        self.G("iota", [], cb, out=self.iot16i[:], pattern=[[1, 16]], base=0, channel_multiplier=0)
        V = self.V
        V("tensor_scalar", cb, cb, out=self.identf[:], in0=self.iot[:], scalar1=0.0, scalar2=None, op0=ALU.is_equal)
        V("tensor_scalar", cb, cb, out=self.identb[:], in0=self.iot[:], scalar1=0.0, scalar2=None, op0=ALU.is_equal)
        V("tensor_scalar", cb, cb, out=self.trineg[:], in0=self.iot[:], scalar1=0.0, scalar2=-1.0,
          op0=ALU.is_le, op1=ALU.mult)
        V("tensor_scalar", cb, cb, out=self.mask1[:], in0=self.iot[:], scalar1=0.0, scalar2=None, op0=ALU.is_gt)
        V("memset", [], cb, ap=self.indall[:], constant=0.0)
        V("memset", cb, cb, ap=self.indall[:, 15:16], constant=1.0)
        for kbi in range(16):
            self.G("iota", cb, cb, out=self.seli[:], pattern=[[0, 128]], base=-kbi, channel_multiplier=1)
            V("tensor_scalar", cb, cb, out=self.selneg[:, kbi, :], in0=self.seli[:], scalar1=0.0, scalar2=-1.0,
              op0=ALU.is_gt, op1=ALU.mult)
        V("tensor_copy", cb, cb, out=self.iot16[:], in_=self.iot16i[:])
        self.dma("sync", self.wdw_lane, [], [self.wdw_buf], out=self.wdw[:], in_=self.wdw_d[:, :])

    def emit_phase0(self):
        for ci, (kind, src) in enumerate(self.chunks):
            s = ci % NWS
            if kind == "w":
                srcap = src.rearrange("(dc p) f -> p dc f", p=128)
                dst = self.wsl[:, s, :, :]
                dram = self.wbf[ci].rearrange("p (dc f) -> p dc f", dc=8)
            else:
                srcap = src
                dst = self.wsl[:, s, 0:4, :].rearrange("p a b -> p (a b)")
                dram = self.wbf[ci][:, 0:2048]
            self.dma("gpsimd", self.wsl_lane[s], [], [self.wsl_buf[s]], out=dst, in_=srcap)
            self.dma("sync", self.wsl_lane[s], [self.wsl_buf[s]], [self.wbf_buf[ci]], out=dram, in_=dst)

    def emit_phase0b(self):
        uvf = self.uv.rearrange("l e d -> (l e) d")
        R = NGS // 2
        rows = 128 * R
        nchunk = (4 * 16384) // rows
        evs = []
        for c in range(nchunk):
            hf = c % 2
            bufs = self.gsl_buf[hf * R:(hf + 1) * R]
            st = self.gsl[:, hf * R:(hf + 1) * R, :]
            src = uvf[c * rows:(c + 1) * rows, :].rearrange("(p r) d -> p r d", r=R)
            dst = self.uvb[c * rows:(c + 1) * rows, :].rearrange("(p r) d -> p r d", r=R)
            self.dma("gpsimd", self.gsl_lane[hf * R], [], bufs, out=st, in_=src)
            evs.append(self.dma("sync", self.gsl_lane[hf * R + 1], bufs, [], out=dst, in_=st))
        for ev in evs[-2:]:
            self._wait_for(self.eng["gpsimd"], ev, "raw")

    def tile_chunk_order(self):
        o = []
        for l in range(2):
            if l >= self.n_layers:
                break
            o += [self.cid[(("w_in", l), j)] for j in range(6)]
            o += [self.cid[(("w_out", l), j)] for j in range(2)]
            o += [self.cid[(("pwq", l), j)] for j in range(4)]
            o += [self.cid[(("sk", l), 0)]]
        if self.n_layers > 2:
            o += [self.cid[(("wk", 0), j)] for j in range(2)]
            o += [self.cid[(("wv", 0), j)] for j in range(2)]
        for l in range(2, 4):
            if l >= self.n_layers:
                break
            o += [self.cid[(("sbq", l - 2), j)] for j in range(2)]
            o += [self.cid[(("sbo", l - 2), j)] for j in range(2)]
            o += [self.cid[(("pwq", l), j)] for j in range(4)]
            o += [self.cid[(("sk", l), 0)]]
        return o

    def wnext(self, expect):
        if self.recording:
            self.wlist.append(self.cid[expect])
            return 0
        while self.w_issued < min(len(self.wlist), self.w_i + NWS):
            ci = self.wlist[self.w_issued]
            s = self.w_issued % NWS
            kind = self.chunks[ci][0]
            if kind == "w":
                dst = self.wsl[:, s, :, :]
                dram = self.wbf[ci].rearrange("p (dc f) -> p dc f", dc=8)
            else:
                dst = self.wsl[:, s, 0:4, :].rearrange("p a b -> p (a b)")
                dram = self.wbf[ci][:, 0:2048]
            self.dma("sync", self.wsl_lane[s], [self.wbf_buf[ci]], [self.wsl_buf[s]], out=dst, in_=dram)
            self.w_issued += 1
        ci = self.wlist[self.w_i]
        assert ci == self.cid[expect], (ci, expect)
        s = self.w_i % NWS
        self.w_i += 1
        return s

    def emit_tiles(self):
        seqs = []
        for b in range(self.n_pseq):
            seqs.append(dict(kind="p", b=b, nt=self.n_ptiles, nvalid=128, kb0=0))
        for b in range(self.n_sseq):
            seqs.append(dict(kind="s", b=b, nt=1, nvalid=16, kb0=self.n_past))
        tiles = [(seq, ti) for seq in seqs for ti in range(seq["nt"])]
        self.stores = []
        self.recording = True
        self.wlist = []
        self.units = {}
        self.schedule(tiles)
        self.recording = False
        self.stores = []
        self.w_i = 0
        self.w_issued = 0
        self.schedule(tiles)

    def schedule(self, tiles):
        nph = 2 * self.n_layers
        pending = list(enumerate(tiles))

        def new_tile(other):
            if not pending:
                return None
            tid, (seq, ti) = pending[0]
            if other is not None and other["seq"] is not seq:
                return None
            pending.pop(0)
            return dict(tid=tid, seq=seq, gen=self.tile_gen(seq, ti, tid), ph=0)

        def run_pair(g, d):
            items = [x for x in (g, d) if x is not None]
            done = {id(x): 0 for x in items}
            fin = {id(x): False for x in items}
            tot = {id(x): max(1, self.units.get((x["tid"], x["ph"]), 1)) for x in items}
            while not all(fin.values()):
                cand = [x for x in items if not fin[id(x)]]
                x = min(cand, key=lambda y: done[id(y)] / tot[id(y)])
                r = next(x["gen"])
                done[id(x)] += 1
                if r == "end":
                    fin[id(x)] = True
                    if self.recording:
                        self.units[(x["tid"], x["ph"])] = done[id(x)]
                    x["ph"] += 1

        d = new_tile(None)
        g = None
        while d is not None or g is not None:
            run_pair(g, d)
            new_g = d
            if g is not None and g["ph"] < nph:
                new_d = g
            else:
                new_d = new_tile(new_g)
            g, d = new_g, new_d

    def tile_gen(self, seq, ti, tid):
        hi = tid % 2
        b = seq["b"]
        h = self.hb[hi]
        hbuf = self.hb_buf[hi]
        if seq["kind"] == "s" and self.n_layers > 2:
            yield from self.load_cache(seq)
        if seq["kind"] == "p":
            self.dma("sync", self.hb_lane_in[hi], [], [hbuf], out=h[:], in_=self.xp[b, ti * 128:(ti + 1) * 128, :])
        else:
            self.V("memset", [], [hbuf], ap=h[:], constant=0.0)
            self.dma("sync", self.hb_lane_in[hi], [], [hbuf], out=h[0:16, :], in_=self.xs[b, :, :])
        self.make_hT(hi)
        yield "u"
        kb = seq["kb0"] + ti
        for l in range(self.n_layers):
            if l < 2:
                yield from self.conv_layer(l, seq, ti, hi)
            else:
                if l == 2:
                    self.kv(seq, ti, hi, kb)
                    yield "u"
                yield from self.attn_layer(l - 2, seq, ti, hi, kb)
            self.layernorm(2 * l, hi)
            self.make_hT(hi)
            yield "u"
            yield from self.peer_front(l, hi)
            yield "end"
            yield from self.peer_gather(l, hi)
            self.layernorm(2 * l + 1, hi)
            if l < self.n_layers - 1:
                self.make_hT(hi)
            else:
                if seq["kind"] == "p":
                    ev = self.dma("sync", self.hb_lane_out[hi], [hbuf], [], out=self.yp[b, ti * 128:(ti + 1) * 128, :],
                                  in_=h[:])
                else:
                    ev = self.dma("sync", self.hb_lane_out[hi], [hbuf], [], out=self.ys[b, :, :], in_=h[0:16, :])
                self.stores.append(ev)
            yield "end"

    def load_cache(self, seq):
        b = seq["b"]
        for kb in range(self.n_past):
            self.dma("sync", self.big2_lane, [], [self.big2_buf], out=self.big2[:, 0:D],
                     in_=self.ck[b, kb * 128:(kb + 1) * 128, :])
            self.act([self.big2_buf], [self.h16_buf], out=self.h16[:], in_=self.big2[:, 0:D], func=AF.Copy)
            self.k_transposes(kb)
            self.dma("sync", self.big2_lane, [], [self.big2_buf], out=self.big2[:, 0:D],
                     in_=self.cv[b, kb * 128:(kb + 1) * 128, :])
            self.act([self.big2_buf], [self.V_buf[kb]], out=self.Vr[:, kb, :], in_=self.big2[:, 0:D], func=AF.Copy)
            yield "u"

    def k_transposes(self, kb):
        bi = 0
        pb = self.bank(bi).bitcast(BF16)
        for pair in range(8):
            self.T("transpose", [self.h16_buf, self.const_buf], [self.pbuf[bi]],
                   out=pb[:, pair * 128:(pair + 1) * 128], in_=self.h16[:, pair * 128:(pair + 1) * 128],
                   identity=self.identb[:])
        self.act([self.pbuf[bi]], [self.KT_buf[kb]], out=self.KT[:, :, kb * 128:(kb + 1) * 128],
                 in_=pb.rearrange("p (a b) -> p a b", a=8), func=AF.Copy)

    def make_hT(self, hi):
        self.act([self.hb_buf[hi]], [self.h16_buf], out=self.h16[:], in_=self.hb[hi][:], func=AF.Copy)
        bi = 7
        pb = self.bank(bi).bitcast(BF16)
        for c in range(8):
            self.T("transpose", [self.h16_buf, self.const_buf], [self.pbuf[bi]],
                   out=pb[:, c * 128:(c + 1) * 128], in_=self.h16[:, c * 128:(c + 1) * 128],
                   identity=self.identb[:])
        self.V("tensor_copy", [self.pbuf[bi]], [self.hTs_buf[hi]], out=self.hTs[hi][:].rearrange("p a b -> p (a b)"),
               in_=pb)

    def layernorm(self, idx, hi):
        self.dma("sync", self.lnp_lane, [], [self.lnp_buf], out=self.lnp[:, 0:1, :],
                 in_=self.lng[idx:idx + 1, :].partition_broadcast(128))
        self.dma("sync", self.lnp_lane, [], [self.lnp_buf], out=self.lnp[:, 1:2, :],
                 in_=self.lnb[idx:idx + 1, :].partition_broadcast(128))
        sbuf = self.small_buf["lnst"]
        V = self.V
        V("bn_stats", [self.pre_buf], [sbuf], out=self.lnst[:, 0:6], in_=self.pre[:, 0:512])
        V("bn_stats", [self.pre_buf], [sbuf], out=self.lnst[:, 6:12], in_=self.pre[:, 512:1024])
        V("bn_aggr", [sbuf], [sbuf], out=self.lnmv[:, 0:2], in_=self.lnst[:, 0:12])
        V("tensor_scalar", [sbuf], [sbuf], out=self.lnmv[:, 2:3], in0=self.lnmv[:, 1:2], scalar1=EPS, scalar2=None,
          op0=ALU.add)
        self.act([sbuf], [sbuf], out=self.lnmv[:, 2:3], in_=self.lnmv[:, 2:3], func=AF.Ln)
        self.act([sbuf], [sbuf], out=self.lnmv[:, 3:4], in_=self.lnmv[:, 2:3], func=AF.Exp, scale=-0.5)
        V("tensor_scalar", [self.pre_buf, sbuf], [self.pre_buf], out=self.pre[:], in0=self.pre[:],
          scalar1=self.lnmv[:, 0:1], scalar2=self.lnmv[:, 3:4], op0=ALU.subtract, op1=ALU.mult)
        V("tensor_tensor", [self.pre_buf, self.lnp_buf], [self.pre_buf], out=self.pre[:], in0=self.pre[:],
          in1=self.lnp[:, 0, :], op=ALU.mult)
        V("tensor_tensor", [self.pre_buf, self.lnp_buf], [self.hb_buf[hi]], out=self.hb[hi][:], in0=self.pre[:],
          in1=self.lnp[:, 1, :], op=ALU.add)

    def residual(self, hi, pbanks):
        pidx = pbanks
        self.V("scalar_tensor_tensor", [self.hb_buf[hi], self.pbuf[2 * pidx], self.pbuf[2 * pidx + 1]],
               [self.pre_buf], out=self.pre[:], in0=self.hb[hi][:], scalar=ALPHA, in1=self.P[pidx][:, :],
               op0=ALU.mult, op1=ALU.add)

    def tok_major_mm(self, key, lhs, lhs_buf, pidx):
        for half in range(2):
            s = self.wnext((key, half))
            bi = 2 * pidx + half
            for dc in range(8):
                self.mm([lhs_buf, self.wsl_buf[s]], [self.pbuf[bi]], out=self.bank(bi), lhsT=lhs[:, dc, :],
                        rhs=self.wsl[:, s, dc, :], start=(dc == 0), stop=(dc == 7))

    def feat_major_chunk(self, key, j, bi, hi):
        s = self.wnext((key, j))
        for fc in range(4):
            for dc in range(8):
                self.mm([self.hTs_buf[hi], self.wsl_buf[s]], [self.pbuf[bi]],
                        out=self.bank(bi)[:, fc * 128:(fc + 1) * 128], lhsT=self.wsl[:, s, dc, fc * 128:(fc + 1) * 128],
                        rhs=self.hTs[hi][:, dc, :], start=(dc == 0), stop=(dc == 7))

    def conv_layer(self, l, seq, ti, hi):
        V = self.V
        ub, ubb = self.ub[l], self.ub_buf[l]
        gates = self.scr[:].bitcast(F32)[:, 0:3072].rearrange("p (a b) -> p a b", a=24)
        if ti == 0:
            if seq["kind"] == "p":
                V("memset", [], [ubb], ap=ub[:, :, 0:2], constant=0.0)
            else:
                cb = self.small_buf["cin"]
                self.dma("sync", self.cin_lane, [], [cb], out=self.cin, in_=self.stc[l, seq["b"], :, :])
                bi = 0
                for c in range(8):
                    self.T("transpose", [cb, self.const_buf], [self.pbuf[bi]], out=self.bank(bi)[:, 2 * c:2 * c + 2],
                           in_=self.big2[0:2, D + c * 128:D + (c + 1) * 128], identity=self.identf[0:2, 0:2])
                V("tensor_copy", [self.pbuf[bi]], [ubb], out=ub[:, :, 0:2],
                  in_=self.bank(bi)[:, 0:16].rearrange("p (a b) -> p a b", a=8))
        for j in range(6):
            bi = j % 2
            self.feat_major_chunk(("w_in", l), j, bi, hi)
            self.act([self.pbuf[bi]], [self.scr_buf], out=gates[:, 4 * j:4 * j + 4, :],
                     in_=self.bank(bi).rearrange("p (a b) -> p a b", a=4), func=AF.Copy)
            yield "u"
        yield "u"
        V("tensor_tensor", [self.scr_buf], [ubb], out=ub[:, :, 2:130], in0=gates[:, 8:16, :], in1=gates[:, 16:24, :],
          op=ALU.mult)
        acc = self.big1[:, 0:1024].rearrange("p (a b) -> p a b", a=8)
        ab = self.big1_buf
        for c in range(8):
            col = lambda w: self.wdw[:, (l * 3 + w) * 8 + c:(l * 3 + w) * 8 + c + 1]
            V("tensor_scalar", [ubb, self.wdw_buf], [ab], out=acc[:, c, :], in0=ub[:, c, 0:128], scalar1=col(0),
              scalar2=None, op0=ALU.mult)
            V("scalar_tensor_tensor", [ubb, ab, self.wdw_buf], [ab], out=acc[:, c, :], in0=ub[:, c, 1:129],
              scalar=col(1), in1=acc[:, c, :], op0=ALU.mult, op1=ALU.add)
            V("scalar_tensor_tensor", [ubb, ab, self.wdw_buf], [ab], out=acc[:, c, :], in0=ub[:, c, 2:130],
              scalar=col(2), in1=acc[:, c, :], op0=ALU.mult, op1=ALU.add)
        V("tensor_tensor", [self.scr_buf, ab], [self.bacc_buf], out=self.bacc[:], in0=gates[:, 0:8, :], in1=acc,
          op=ALU.mult)
        yield "u"
        nv = seq["nvalid"]
        if ti == seq["nt"] - 1:
            cb = self.small_buf["cst"]
            for c in range(8):
                bi = c // 4
                self.T("transpose", [ubb, self.const_buf], [self.pbuf[bi]],
                       out=self.P[0][0:2, c * 128:(c + 1) * 128], in_=ub[:, c, nv:nv + 2], identity=self.identf[:])
            self.act([self.pbuf[0], self.pbuf[1]], [cb], out=self.cst, in_=self.P[0][0:2, :], func=AF.Copy)
            dst = (self.ncp if seq["kind"] == "p" else self.ncs)[l, seq["b"], :, :]
            ev = self.dma("sync", self.cst_lane, [cb], [], out=dst, in_=self.cst)
            self.stores.append(ev)
        else:
            V("tensor_copy", [ubb], [ubb], out=ub[:, :, 0:2], in_=ub[:, :, 128:130])
        self.tok_major_mm(("w_out", l), self.bacc, self.bacc_buf, 0)
        yield "u"
        self.residual(hi, 0)

    def peer_front(self, l, hi):
        V, A = self.V, self.A
        sm = dict(self.small_buf)
        sm["eidx"] = self.eidx_bufs[hi]
        sm["gw"] = self.gw_bufs[hi]
        eidx_t, gw_t = self.eidxs[hi], self.gws[hi]
        h, hbuf = self.hb[hi], self.hb_buf[hi]
        for j in range(4):
            bi = j % 2
            self.feat_major_chunk(("pwq", l), j, bi, hi)
            self.act([self.pbuf[bi]], [self.qT_buf], out=self.qT[:, 4 * j:4 * j + 4, :],
                     in_=self.bank(bi).rearrange("p (a b) -> p a b", a=4), func=AF.Copy)
            yield "u"
        s = self.wnext((("sk", l), 0))
        skT = self.wsl[:, s, 0:4, :].rearrange("p a b -> p (a b)")
        scs = self.big1[:].rearrange("p (a b) -> p a b", a=16)
        for g in range(16):
            bi = 4 + g // 4
            self.mm([self.qT_buf, self.wsl_buf[s]], [self.pbuf[bi]], out=self.bank(bi)[:, (g % 4) * 128:(g % 4 + 1) * 128],
                    lhsT=self.qT[:, g, :], rhs=skT[:, g * 128:(g + 1) * 128], start=True, stop=True)
        for q in range(4):
            self.act([self.pbuf[4 + q]], [self.big1_buf], out=self.big1[:, q * 512:(q + 1) * 512], in_=self.bank(4 + q),
                     func=AF.Copy)
        b1 = self.big1_buf
        yield "u"
        tmp1s = [self.big2[:, 0:128], self.big2[:, 128:256]]
        for g0 in range(0, 16, 2):
            gs = (g0, g0 + 1)
            for k, g in enumerate(gs):
                V("max", [b1], [sm["v16"]], out=self.v16[:, g, 0:8], in_=scs[:, g, :])
            for k, g in enumerate(gs):
                V("max_index", [b1, sm["v16"]], [sm["i16"]], out=self.i16[:, g, 0:8], in_max=self.v16[:, g, 0:8],
                  in_values=scs[:, g, :])
            for k, g in enumerate(gs):
                V("match_replace", [b1, sm["v16"]], [sm["tmp1"]], out=tmp1s[k], in_to_replace=self.v16[:, g, 0:8],
                  in_values=scs[:, g, :], imm_value=NEG)
            for k, g in enumerate(gs):
                V("max", [sm["tmp1"]], [sm["v16"]], out=self.v16[:, g, 8:16], in_=tmp1s[k])
            for k, g in enumerate(gs):
                V("max_index", [sm["tmp1"], sm["v16"]], [sm["i16"]], out=self.i16[:, g, 8:16],
                  in_max=self.v16[:, g, 8:16], in_values=tmp1s[k])
            yield "u"
        V("tensor_copy", [sm["i16"]], [sm["i16f"]], out=self.i16f[:], in_=self.i16[:])
        cand = self.big2[:].rearrange("p (h a b) -> p h a b", h=8, a=16)
        v16h = self.v16[:].rearrange("p (h t) k -> p h t k", t=2)
        i16h = self.i16f[:].rearrange("p (h t) k -> p h t k", t=2)
        V("tensor_tensor", [sm["v16"]], [self.big2_buf], out=cand,
          in0=v16h[:, :, 0, :].unsqueeze(3).to_broadcast([128, 8, 16, 16]),
          in1=v16h[:, :, 1, :].unsqueeze(2).to_broadcast([128, 8, 16, 16]), op=ALU.add)
        b2 = self.big2_buf
        tmp2s = [self.big1[:, 0:256], self.big1[:, 256:512]]
        for h0 in range(0, 8, 2):
            hs = (h0, h0 + 1)
            cfs = [self.big2[:, hh * 256:(hh + 1) * 256] for hh in hs]
            for k, hh in enumerate(hs):
                V("max", [b2], [sm["best"]], out=self.best[:, hh, 0:8], in_=cfs[k])
            for k, hh in enumerate(hs):
                V("max_index", [b2, sm["best"]], [sm["pos"]], out=self.pos[:, hh, 0:8], in_max=self.best[:, hh, 0:8],
                  in_values=cfs[k])
            for k, hh in enumerate(hs):
                V("match_replace", [b2, sm["best"]], [sm["tmp2"]], out=tmp2s[k], in_to_replace=self.best[:, hh, 0:8],
                  in_values=cfs[k], imm_value=NEG)
            for k, hh in enumerate(hs):
                V("max", [sm["tmp2"]], [sm["best"]], out=self.best[:, hh, 8:16], in_=tmp2s[k])
            for k, hh in enumerate(hs):
                V("max_index", [sm["tmp2"], sm["best"]], [sm["pos"]], out=self.pos[:, hh, 8:16],
                  in_max=self.best[:, hh, 8:16], in_values=tmp2s[k])
            yield "u"
        posi = self.pos[:].bitcast(I32)
        V("tensor_single_scalar", [sm["pos"]], [sm["posa"]], out=self.posa[:], in_=posi, scalar=4,
          op=ALU.arith_shift_right)
        V("tensor_single_scalar", [sm["pos"]], [sm["posb"]], out=self.posb[:], in_=posi, scalar=15,
          op=ALU.bitwise_and)
        V("tensor_copy", [sm["posa"]], [sm["af"]], out=self.af[:], in_=self.posa[:])
        V("tensor_copy", [sm["posb"]], [sm["bf"]], out=self.bf[:], in_=self.posb[:])
        eq = self.big1[:].rearrange("p (h a b) -> p h a b", h=8, a=16)
        io = self.iot16[:].unsqueeze(1).unsqueeze(1).to_broadcast([128, 8, 16, 16])
        for (src, srcb, t, dst, dstb) in ((self.af, sm["af"], 0, self.e0, sm["e0"]), (self.bf, sm["bf"], 1, self.e1, sm["e1"])):
            V("tensor_tensor", [srcb, self.const_buf], [b1], out=eq,
              in0=src[:].unsqueeze(3).to_broadcast([128, 8, 16, 16]), in1=io, op=ALU.is_equal)
            V("tensor_tensor", [b1, sm["i16f"]], [b1], out=eq, in0=eq,
              in1=i16h[:, :, t, :].unsqueeze(2).to_broadcast([128, 8, 16, 16]), op=ALU.mult)
            V("tensor_reduce", [b1], [dstb], out=dst[:], in_=eq, axis=AX.X, op=ALU.add)
            yield "u"
        V("scalar_tensor_tensor", [sm["e0"], sm["e1"]], [sm["ef"]], out=self.ef[:],
          in0=self.e0[:].rearrange("p a b -> p (a b)"), scalar=128.0, in1=self.e1[:].rearrange("p a b -> p (a b)"),
          op0=ALU.mult, op1=ALU.add)
        if l > 0:
            V("tensor_scalar", [sm["ef"]], [sm["ef"]], out=self.ef[:], in0=self.ef[:], scalar1=float(l * 16384),
              scalar2=None, op0=ALU.add)
        V("tensor_copy", [sm["ef"]], [sm["eidx"]], out=eidx_t[:], in_=self.ef[:])
        V("tensor_tensor", [sm["best"]], [sm["gex"]], out=self.gex[:], in0=self.best[:],
          in1=self.best[:, :, 0:1].to_broadcast([128, 8, 16]), op=ALU.subtract)
        self.act([sm["gex"]], [sm["gex"]], out=self.gex[:], in_=self.gex[:], func=AF.Exp)
        V("tensor_reduce", [sm["gex"]], [sm["gsum"]], out=self.gsum[:], in_=self.gex[:], axis=AX.X, op=ALU.add)
        V("reciprocal", [sm["gsum"]], [sm["gsum"]], out=self.gsum[:], in_=self.gsum[:])
        V("tensor_tensor", [sm["gex"], sm["gsum"]], [sm["gw"]], out=gw_t[:], in0=self.gex[:],
          in1=self.gsum[:].unsqueeze(2).to_broadcast([128, 8, 16]), op=ALU.mult)
        yield "u"

    def peer_gather(self, l, hi):
        V, A = self.V, self.A
        sm = dict(self.small_buf)
        sm["eidx"] = self.eidx_bufs[hi]
        sm["gw"] = self.gw_bufs[hi]
        eidx_t, gw_t = self.eidxs[hi], self.gws[hi]
        h, hbuf = self.hb[hi], self.hb_buf[hi]
        gwf = gw_t[:].rearrange("p a b -> p (a b)")
        uvt = self.uvb
        for j0 in range(0, 128, GRP):
            gi = (j0 // GRP) % 2
            for jj in range(GRP):
                j = j0 + jj
                sl = j % NGS
                self.dma("gpsimd", self.gsl_lane[sl], [sm["eidx"]], [self.gsl_buf[sl]], out=self.gsl[:, sl, :],
                         in_=uvt[:, :], method="indirect_dma_start", out_offset=None,
                         in_offset=bass.IndirectOffsetOnAxis(ap=eidx_t[:, j:j + 1], axis=0))
                V("scalar_tensor_tensor", [self.gsl_buf[sl], hbuf], [self.junk_buf, sm["actt"]], out=self.junk[:],
                  in0=self.gsl[:, sl, 0:D], scalar=1.0, in1=h[:], op0=ALU.mult, op1=ALU.mult,
                  accum_out=self.actt[:, j:j + 1])
            self.act([sm["actt"]], [sm["gel"]], out=self.gel[:, j0:j0 + GRP], in_=self.actt[:, j0:j0 + GRP], func=AF.Gelu)
            V("tensor_tensor", [sm["gel"], sm["gw"]], [sm["w4"]], out=self.w4[:, j0:j0 + GRP], in0=self.gel[:, j0:j0 + GRP],
              in1=gwf[:, j0:j0 + GRP], op=ALU.mult)
            V("tensor_tensor", [sm["w4"], self.const_buf], [self.dg_buf[gi]], out=self.dg[gi][:],
              in0=self.identb[:].unsqueeze(1).to_broadcast([128, GRP, 128]),
              in1=self.w4[:, j0:j0 + GRP].unsqueeze(2).to_broadcast([128, GRP, 128]), op=ALU.mult)
            for jj in range(GRP):
                j = j0 + jj
                sl = j % NGS
                for half in range(2):
                    bi = 2 + half
                    self.mm([self.dg_buf[gi], self.gsl_buf[sl]], [self.pbuf[bi]], out=self.bank(bi),
                            lhsT=self.dg[gi][:, jj, :], rhs=self.gsl[:, sl, D + half * 512:D + (half + 1) * 512],
                            start=(j == 0), stop=(j == 127))
            yield "u"
        self.residual(hi, 1)

    def kv(self, seq, ti, hi, kb):
        nv, b = seq["nvalid"], seq["b"]
        r0 = ti * 128
        self.tok_major_mm(("wk", 0), self.hTs[hi], self.hTs_buf[hi], 0)
        self.act([self.pbuf[0], self.pbuf[1]], [self.pre_buf], out=self.pre[:], in_=self.P[0][:, :], func=AF.Copy)
        self.act([self.pbuf[0], self.pbuf[1]], [self.h16_buf], out=self.h16[:], in_=self.P[0][:, :], func=AF.Copy)
        if seq["kind"] == "p":
            ev = self.dma("sync", self.pre_lane, [self.pre_buf], [], out=self.nkp[b, r0:r0 + 128, :], in_=self.pre[:])
        else:
            ev = self.dma("sync", self.pre_lane, [self.pre_buf], [], out=self.nks[b, :, :], in_=self.pre[0:16, :])
        self.stores.append(ev)
        self.k_transposes(kb)
        self.tok_major_mm(("wv", 0), self.hTs[hi], self.hTs_buf[hi], 0)
        self.act([self.pbuf[0], self.pbuf[1]], [self.big2_buf], out=self.big2[:, 0:D], in_=self.P[0][:, :], func=AF.Copy)
        self.act([self.pbuf[0], self.pbuf[1]], [self.V_buf[kb]], out=self.Vr[:, kb, :], in_=self.P[0][:, :], func=AF.Copy)
        if seq["kind"] == "p":
            ev = self.dma("sync", self.big2_lane, [self.big2_buf], [], out=self.nvp[b, r0:r0 + 128, :], in_=self.big2[:, 0:D])
        else:
            ev = self.dma("sync", self.big2_lane, [self.big2_buf], [], out=self.nvs[b, :, :], in_=self.big2[0:16, 0:D])
        self.stores.append(ev)

    def attn_layer(self, j, seq, ti, hi, kb):
        V = self.V
        nkb = kb + 1
        V("memset", [], [self.qT_buf], ap=self.qT[:], constant=0.0)
        for jj in range(2):
            bi = jj % 2
            self.feat_major_chunk(("sbq", j), jj, bi, hi)
            for r in range(2):
                self.act([self.pbuf[bi]], [self.qT_buf], out=self.qT[r * 64:(r + 1) * 64, 8 * jj + r:8 * jj + 8:2, :],
                         in_=self.bank(bi)[r * 64:(r + 1) * 64, :].rearrange("p (a b) -> p a b", a=4), func=AF.Copy,
                         scale=0.125)
            yield "u"
        if DBG == 1:
            self.V("tensor_scalar", [self.hb_buf[hi]], [self.pre_buf], out=self.pre[:], in0=self.hb[hi][:], scalar1=ALPHA,
                   scalar2=None, op0=ALU.mult)
            for half in range(2):
                self.wnext((("sbo", j), half))
            return
        Pb = self.scr[:].rearrange("p (a b) -> p a b", a=16)
        Eb = self.big2[:, 0:512]
        oT = self.P[0]
        first_done = {}
        for g in range(4):
            heads = [4 * g + hh for hh in range(4)]

            def zmm(bi, start_first, last_stop):
                for hh, hd in enumerate(heads):
                    pair, r = hd // 2, hd % 2
                    self.mm([self.KT_buf[kbi_], self.QT_buf], [self.pbuf[bi]],
                            out=self.bank(bi)[:, hh * 128:(hh + 1) * 128],
                            lhsT=self.KT[:, pair, kbi_ * 128:(kbi_ + 1) * 128],
                            rhs=self.qT[:, hd, :],
                            start=(start_first and hh == 0), stop=(last_stop and hh == 3), skip_group_check=True)
            for kbi_ in range(nkb):
                zb = 4 + kbi_ % 2
                zmm(zb, True, True)
                self.act([self.pbuf[zb]], [self.big2_buf], out=Eb, in_=self.bank(zb), func=AF.Exp)
                self.act([self.big2_buf], [self.scr_buf], out=Pb[:, kbi_, :], in_=Eb, func=AF.Ln, bias=1.0)
                if kbi_ == kb:
                    V("tensor_tensor", [self.scr_buf, self.const_buf], [self.scr_buf], out=Pb[:, kbi_, :].rearrange("p (a b) -> p a b", a=4),
                      in0=Pb[:, kbi_, :].rearrange("p (a b) -> p a b", a=4),
                      in1=self.mask1[:].unsqueeze(1).to_broadcast([128, 4, 128]), op=ALU.mult)
                self.mm([self.scr_buf, self.const_buf], [self.pbuf[6]], out=self.bank(6)[0:16, :],
                        lhsT=self.indall[:, 15 - kbi_:31 - kbi_], rhs=Pb[:, kbi_, :], start=(kbi_ == 0),
                        stop=(kbi_ == nkb - 1))
                if kbi_ % 2 == 1:
                    yield "u"
            if DBG == 2:
                continue
            self.act([self.pbuf[6]], [self.Cs_buf], out=self.Cs[:], in_=self.bank(6)[0:16, :], func=AF.Copy)
            for kbi_ in range(nkb):
                zs = 4 + kbi_ % 2
                ai = kbi_ % 2
                self.mm([self.scr_buf, self.const_buf], [self.pbuf[zs]], out=self.bank(zs), lhsT=self.trineg[:],
                        rhs=Pb[:, kbi_, :], start=True, stop=False, skip_group_check=True)
                self.mm([self.Cs_buf, self.const_buf], [self.pbuf[zs]], out=self.bank(zs), lhsT=self.selneg[0:16, kbi_, :],
                        rhs=self.Cs[:], start=False, stop=False, skip_group_check=True)
                zmm(zs, False, True)
                self.act([self.pbuf[zs]], [self.ab_buf[ai]], out=self.ab[ai], in_=self.bank(zs), func=AF.Exp)
                if DBG == 3:
                    continue
                if kbi_ == kb:
                    V("tensor_tensor", [self.ab_buf[ai], self.const_buf], [self.ab_buf[ai]], out=self.abt[ai][:],
                      in0=self.abt[ai][:], in1=self.mask1[:].unsqueeze(1).to_broadcast([128, 4, 128]), op=ALU.mult)
                for hh, hd in enumerate(heads):
                    pair, r = hd // 2, hd % 2
                    bk = pair // 4
                    key = (bk, r)
                    st = key not in first_done
                    first_done[key] = True
                    self.mm([self.ab_buf[ai], self.V_buf[kbi_]], [self.pbuf[bk]],
                            out=oT[r * 64:(r + 1) * 64, pair * 128:(pair + 1) * 128],
                            lhsT=self.Vr[:, kbi_, hd * 64:(hd + 1) * 64], rhs=self.ab[ai][:, hh * 128:(hh + 1) * 128],
                            start=st, stop=(kbi_ == nkb - 1), skip_group_check=True)
                if kbi_ % 2 == 1:
                    yield "u"
        if DBG in (2, 3):
            self.V("tensor_scalar", [self.hb_buf[hi]], [self.pre_buf], out=self.pre[:], in0=self.hb[hi][:], scalar1=ALPHA,
                   scalar2=None, op0=ALU.mult)
            for half in range(2):
                self.wnext((("sbo", j), half))
            return
        self.act([self.pbuf[0], self.pbuf[1]], [self.oTb_buf], out=self.oTb[:].rearrange("p a b -> p (a b)"),
                 in_=oT[:, :], func=AF.Copy)
        self.tok_major_mm(("sbo", j), self.oTb, self.oTb_buf, 0)
        yield "u"
        self.residual(hi, 0)

    def emit_final(self):
        eng = self.eng["sync"]
        for ev in self.stores:
            self._wait_for(eng, ev, "raw")

    def replay(self):
        nc = self.nc
        engs = self.eng
        with nc.Block() as block:
            def run(e, name):
                for it in engs[name].q:
                    if it[0] == "w":
                        e.wait_ge(it[1], it[2])
                    else:
                        _, method, kw, sem, inc = it
                        ins = getattr(e, method)(**kw)
                        ins.then_inc(sem, inc)

            @block.sync
            def _(e):
                run(e, "sync")

            @block.gpsimd
            def _(e):
                run(e, "gpsimd")

            @block.tensor
            def _(e):
                run(e, "tensor")

            @block.vector
            def _(e):
                run(e, "vector")

            @block.scalar
            def _(e):
                run(e, "scalar")
        self.es.close()


def make_in_maps(inputs, n_cores, n_pseq, n_sseq):
    f = lambda a: np.ascontiguousarray(np.asarray(a, dtype=np.float32))
    xp, xs = f(inputs["x_prompt"]), f(inputs["x_sample"])
    stc, ck, cv = f(inputs["state_conv"]), f(inputs["cache_k"]), f(inputs["cache_v"])
    ck = ck.reshape(ck.shape[0], ck.shape[1], -1)
    cv = cv.reshape(cv.shape[0], cv.shape[1], -1)
    wdw = f(inputs["conv_w_dw"])
    wdw_l = np.ascontiguousarray(wdw.reshape(2, 3, 8, 128).transpose(3, 0, 1, 2).reshape(128, 48))
    sk = f(inputs["peer_subkeys"])
    skT = np.ascontiguousarray(sk.transpose(0, 4, 1, 2, 3).reshape(4, 128, 2048))
    uv = np.concatenate([f(inputs["peer_u"]), f(inputs["peer_v"])], axis=-1)
    shared = dict(w_in=f(inputs["conv_w_in"]), wdw=wdw_l, w_out=f(inputs["conv_w_out"]), sbq=f(inputs["sb_w_q"]),
                  sbo=f(inputs["sb_w_o"]), wk=f(inputs["kv_w_k"]), wv=f(inputs["kv_w_v"]), pwq=f(inputs["peer_w_q"]),
                  skT=skT, uv=uv, lng=f(inputs["ln_g"]).reshape(8, D), lnb=f(inputs["ln_b"]).reshape(8, D))
    maps = []
    for c in range(n_cores):
        m = dict(shared)
        m["xp"] = np.ascontiguousarray(xp[c * n_pseq:(c + 1) * n_pseq])
        m["xs"] = np.ascontiguousarray(xs[c * n_sseq:(c + 1) * n_sseq])
        m["stc"] = np.ascontiguousarray(stc[:, c * n_sseq:(c + 1) * n_sseq])
        m["ck"] = np.ascontiguousarray(ck[c * n_sseq:(c + 1) * n_sseq])
        m["cv"] = np.ascontiguousarray(cv[c * n_sseq:(c + 1) * n_sseq])
        maps.append(m)
    return maps


def assemble(results, n_cores):
    cat = lambda k, ax: np.concatenate([np.asarray(r[k], dtype=np.float32) for r in results], axis=ax)
    yp, ys = cat("yp", 0), cat("ys", 0)
    ncp, ncs = cat("ncp", 1), cat("ncs", 1)
    nkp, nvp, nks, nvs = cat("nkp", 0), cat("nvp", 0), cat("nks", 0), cat("nvs", 0)
    r4 = lambda a: a.reshape(a.shape[0], a.shape[1], 16, 64)
    return (yp, ys, ncp, r4(nkp), r4(nvp), ncs, r4(nks), r4(nvs))


_PROG = {}


def kernel(**inputs):
    n_cores = 8
    if "full" not in _PROG:
        _PROG["full"] = Prog(4, 16, 4, 8)
    prog = _PROG["full"]
    maps = make_in_maps(inputs, n_cores, 4, 4)
    res = run_bass_kernel_spmd(prog.nc, maps, core_ids=list(range(n_cores)))
    return assemble(res.results, n_cores)
```
